# Optimizing a Trainium2 kernel written in Bass

```python
import math
import jax, jax.numpy as jnp
from jax import lax
import numpy as np

D_MODEL = 1024
BATCH = 2
SEQ = 8192
DEPTH = 2
DEC_BATCH = 4
DEC_SEQ = 8192
PAST_LEN = 128

CHUNK = 128
A_WIDTH = 1024
A_GROUPS = 8
A_GROUP_DIM = A_WIDTH // A_GROUPS
B_WIDTH = 1024
B_GROUP_DIM = 16
B_GROUPS = B_WIDTH // B_GROUP_DIM
B_STATE = 64
N_BRANCH = 2
D_FF = 4 * D_MODEL
IN_WIDTH = 2 * A_WIDTH + B_WIDTH + N_BRANCH * D_MODEL
EPS = 1e-6
DT_MIN = 1e-3
DT_MAX = 1e-1

kernel_name = "hybrid_gmlp_s5_bidir_encoder"


def rmsnorm(x, g):
    xf = x.astype(jnp.float32)
    y = xf * lax.rsqrt(jnp.mean(xf * xf, axis=-1, keepdims=True) + EPS) * g.astype(jnp.float32)
    return y.astype(x.dtype)


def spatial_gating(u, v, norm_v, w_s, b_s):
    bsz, s, _ = v.shape
    v = rmsnorm(v, norm_v).reshape(bsz, s // CHUNK, CHUNK, A_GROUPS, A_GROUP_DIM)
    mixed = jnp.einsum('gpq,bnqgc->bnpgc', w_s, v) + b_s.T[None, None, :, :, None]
    return u * mixed.reshape(bsz, s, A_WIDTH)


def s5_direction(u, lam_re, lam_im, log_dt, b_re, b_im, c_re, c_im):
    lam = lax.complex(lam_re.astype(jnp.float32), lam_im.astype(jnp.float32))
    dt = jnp.exp(log_dt.astype(jnp.float32))[:, None]
    a_bar = jnp.exp(lam * dt)
    b_mat = lax.complex(b_re.astype(jnp.float32), b_im.astype(jnp.float32))
    b_bar = ((a_bar - 1.0) / lam)[:, :, None] * b_mat
    bu = jnp.einsum('gph,bsgh->bsgp', b_bar, u.astype(b_bar.dtype))
    a_seq = jnp.broadcast_to(a_bar, bu.shape)

    def combine(left, right):
        a_l, s_l = left
        a_r, s_r = right
        return a_l * a_r, a_r * s_l + s_r

    _, states = lax.associative_scan(combine, (a_seq, bu), axis=1)
    c_mat = lax.complex(c_re.astype(jnp.float32), c_im.astype(jnp.float32))
    return jnp.einsum('ghp,bsgp->bsgh', c_mat, states).real


def s5_branch(u, l, p):
    bsz, s, _ = u.shape
    ug = u.astype(jnp.float32).reshape(bsz, s, B_GROUPS, B_GROUP_DIM)
    y_f = s5_direction(ug, p['lam_re'][l, 0], p['lam_im'][l, 0], p['log_dt'][l, 0],
                       p['b_re'][l, 0], p['b_im'][l, 0], p['c_re'][l, 0], p['c_im'][l, 0])
    y_b = jnp.flip(s5_direction(jnp.flip(ug, axis=1), p['lam_re'][l, 1], p['lam_im'][l, 1],
                                p['log_dt'][l, 1], p['b_re'][l, 1], p['b_im'][l, 1],
                                p['c_re'][l, 1], p['c_im'][l, 1]), axis=1)
    y = (y_f + y_b).reshape(bsz, s, B_WIDTH) + p['d_skip'][l].astype(jnp.float32) * ug.reshape(bsz, s, B_WIDTH)
    z = jax.nn.gelu(y).astype(u.dtype)
    glu = z @ p['w_glu'][l]
    g1, g2 = jnp.split(glu, 2, axis=-1)
    return g1 * jax.nn.sigmoid(g2)


def layer(x, l, p):
    h = rmsnorm(x, p['norm_pre_mix'][l])
    proj = h @ p['w_in'][l]
    u_a, v_a, u_b, gate_logits = jnp.split(
        proj, [A_WIDTH, 2 * A_WIDTH, 2 * A_WIDTH + B_WIDTH], axis=-1)
    a = spatial_gating(jax.nn.gelu(u_a), jax.nn.gelu(v_a), p['norm_v'][l],
                       p['w_s'][l], p['b_s'][l]) @ p['w_out_a'][l]
    b = s5_branch(u_b, l, p)
    g_a, g_b = jnp.split(jax.nn.sigmoid(gate_logits), 2, axis=-1)
    mix = (g_a * a + g_b * b) @ p['w_o'][l]
    x = x + rmsnorm(mix, p['norm_post_mix'][l])
    h = rmsnorm(x, p['norm_pre_ff'][l])
    f = jnp.square(jax.nn.relu(h @ p['w_ff1'][l])) @ p['w_ff2'][l]
    return x + rmsnorm(f, p['norm_post_ff'][l])


def trunk(x, p):
    for l in range(DEPTH):
        x = layer(x, l, p)
    return x


def setup_inputs(seed: int = 0) -> dict:
    key = jax.random.key(seed)
    k = jax.random.split(key, 24)
    f32 = jnp.float32

    def nrm(kk, shape, scale):
        return jax.random.normal(kk, shape, f32) * scale

    n_idx = jnp.arange(B_STATE, dtype=f32)
    lam_re = -0.5 + nrm(k[10], (DEPTH, 2, B_GROUPS, B_STATE), 0.01)
    lam_im = math.pi * n_idx + nrm(k[11], (DEPTH, 2, B_GROUPS, B_STATE), 0.01)
    log_dt = jax.random.uniform(k[12], (DEPTH, 2, B_GROUPS), f32,
                                math.log(DT_MIN), math.log(DT_MAX))
    return {
        'x_prompt': jax.random.normal(k[0], (BATCH, SEQ, D_MODEL), f32),
        'x_sample': jax.random.normal(k[1], (DEC_BATCH, DEC_SEQ, D_MODEL), f32),
        'norm_pre_mix': 1.0 + nrm(k[2], (DEPTH, D_MODEL), 0.02),
        'w_in': nrm(k[3], (DEPTH, D_MODEL, IN_WIDTH), D_MODEL ** -0.5),
        'norm_v': 1.0 + nrm(k[4], (DEPTH, A_WIDTH), 0.02),
        'w_s': nrm(k[5], (DEPTH, A_GROUPS, CHUNK, CHUNK), CHUNK ** -0.5),
        'b_s': 1.0 + nrm(k[6], (DEPTH, A_GROUPS, CHUNK), 0.02),
        'w_out_a': nrm(k[7], (DEPTH, A_WIDTH, D_MODEL), A_WIDTH ** -0.5),
        'lam_re': lam_re,
        'lam_im': lam_im,
        'log_dt': log_dt,
        'b_re': nrm(k[13], (DEPTH, 2, B_GROUPS, B_STATE, B_GROUP_DIM), (2 * B_GROUP_DIM) ** -0.5),
        'b_im': nrm(k[14], (DEPTH, 2, B_GROUPS, B_STATE, B_GROUP_DIM), (2 * B_GROUP_DIM) ** -0.5),
        'c_re': nrm(k[15], (DEPTH, 2, B_GROUPS, B_GROUP_DIM, B_STATE), (2 * B_STATE) ** -0.5),
        'c_im': nrm(k[16], (DEPTH, 2, B_GROUPS, B_GROUP_DIM, B_STATE), (2 * B_STATE) ** -0.5),
        'd_skip': nrm(k[17], (DEPTH, B_WIDTH), 1.0),
        'w_glu': nrm(k[18], (DEPTH, B_WIDTH, 2 * D_MODEL), B_WIDTH ** -0.5),
        'w_o': nrm(k[19], (DEPTH, D_MODEL, D_MODEL), D_MODEL ** -0.5),
        'norm_post_mix': 1.0 + nrm(k[20], (DEPTH, D_MODEL), 0.02),
        'norm_pre_ff': 1.0 + nrm(k[21], (DEPTH, D_MODEL), 0.02),
        'w_ff1': nrm(k[22], (DEPTH, D_MODEL, D_FF), D_MODEL ** -0.5),
        'w_ff2': nrm(k[23], (DEPTH, D_FF, D_MODEL), D_FF ** -0.5),
        'norm_post_ff': 1.0 + nrm(k[9], (DEPTH, D_MODEL), 0.02),
    }


def reference(x_prompt, x_sample, norm_pre_mix, w_in, norm_v, w_s, b_s, w_out_a,
              lam_re, lam_im, log_dt, b_re, b_im, c_re, c_im, d_skip, w_glu, w_o,
              norm_post_mix, norm_pre_ff, w_ff1, w_ff2, norm_post_ff):
    p = {
        'norm_pre_mix': norm_pre_mix, 'w_in': w_in, 'norm_v': norm_v, 'w_s': w_s, 'b_s': b_s,
        'w_out_a': w_out_a, 'lam_re': lam_re, 'lam_im': lam_im, 'log_dt': log_dt,
        'b_re': b_re, 'b_im': b_im, 'c_re': c_re, 'c_im': c_im, 'd_skip': d_skip,
        'w_glu': w_glu, 'w_o': w_o, 'norm_post_mix': norm_post_mix,
        'norm_pre_ff': norm_pre_ff, 'w_ff1': w_ff1, 'w_ff2': w_ff2,
        'norm_post_ff': norm_post_ff,
    }
    y_prompt = trunk(x_prompt, p)
    y_sample = trunk(x_sample, p)
    return (y_prompt, y_sample)
```

```python
import contextlib
import math
import os
import numpy as np
import concourse.bass as bass
import concourse.mybir as mybir
from concourse.bass_utils import run_bass_kernel_spmd

F32 = mybir.dt.float32
BF16 = mybir.dt.bfloat16
AF = mybir.ActivationFunctionType
ALU = mybir.AluOpType

D = 1024
S = 8192
TT = 512
NT = S // TT
NB = 64
L = 2
NWB = 34
EPS = 1e-6
NCORES = 8
MAGIC = 12582912.0
C1 = 6.28125
C2 = 2.0 * math.pi - 6.28125
SINSCALE = 0.999999


class Res:
    __slots__ = ("last_w", "readers")

    def __init__(self):
        self.last_w = None
        self.readers = []


class Op:
    __slots__ = ("eng", "fn", "deps", "needed", "sig", "dma", "dbg", "tag")

    def __init__(self, eng, fn, dma):
        self.eng = eng
        self.fn = fn
        self.deps = []
        self.needed = False
        self.sig = None
        self.dma = dma


EPOCH = 16000
NDMA = 8


def _flat(x, out):
    for r in x:
        if isinstance(r, Res):
            out.append(r)
        else:
            _flat(r, out)
    return out


class Prog:
    ENG = ("pe", "act", "dve", "pool", "sp")

    def __init__(self, nc):
        self.nc = nc
        self.ops = []

    def op(self, eng, fn, reads=(), writes=(), dma=False):
        o = Op(eng, fn, dma)
        import sys as _s
        fr = _s._getframe(1)
        lines = []
        while fr is not None and len(lines) < 4:
            lines.append(fr.f_lineno)
            fr = fr.f_back
        o.dbg = lines
        o.tag = getattr(self, "cur_tag", "")
        reads = _flat(reads, [])
        writes = _flat(writes, [])
        deps = {}
        for r in reads:
            if r.last_w is not None:
                deps[id(r.last_w)] = r.last_w
        for r in writes:
            if r.last_w is not None:
                deps[id(r.last_w)] = r.last_w
            for q in r.readers:
                deps[id(q)] = q
        for r in reads:
            if not dma:
                r.readers = [q for q in r.readers if q.dma or q.eng != eng]
            r.readers.append(o)
        for r in writes:
            r.last_w = o
            r.readers = []
        deps.pop(id(o), None)
        for d in deps.values():
            if d.eng == "pe" and eng == "pe" and not d.dma and not dma:
                continue
            o.deps.append(d)
            d.needed = True
        self.ops.append(o)
        return o

    def dma(self, q, out, in_, reads=(), writes=()):
        return self.op(q, lambda e: e.dma_start(out=out, in_=in_), reads, writes, dma=True)

    def emit(self):
        nc = self.nc
        import os
        lim = int(os.environ.get("KLIMIT", "0"))
        if lim:
            self.ops = self.ops[:lim]
        cnt = {e: 0 for e in self.ENG}
        dcnt = {e: 0 for e in self.ENG}
        for o in self.ops:
            if o.dma:
                k = dcnt[o.eng]
                dcnt[o.eng] += 1
                o.sig = ("d", o.eng, k % NDMA, (k // NDMA + 1) * 16)
            elif o.needed:
                k = cnt[o.eng]
                cnt[o.eng] += 1
                o.sig = ("c", o.eng, k // EPOCH, k % EPOCH + 1)
        sems = {}
        namemap = {} if os.environ.get("KMAP") else None
        self.namemap = namemap
        with contextlib.ExitStack() as stack:
            for e in self.ENG:
                for ep in range((cnt[e] + EPOCH - 1) // EPOCH):
                    sems[("c", e, ep)] = stack.enter_context(nc.semaphore(f"c_{e}_{ep}"))
                for j in range(min(NDMA, dcnt[e])):
                    sems[("d", e, j)] = stack.enter_context(nc.semaphore(f"d_{e}_{j}"))
            block = stack.enter_context(nc.Block())
            per_eng = {e: [o for o in self.ops if o.eng == e] for e in self.ENG}

            def body(ename):
                def f(eng):
                    waited = {}

                    def wait(key, val):
                        if waited.get(key, 0) >= val:
                            return
                        eng.wait_ge(sems[key], val)
                        waited[key] = val
                    for o in per_eng[ename]:
                        for d in o.deps:
                            s = d.sig
                            wait((s[0], s[1], s[2]), s[3])
                        if o.dma and o.sig[3] > 16:
                            s = o.sig
                            wait((s[0], s[1], s[2]), s[3] - 16)
                        ins = o.fn(eng)
                        if namemap is not None:
                            namemap[ins.ins.name] = o.tag
                        if o.sig is not None:
                            s = o.sig
                            ins.then_inc(sems[(s[0], s[1], s[2])], 16 if o.dma else 1)
                    k = dcnt[ename]
                    for j in range(min(NDMA, k)):
                        n = (k - j + NDMA - 1) // NDMA
                        wait(("d", ename, j), n * 16)
                return f
            for e, meth in (("sp", block.sync), ("act", block.scalar), ("dve", block.vector),
                            ("pool", block.gpsimd), ("pe", block.tensor)):
                if per_eng[e]:
                    meth(body(e))
        if namemap is not None:
            import json as _json
            _json.dump(namemap, open(os.environ["KMAP"], "w"))
        return {e: len(v) for e, v in per_eng.items()}


UNIT = 512


class Arena:
    def __init__(self, nc, stack, nbytes):
        self.t = stack.enter_context(nc.sbuf_tensor("arena", [128, nbytes // 4], F32))
        self.units = [Res() for _ in range((nbytes + UNIT - 1) // UNIT)]
        self.top = 0
        self.nbytes = nbytes

    def alloc(self, nbytes):
        off = self.top
        self.top += (nbytes + UNIT - 1) // UNIT * UNIT
        assert self.top <= self.nbytes, (self.top, self.nbytes)
        return off


class Buf:
    def __init__(self, arena, off, dtype, shape):
        self.arena = arena
        self.off = off
        self.es = 4 if dtype == F32 else 2
        n = 1
        for s in shape:
            n *= s
        self.n = n
        ap = arena.t[:, off // 4:(off + n * self.es + 3) // 4]
        if dtype != F32:
            ap = ap.bitcast(dtype)
        if len(shape) == 2:
            ap = ap.rearrange("p (a b) -> p a b", a=shape[0])
        elif len(shape) == 3:
            ap = ap.rearrange("p (a b c) -> p a b c", a=shape[0], b=shape[1])
        elif len(shape) == 4:
            ap = ap.rearrange("p (a b c d) -> p a b c d", a=shape[0], b=shape[1], c=shape[2])
        self.ap = ap
        self.shape = shape
        self.ua = self.u(0, n)

    def u(self, lo, hi):
        b0 = (self.off + lo * self.es) // UNIT
        b1 = (self.off + hi * self.es + UNIT - 1) // UNIT
        return self.arena.units[b0:b1]

    def us(self, i, n_i):
        w = self.n // n_i
        return self.u(i * w, (i + 1) * w)


def build(dbg=False):
    nc = bass.Bass("TRN2", target_bir_lowering=False)

    def din(name, shape, dt=F32):
        return nc.dram_tensor(name, list(shape), dt, kind="ExternalInput").ap()

    def dint(name, shape, dt):
        return nc.dram_tensor(name, list(shape), dt, kind="Internal").ap()

    x_in = din("x", [S, D])
    wf = din("wf", [L, NWB, 128, 4096])
    a_lam = din("a_lam", [L, 2, 128, 2, 4096])
    a_b = din("a_b", [L, 2, 128, 2, 4096])
    a_dt = din("a_dt", [L, 2, 128, 64])
    a_pw = din("a_pw", [128, 2])
    a_dsk = din("a_dsk", [L, 128, 64])
    b_lam = din("b_lam", [L, 2, 128, 2, 32])
    b_dt = din("b_dt", [L, 2, 128, 32])
    b_c = din("b_c", [L, 2, 128, 2, 512])
    b_b = din("b_b", [L, 2, 128, 2, 512])
    cst = din("cst", [128, 8 + 8 + 8 + 8 + 64 + 128 + 128 + 128])
    gfm = din("gfm", [L, 128, 3, 8])
    gbc = din("gbc", [L, 2, 128, 1024])
    bsb = din("bsb", [L, 128, 1024])
    wst = din("wst", [L, 128, 1024])
    y_out = nc.dram_tensor("y", [S, D], F32, kind="ExternalOutput").ap()
    wb = dint("wb", [L, NWB, 128, 4096], BF16)
    s5b = dint("s5b", [L, 10, 128, 4096], BF16)
    s5t = dint("s5t", [L, 4, 128, 2048], F32)
    x_mid = dint("x_mid", [S, D], F32)
    ucache = dint("ucache", [NT, 128, 4096], BF16)
    dbgo = {}
    if dbg:
        dbgo["s5b"] = nc.dram_tensor("dbg_s5b", [L, 10, 128, 4096], F32, kind="ExternalOutput").ap()
        dbgo["s5t"] = nc.dram_tensor("dbg_s5t", [L, 4, 128, 2048], F32, kind="ExternalOutput").ap()

    P = Prog(nc)
    with contextlib.ExitStack() as es:
        AR = Arena(nc, es, 206 * 1024)
        banks = [es.enter_context(nc.psum_tensor(f"ps{i}", [128, 512], F32)) for i in range(8)]
        bres = [Res() for _ in range(8)]
        bres_h = [[Res(), Res()] for _ in range(8)]

        def buf(dtype, shape):
            n = 1
            for s_ in shape:
                n *= s_
            off = AR.alloc(n * (4 if dtype == F32 else 2))
            return Buf(AR, off, dtype, shape)

        def alias(b, byte_off, dtype, shape):
            return Buf(AR, b.off + byte_off, dtype, shape)

        TMP = buf(BF16, (1024,))
        TMPB = TMP
        IDB = buf(BF16, (128,))
        CST = buf(F32, (480,))
        GFM = buf(F32, (3, 8))
        GBC = buf(F32, (2, 1024))
        BSB = buf(F32, (8, 128))
        WST = buf(BF16, (8, 128))
        STAT = buf(F32, (64,))
        CARF = buf(F32, (2, 32))
        SBIN = buf(F32, (NT, 2, 32))
        RTAB = buf(F32, (L, 2, 32))
        WINIT = buf(F32, (2, 2, 32))
        SML = buf(F32, (8, 8))
        EPSC = buf(F32, (4,))
        PWA = buf(F32, (2,))
        ROT = buf(F32, (L, 2, 4, 32))
        WLAST = buf(F32, (2, 2, 32))
        ZT34 = [buf(F32, (8, 64)) for _ in range(2)]
        ROTT = buf(F32, (4, 32))
        MASK0 = buf(F32, (8, 64))
        RZ = [buf(F32, (8, 64))]
        RW = buf(F32, (2, 2, 32))
        main_base = AR.top
        XB = [buf(F32, (4, 1024)) for _ in range(2)]
        H = buf(BF16, (8, 512))
        SG = buf(BF16, (2, 8, 512))
        BIG = buf(BF16, (8, 1024))
        MIXF = alias(BIG, 0, F32, (4, 1024))
        RA = buf(BF16, (64, 64))
        VN = None
        RB = buf(BF16, (8, 512))
        VN = alias(RB, 0, BF16, (4, 1024))
        RC = buf(BF16, (8, 512))
        RD = buf(BF16, (8, 512))
        HN = alias(RD, 0, BF16, (4, 1024))
        HID = buf(BF16, (32, 512))
        hid_off = HID.off
        ZW = [Buf(AR, hid_off + i * 2048, F32, (8, 64)) for i in range(6)]
        SBF = Buf(AR, hid_off + 12288, BF16, (32, 2, 65))
        SBB = Buf(AR, hid_off + 12288 + 8320, BF16, (32, 2, 65))
        assert 12288 + 2 * 8320 <= 32768
        NRING = 4
        RING = [buf(F32, (2048,)) for _ in range(NRING)]
        assert AR.top - main_base >= 141 * 1024

        def TT_(eng, out, i0, i1, op, r, w):
            P.op(eng, lambda e: e.tensor_tensor(out=out, in0=i0, in1=i1, op=op), r, w)

        def TS_(eng, out, i0, s1, s2, op0, op1, r, w):
            if op1 is None:
                P.op(eng, lambda e: e.tensor_scalar(out=out, in0=i0, scalar1=s1, scalar2=None, op0=op0), r, w)
            else:
                P.op(eng, lambda e: e.tensor_scalar(out=out, in0=i0, scalar1=s1, scalar2=s2, op0=op0, op1=op1), r, w)

        def STT_(out, i0, sc, i1, op0, op1, r, w, eng="dve"):
            P.op(eng, lambda e: e.scalar_tensor_tensor(out=out, in0=i0, scalar=sc, in1=i1, op0=op0, op1=op1), r, w)

        def ACT_(out, in_, func, r, w, scale=1.0, bias=None, accum=None):
            def f(e):
                kw = {}
                if bias is not None:
                    kw["bias"] = bias
                if accum is not None:
                    kw["accum_out"] = accum
                return e.activation(out=out, in_=in_, func=func, scale=scale, **kw)
            P.op("act", f, r, w)

        def MM(out, lhsT, rhs, start, stop, r, w):
            P.op("pe", lambda e: e.matmul(out, lhsT=lhsT, rhs=rhs, start=start, stop=stop), r, w)

        def TR(out, in_, ident, r, w):
            P.op("pe", lambda e: e.transpose(out, in_, ident), r, w)

        dq = [0]

        def DMA(out, in_, r, w, q=None):
            if q is None:
                q = ("sp", "act")[dq[0] % 2]
                dq[0] += 1
            P.dma(q, out, in_, r, w)

        rw_wb = [[Res() for _ in range(NWB)] for _ in range(L)]
        rw_s5b = [[Res() for _ in range(10)] for _ in range(L)]
        rw_s5t = [[Res() for _ in range(4)] for _ in range(L)]
        r_xmid = [Res() for _ in range(NT)]
        r_yout = [Res() for _ in range(NT)]
        DMA(CST.ap, cst, [], [CST.ua])
        DMA(PWA.ap, a_pw, [], [PWA.ua])
        pw3 = [CST.ap[:, 0:8], CST.ap[:, 8:16]]
        pwL = [CST.ap[:, 16:24], CST.ap[:, 24:32]]
        qvec = CST.ap[:, 32:96]
        maskF = CST.ap[:, 96:224]
        maskB = CST.ap[:, 224:352]
        identF = CST.ap[:, 352:480]
        P.op("dve", lambda e: e.tensor_copy(out=IDB.ap, in_=identF), [CST.ua], [IDB.ua])
        P.op("pool", lambda e: e.memset(EPSC.ap[:, 0:1], EPS), [], [EPSC.ua])
        P.op("pool", lambda e: e.memset(EPSC.ap[:, 1:2], SINSCALE * math.pi / 2), [], [EPSC.ua])
        P.op("pool", lambda e: e.memset(EPSC.ap[:, 2:3], 0.0), [], [EPSC.ua])
        P.op("pool", lambda e: e.memset(MASK0.ap, 1.0), [], [MASK0.ua])
        P.op("pool", lambda e: e.memset(MASK0.ap[:, :, 0:1], 0.0), [], [MASK0.ua])
        eps_ap = EPSC.ap[:, 0:1]
        hpi_ap = EPSC.ap[:, 1:2]
        zero_ap = EPSC.ap[:, 2:3]

        for l in range(L):
            for b in list(range(4, 10)) + list(range(0, 4)) + list(range(10, NWB)):
                P.dma("pool", wb[l, b], wf[l, b], [], [rw_wb[l][b]])

        def setup_layer(l):
            base = main_base
            KB = 1024
            nA = 18
            A_ = [Buf(AR, base + i * 4096, F32, (16, 64)) for i in range(nA)]
            stg = Buf(AR, base + 72 * KB, BF16, (16, 2, 2, 64))
            dts = Buf(AR, base + 80 * KB, F32, (2, 64))

            def mul(o, a, b, eng="dve"):
                TT_(eng, o.ap, a.ap, b.ap, ALU.mult, [a.ua, b.ua], [o.ua])

            def sincos(x, c_out, s_out, t1, t2, shape_ap=lambda b: b.ap):
                for (dst, off) in ((s_out, 0.0), (c_out, 0.25)):
                    cur = x
                    for rep in range(2):
                        TS_("dve", shape_ap(t1), shape_ap(cur), 1.0 / (2 * math.pi), off, ALU.mult, ALU.add, [cur.ua], [t1.ua])
                        TS_("dve", shape_ap(t1), shape_ap(t1), MAGIC, -MAGIC, ALU.add, ALU.add, [t1.ua], [t1.ua])
                        STT_(shape_ap(t2), shape_ap(t1), -C1, shape_ap(cur), ALU.mult, ALU.add, [t1.ua, cur.ua], [t2.ua])
                        STT_(shape_ap(t2), shape_ap(t1), -C2, shape_ap(t2), ALU.mult, ALU.add, [t1.ua, t2.ua], [t2.ua])
                        cur = t2
                    ACT_(shape_ap(dst), shape_ap(t2), AF.Sin, [t2.ua, EPSC.ua], [dst.ua], scale=SINSCALE,
                         bias=(zero_ap if off == 0.0 else hpi_ap))

            def cmul(o_r, o_i, a_r, a_i, b_r, b_i, t1, t2, neg_im=False, ap=lambda b: b.ap):
                TT_("dve", ap(t1), ap(a_r), ap(b_r), ALU.mult, [a_r.ua, b_r.ua], [t1.ua])
                TT_("pool", ap(t2), ap(a_i), ap(b_i), ALU.mult, [a_i.ua, b_i.ua], [t2.ua])
                TT_("dve", ap(o_r), ap(t1), ap(t2), ALU.subtract, [t1.ua, t2.ua], [o_r.ua])
                TT_("dve", ap(t1), ap(a_r), ap(b_i), ALU.mult, [a_r.ua, b_i.ua], [t1.ua])
                TT_("pool", ap(t2), ap(a_i), ap(b_r), ALU.mult, [a_i.ua, b_r.ua], [t2.ua])
                TT_("dve", ap(o_i), ap(t1), ap(t2), ALU.add, [t1.ua, t2.ua], [o_i.ua])

            for d in range(2):
                DMA(dts.ap[:, d], a_dt[l, d], [], [dts.ua])
            ACT_(dts.ap, dts.ap, AF.Exp, [dts.ua], [dts.ua])
            for c in range(4):
                for d in range(2):
                    LR, LI, BR, BI, AR_, AI_, EA, CA, SA, T1, T2, KR, KI, QR, QI, PR, PI, T3 = A_
                    g0 = c * 16
                    DMA(LR.ap, a_lam[l, d, :, 0, g0 * 64:(g0 + 16) * 64].rearrange("p (g s) -> p g s", g=16), [], [LR.ua])
                    DMA(LI.ap, a_lam[l, d, :, 1, g0 * 64:(g0 + 16) * 64].rearrange("p (g s) -> p g s", g=16), [], [LI.ua])
                    DMA(BR.ap, a_b[l, d, :, 0, g0 * 64:(g0 + 16) * 64].rearrange("p (g s) -> p g s", g=16), [], [BR.ua])
                    DMA(BI.ap, a_b[l, d, :, 1, g0 * 64:(g0 + 16) * 64].rearrange("p (g s) -> p g s", g=16), [], [BI.ua])
                    dtb = dts.ap[:, d, g0:g0 + 16].rearrange("p (g o) -> p g o", o=1).to_broadcast([128, 16, 64])
                    TT_("dve", AR_.ap, LR.ap, dtb, ALU.mult, [LR.ua, dts.ua], [AR_.ua])
                    TT_("pool", AI_.ap, LI.ap, dtb, ALU.mult, [LI.ua, dts.ua], [AI_.ua])
                    ACT_(EA.ap, AR_.ap, AF.Exp, [AR_.ua], [EA.ua])
                    sincos(AI_, CA, SA, T1, T2)
                    mul(CA, CA, EA)
                    mul(SA, SA, EA)
                    mul(T1, LR, LR)
                    mul(T2, LI, LI, "pool")
                    TT_("dve", T1.ap, T1.ap, T2.ap, ALU.add, [T1.ua, T2.ua], [T1.ua])
                    P.op("dve", lambda e, T1=T1: e.reciprocal(out=T1.ap, in_=T1.ap), [T1.ua], [T1.ua])
                    TS_("dve", EA.ap, CA.ap, -1.0, None, ALU.add, None, [CA.ua], [EA.ua])
                    mul(T2, EA, LR)
                    mul(T3, SA, LI, "pool")
                    TT_("dve", KR.ap, T2.ap, T3.ap, ALU.add, [T2.ua, T3.ua], [KR.ua])
                    mul(T2, SA, LR)
                    mul(T3, EA, LI, "pool")
                    TT_("dve", KI.ap, T2.ap, T3.ap, ALU.subtract, [T2.ua, T3.ua], [KI.ua])
                    mul(KR, KR, T1)
                    mul(KI, KI, T1)
                    cmul(QR, QI, KR, KI, BR, BI, T1, T2)
                    pcol = PWA.ap[:, d:d + 1]
                    TS_("dve", T1.ap, AR_.ap, pcol, None, ALU.mult, None, [AR_.ua, PWA.ua], [T1.ua])
                    ACT_(EA.ap, T1.ap, AF.Exp, [T1.ua], [EA.ua])
                    TS_("dve", T3.ap, AI_.ap, pcol, None, ALU.mult, None, [AI_.ua, PWA.ua], [T3.ua])
                    sincos(T3, PR, PI, T1, T2)
                    mul(PR, PR, EA)
                    mul(PI, PI, EA)
                    TT_("dve", T1.ap, PR.ap, QR.ap, ALU.mult, [PR.ua, QR.ua], [T1.ua])
                    TT_("pool", T2.ap, PI.ap, QI.ap, ALU.mult, [PI.ua, QI.ua], [T2.ua])
                    TT_("dve", stg.ap[:, :, d, 0, :], T1.ap, T2.ap, ALU.subtract, [T1.ua, T2.ua], [stg.ua])
                    TT_("dve", T1.ap, PR.ap, QI.ap, ALU.mult, [PR.ua, QI.ua], [T1.ua])
                    TT_("pool", T2.ap, PI.ap, QR.ap, ALU.mult, [PI.ua, QR.ua], [T2.ua])
                    TT_("dve", stg.ap[:, :, d, 1, :], T1.ap, T2.ap, ALU.add, [T1.ua, T2.ua], [stg.ua])
                DMA(s5b[l, 2 + c], stg.ap.rearrange("p a b c d -> p (a b c d)"), [stg.ua], [rw_s5b[l][2 + c]])

            small = [Buf(AR, base + i * 128, F32, (32,)) for i in range(24)]
            (bLR, bLI, bDT, bAR, bAI, bEA, bCA, bSA, bT1, bT2, bT3, bKR, bKI, bR8, bPH) = small[:15]
            bC = [Buf(AR, base + 4 * KB + i * 2048, F32, (32, 16)) for i in range(2)]
            bB = [Buf(AR, base + 8 * KB + i * 2048, F32, (32, 16)) for i in range(2)]
            bQ = [Buf(AR, base + 12 * KB + i * 2048, F32, (32, 16)) for i in range(2)]
            bTq = [Buf(AR, base + 16 * KB + i * 2048, F32, (32, 16)) for i in range(2)]
            P38 = [Buf(AR, base + 20 * KB + i * 1024, F32, (32, 8)) for i in range(4)]
            P38b = [Buf(AR, base + 24 * KB + i * 1024, F32, (32, 8)) for i in range(4)]
            PL8 = [Buf(AR, base + 28 * KB + i * 1024, F32, (32, 8)) for i in range(4)]
            PL8b = [Buf(AR, base + 32 * KB + i * 1024, F32, (32, 8)) for i in range(4)]
            ANG = [Buf(AR, base + 36 * KB + i * 8192, F32, (32, 64)) for i in range(2)]
            TAB = [Buf(AR, base + 52 * KB + i * 8192, F32, (32, 64)) for i in range(2)]
            M3F = [Buf(AR, base + 68 * KB + d * 8192, F32, (8, 2, 128)) for d in range(2)]
            QLF = [Buf(AR, base + 84 * KB + d * 8192, BF16, (8, 2, 128)) for d in range(2)]
            E4 = [Buf(AR, base + 100 * KB + i * 4096, F32, (8, 8, 16)) for i in range(4)]
            stg3 = Buf(AR, base + 116 * KB, BF16, (8, 2, 2, 128))
            stg1 = Buf(AR, base + 124 * KB, BF16, (32, 128))
            M1T = [Buf(AR, base + 132 * KB + i * 2048, F32, (4, 128)) for i in range(2)]
            M1G = [Buf(AR, base + 136 * KB + i * 2048, F32, (4, 128)) for i in range(2)]
            dsk = Buf(AR, base + 140 * KB, F32, (64,))
            DMA(dsk.ap, a_dsk[l], [], [dsk.ua])

            for d in range(2):
                DMA(bLR.ap, b_lam[l, d, :, 0], [], [bLR.ua])
                DMA(bLI.ap, b_lam[l, d, :, 1], [], [bLI.ua])
                DMA(bDT.ap, b_dt[l, d], [], [bDT.ua])
                for i in range(2):
                    DMA(bC[i].ap, b_c[l, d, :, i].rearrange("p (a b) -> p a b", a=32), [], [bC[i].ua])
                    DMA(bB[i].ap, b_b[l, d, :, i].rearrange("p (a b) -> p a b", a=32), [], [bB[i].ua])
                ACT_(bDT.ap, bDT.ap, AF.Exp, [bDT.ua], [bDT.ua])
                mul(bAR, bLR, bDT)
                mul(bAI, bLI, bDT)
                ACT_(bEA.ap, bAR.ap, AF.Exp, [bAR.ua], [bEA.ua])
                sincos(bAI, bCA, bSA, bT1, bT2)
                mul(bCA, bCA, bEA)
                mul(bSA, bSA, bEA)
                mul(bT1, bLR, bLR)
                mul(bT2, bLI, bLI)
                TT_("dve", bT1.ap, bT1.ap, bT2.ap, ALU.add, [bT1.ua, bT2.ua], [bT1.ua])
                P.op("dve", lambda e: e.reciprocal(out=bT1.ap, in_=bT1.ap), [bT1.ua], [bT1.ua])
                TS_("dve", bEA.ap, bCA.ap, -1.0, None, ALU.add, None, [bCA.ua], [bEA.ua])
                mul(bT2, bEA, bLR)
                mul(bT3, bSA, bLI)
                TT_("dve", bKR.ap, bT2.ap, bT3.ap, ALU.add, [bT2.ua, bT3.ua], [bKR.ua])
                mul(bT2, bSA, bLR)
                mul(bT3, bEA, bLI)
                TT_("dve", bKI.ap, bT2.ap, bT3.ap, ALU.subtract, [bT2.ua, bT3.ua], [bKI.ua])
                mul(bKR, bKR, bT1)
                mul(bKI, bKI, bT1)
                b16 = lambda b_: b_.ap.rearrange("p (a o) -> p a o", o=1).to_broadcast([128, 32, 16])
                TT_("dve", bTq[0].ap, bB[0].ap, b16(bKR), ALU.mult, [bB[0].ua, bKR.ua], [bTq[0].ua])
                TT_("dve", bTq[1].ap, bB[1].ap, b16(bKI), ALU.mult, [bB[1].ua, bKI.ua], [bTq[1].ua])
                TT_("dve", bQ[0].ap, bTq[0].ap, bTq[1].ap, ALU.subtract, [bTq[0].ua, bTq[1].ua], [bQ[0].ua])
                TT_("dve", bTq[0].ap, bB[1].ap, b16(bKR), ALU.mult, [bB[1].ua, bKR.ua], [bTq[0].ua])
                TT_("dve", bTq[1].ap, bB[0].ap, b16(bKI), ALU.mult, [bB[0].ua, bKI.ua], [bTq[1].ua])
                TT_("dve", bQ[1].ap, bTq[0].ap, bTq[1].ap, ALU.add, [bTq[0].ua, bTq[1].ua], [bQ[1].ua])
                ACT_(RTAB.ap[:, l, d, :], bAR.ap, AF.Exp, [bAR.ua], [RTAB.ua], scale=8.0)
                TS_("dve", bPH.ap, bAI.ap, 8.0, None, ALU.mult, None, [bAI.ua], [bPH.ua])
                TT_("dve", ANG[0].ap, bPH.ap.rearrange("p (a o) -> p a o", o=1).to_broadcast([128, 32, 64]),
                    qvec.rearrange("p (o q) -> p o q", o=1).to_broadcast([128, 32, 64]), ALU.mult,
                    [bPH.ua, CST.ua], [ANG[0].ua])
                sincos(ANG[0], TAB[0], TAB[1], ANG[1], Buf(AR, E4[0].off, F32, (32, 64)))
                for cs in range(2):
                    for (ki, col) in ((0, 1), (2, 63)):
                        P.op("dve", lambda e, cs=cs, ki=ki, col=col, d=d: e.tensor_copy(out=ROT.ap[:, l, d, ki + cs, :],
                                                                                       in_=TAB[cs].ap[:, :, col]),
                             [TAB[cs].ua], [ROT.ua])
                for c in range(4):
                    for cs in range(2):
                        DMA(s5t[l, c, :, (d * 2 + cs) * 512:(d * 2 + cs + 1) * 512].rearrange("p (a q) -> p a q", a=8),
                            TAB[cs].ap[:, c * 8:(c + 1) * 8, :], [TAB[cs].ua], [rw_s5t[l][c]])
                for (PP, pv) in ((P38 if d == 0 else P38b, pw3[d]), (PL8 if d == 0 else PL8b, pwL[d])):
                    b8 = lambda b_: b_.ap.rearrange("p (a o) -> p a o", o=1).to_broadcast([128, 32, 8])
                    pvb = pv.rearrange("p (o q) -> p o q", o=1).to_broadcast([128, 32, 8])
                    TT_("dve", PP[3].ap, b8(bAR), pvb, ALU.mult, [bAR.ua, CST.ua], [PP[3].ua])
                    ACT_(PP[0].ap, PP[3].ap, AF.Exp, [PP[3].ua], [PP[0].ua])
                    TT_("dve", PP[3].ap, b8(bAI), pvb, ALU.mult, [bAI.ua, CST.ua], [PP[3].ua])
                    t_a = Buf(AR, E4[1].off, F32, (32, 8))
                    t_b = Buf(AR, E4[1].off + 1024, F32, (32, 8))
                    sincos(PP[3], PP[1], PP[2], t_a, t_b)
                    mul(PP[1], PP[1], PP[0])
                    mul(PP[2], PP[2], PP[0])
                PP3 = P38 if d == 0 else P38b
                PPL = PL8 if d == 0 else PL8b
                for c in range(4):
                    ps_ = slice(c * 8, (c + 1) * 8)

                    def bc_i(b_):
                        return b_.ap[:, ps_, :].rearrange("p a (i o) -> p a i o", o=1).to_broadcast([128, 8, 8, 16])

                    def bc_h(b_):
                        return b_.ap[:, ps_, :].rearrange("p a (o h) -> p a o h", o=1).to_broadcast([128, 8, 8, 16])

                    def o4(b_, part):
                        return b_.ap[:, :, part, :].rearrange("p a (i h) -> p a i h", i=8)
                    TT_("dve", E4[0].ap, bc_h(bC[0]), bc_i(PP3[1]), ALU.mult, [bC[0].ua, PP3[1].ua], [E4[0].ua])
                    TT_("pool", E4[1].ap, bc_h(bC[1]), bc_i(PP3[2]), ALU.mult, [bC[1].ua, PP3[2].ua], [E4[1].ua])
                    TT_("dve", o4(M3F[d], 0), E4[0].ap, E4[1].ap, ALU.subtract, [E4[0].ua, E4[1].ua], [M3F[d].ua])
                    TT_("dve", E4[0].ap, bc_h(bC[0]), bc_i(PP3[2]), ALU.mult, [bC[0].ua, PP3[2].ua], [E4[0].ua])
                    TT_("pool", E4[1].ap, bc_h(bC[1]), bc_i(PP3[1]), ALU.mult, [bC[1].ua, PP3[1].ua], [E4[1].ua])
                    STT_(o4(M3F[d], 1), E4[0].ap, -1.0, E4[1].ap, ALU.mult, ALU.subtract, [E4[0].ua, E4[1].ua], [M3F[d].ua])
                    TT_("dve", E4[2].ap, bc_h(bQ[0]), bc_i(PPL[1]), ALU.mult, [bQ[0].ua, PPL[1].ua], [E4[2].ua])
                    TT_("pool", E4[3].ap, bc_h(bQ[1]), bc_i(PPL[2]), ALU.mult, [bQ[1].ua, PPL[2].ua], [E4[3].ua])
                    TT_("dve", o4(QLF[d], 0), E4[2].ap, E4[3].ap, ALU.subtract, [E4[2].ua, E4[3].ua], [QLF[d].ua])
                    TT_("dve", E4[2].ap, bc_h(bQ[0]), bc_i(PPL[2]), ALU.mult, [bQ[0].ua, PPL[2].ua], [E4[2].ua])
                    TT_("pool", E4[3].ap, bc_h(bQ[1]), bc_i(PPL[1]), ALU.mult, [bQ[1].ua, PPL[1].ua], [E4[3].ua])
                    TT_("dve", o4(QLF[d], 1), E4[2].ap, E4[3].ap, ALU.add, [E4[2].ua, E4[3].ua], [QLF[d].ua])
                    P.op("act", lambda e, d=d: e.activation(out=stg3.ap[:, :, d, :, :], in_=M3F[d].ap, func=AF.Copy),
                         [M3F[d].ua], [stg3.ua])
                    DMA(s5b[l, 6 + c].rearrange("p (a b c e) -> p a b c e", a=8, b=2, c=2)[:, :, d, :, :],
                        stg3.ap[:, :, d, :, :], [stg3.ua], [rw_s5b[l][6 + c]])
                    for half in range(2):
                        for pl4 in range(4):
                            pl = half * 4 + pl4
                            for par in range(2):
                                rows = slice(par * 64, par * 64 + 64)
                                bk = 4 + par
                                outp = banks[bk][:, pl4 * 128:(pl4 + 1) * 128]
                                MM(outp, QLF[d].ap[rows, pl, 0, :], stg3.ap[rows, pl, d, 0, :], True, False,
                                   [QLF[d].ua, stg3.ua], [bres[bk]])
                                MM(outp, QLF[d].ap[rows, pl, 1, :], stg3.ap[rows, pl, d, 1, :], False, True,
                                   [QLF[d].ua, stg3.ua], [bres[bk]])
                        for par in range(2):
                            bk = 4 + par
                            idx = (c * 2 + half) * 2 + par
                            msk = (maskF if d == 0 else maskB).rearrange("p (o n) -> p o n", o=1).to_broadcast([128, 4, 128])
                            TT_("dve", M1T[par].ap, banks[bk][:].rearrange("p (a n) -> p a n", a=4), msk, ALU.mult,
                                [bres[bk], CST.ua], [M1T[par].ua])
                            if d == 0:
                                DMA(m1f[l, idx], M1T[par].ap.rearrange("p a n -> p (a n)"), [M1T[par].ua], [r_m1f[l][idx]])
                            else:
                                DMA(M1G[par].ap.rearrange("p a n -> p (a n)"), m1f[l, idx], [r_m1f[l][idx]], [M1G[par].ua])
                                TT_("dve", M1T[par].ap, M1T[par].ap, M1G[par].ap, ALU.add, [M1T[par].ua, M1G[par].ua], [M1T[par].ua])
                                for m4 in range(4):
                                    gg = 2 * (c * 8 + half * 4 + m4) + par
                                    STT_(stg1.ap[:, gg % 32, :], identF, dsk.ap[:, gg:gg + 1], M1T[par].ap[:, m4, :],
                                         ALU.mult, ALU.add, [CST.ua, dsk.ua, M1T[par].ua], [stg1.ua])
                    if d == 1 and c % 2 == 1:
                        DMA(s5b[l, c // 2], stg1.ap.rearrange("p a b -> p (a b)"), [stg1.ua], [rw_s5b[l][c // 2]])

        m1f = dint("m1f", [L, 16, 128, 512], F32)
        r_m1f = [[Res() for _ in range(16)] for _ in range(L)]

        for l in range(L):
            setup_layer(l)
        if dbg:
            for l in range(L):
                for b in range(10):
                    P.dma("pool", dbgo["s5b"][l, b], s5b[l, b], [rw_s5b[l][b]], [Res()])
                for b in range(4):
                    P.dma("sp", dbgo["s5t"][l, b], s5t[l, b], [rw_s5t[l][b]], [Res()])
            X = XB[0]
            P.op("pool", lambda e: e.memset(X.ap, 0.0), [], [X.ua])
            for k in range(NT):
                DMA(y_out[k * TT:(k + 1) * TT, :].rearrange("(s p) d -> p s d", p=128), X.ap, [X.ua], [r_yout[k]])
            print("ops:", P.emit())
            return nc

        PTB = [banks[6 + h][:].bitcast(BF16)[:, 0:512] for h in range(2)]
        PTR = [bres[6], bres[7]]
        pa_list = [0, 1, 2, 3]
        pa_i = [0]

        def pa_next():
            b_ = pa_list[pa_i[0] % len(pa_list)]
            pa_i[0] += 1
            return b_
        ring_i = [0]

        def ring_load(dram_ap, rres, as_f32=False):
            s_ = ring_i[0] % NRING
            ring_i[0] += 1
            rb = RING[s_]
            if as_f32:
                P.dma("sp", rb.ap, dram_ap, [rres], [rb.ua])
            else:
                P.dma("sp", rb.ap.bitcast(BF16), dram_ap, [rres], [rb.ua])
            return rb

        def wview(rb):
            return rb.ap.bitcast(BF16).rearrange("p (k c) -> p k c", k=8)

        id64 = IDB.ap[0:64, 0:64]
        pt_i = [0]

        def pt_next():
            h_ = pt_i[0] % 2
            pt_i[0] += 1
            return h_

        def rstd_from(ss_cols, n):
            ACT_(STAT.ap[:, ss_cols + 4:ss_cols + 4 + n], STAT.ap[:, ss_cols:ss_cols + n], AF.Sqrt, [STAT.ua, EPSC.ua], [STAT.ua],
                 scale=1.0 / 1024.0, bias=eps_ap)
            P.op("dve", lambda e: e.reciprocal(out=STAT.ap[:, ss_cols + 8:ss_cols + 8 + n], in_=STAT.ap[:, ss_cols + 4:ss_cols + 4 + n]),
                 [STAT.ua], [STAT.ua])

        def prenorm(gidx, X):
            for s_ in range(4):
                ACT_(TMPB.ap, X.ap[:, s_, :], AF.Square, [X.us(s_, 4)], [TMP.ua, STAT.ua], accum=STAT.ap[:, s_:s_ + 1])
            rstd_from(0, 4)
            for s_ in range(4):
                ACT_(HN.ap[:, s_, :], X.ap[:, s_, :], AF.Copy, [X.us(s_, 4), STAT.ua], [HN.us(s_, 4)], scale=STAT.ap[:, 8 + s_:9 + s_])
            for kt in range(8):
                h_ = pt_next()
                for s_ in range(4):
                    TR(PTB[h_][:, s_ * 128:(s_ + 1) * 128], HN.ap[:, s_, kt * 128:(kt + 1) * 128], IDB.ap,
                       [HN.us(s_, 4), IDB.ua], [PTR[h_]])
                TS_("dve", H.ap[:, kt, :], PTB[h_], GFM.ap[:, gidx, kt:kt + 1], None, ALU.mult, None,
                    [PTR[h_], GFM.ua], [H.us(kt, 8)])

        def dense_fm(rhs, rb, rres_w, evac):
            wv = wview(rb)
            for m in range(4):
                bk = pa_next()
                for kt in range(8):
                    MM(banks[bk][:], wv[:, kt, m * 128:(m + 1) * 128], rhs.ap[:, kt, :], kt == 0, kt == 7,
                       [rb.ua, rhs.us(kt, 8)], [bres[bk]])
                evac(m, bk)

        def ub_and_transposes(l):
            for cb in range(2):
                rb = ring_load(wb[l, 4 + cb], rw_wb[l][4 + cb])
                wv = wview(rb)
                for j in range(8):
                    bk = pa_next()
                    for kt in range(8):
                        MM(banks[bk][0:64, :], H.ap[:, kt, j:512:8], wv[:, kt, :], kt == 0, kt == 7,
                           [rb.ua, H.us(kt, 8)], [bres[bk]])
                    o_ = BIG.ap.rearrange("p a b -> p (a b)").rearrange("p (g j h) -> p g j h", g=64, j=8)[0:64, cb * 32:(cb + 1) * 32, j, :]
                    i_ = banks[bk][0:64, :].rearrange("p (g h) -> p g h", h=16)
                    if j % 2 == 0:
                        ACT_(o_, i_, AF.Copy, [bres[bk]], [BIG.u(cb * 4096, (cb + 1) * 4096)])
                    else:
                        P.op("dve", lambda e, o_=o_, i_=i_: e.tensor_copy(out=o_, in_=i_), [bres[bk]],
                             [BIG.u(cb * 4096, (cb + 1) * 4096)])
            for g8 in range(8):
                h_ = pt_next()
                for gl in range(8):
                    g = g8 * 8 + gl
                    TR(PTB[h_][:, gl * 64:(gl + 1) * 64], BIG.ap.rearrange("p a b -> p (a b)")[0:64, g * 128:(g + 1) * 128], id64,
                       [BIG.u(g * 128, (g + 1) * 128), IDB.ua], [PTR[h_]])
                P.op("dve", lambda e, h_=h_, g8=g8: e.tensor_copy(out=RA.ap[:, g8 * 8:(g8 + 1) * 8, :],
                                                                 in_=PTB[h_].rearrange("p (a b) -> p a b", a=8)),
                     [PTR[h_]], [RA.us(g8, 8)])

        def rot_small(o_re, o_im, c_, s_, i_re, i_im, rd, wr):
            t = [SML.ap[:, i, :] for i in range(4)]
            TT_("dve", t[0], c_, i_re, ALU.mult, rd, [SML.ua])
            TT_("dve", t[1], s_, i_im, ALU.mult, rd, [SML.ua])
            TT_("dve", t[2], c_, i_im, ALU.mult, rd, [SML.ua])
            TT_("dve", t[3], s_, i_re, ALU.mult, rd, [SML.ua])
            TT_("dve", o_re, t[0], t[1], ALU.subtract, [SML.ua], wr)
            TT_("dve", o_im, t[2], t[3], ALU.add, [SML.ua], wr)

        def rot32(o_re, o_im, c_, s_, i_re, i_im, rd, wr, eng):
            t = [ROTT.ap[:, i, :] for i in range(4)]
            TT_(eng, t[0], c_, i_re, ALU.mult, rd, [ROTT.ua])
            TT_(eng, t[1], s_, i_im, ALU.mult, rd, [ROTT.ua])
            TT_(eng, t[2], c_, i_im, ALU.mult, rd, [ROTT.ua])
            TT_(eng, t[3], s_, i_re, ALU.mult, rd, [ROTT.ua])
            TT_(eng, o_re, t[0], t[1], ALU.subtract, [ROTT.ua], wr)
            TT_(eng, o_im, t[2], t[3], ALU.add, [ROTT.ua], wr)

        def s5_states(l, k, dirs, pass2, fillers=()):
            Zre, Zim, Wre, Wim, T1, T2 = ZW
            T3, T4 = ZT34
            fillers = list(fillers)
            nslots = 4 * len(dirs)
            slot = [0]
            nfill = len(fillers)
            for d in dirs:
                if d == 0:
                    cin_re, cin_im, cin_u = CARF.ap[:, 0, :], CARF.ap[:, 1, :], CARF.ua
                else:
                    cin_re, cin_im, cin_u = SBIN.ap[:, k, 0, :], SBIN.ap[:, k, 1, :], SBIN.ua
                if pass2:
                    sb_ = SBF if d == 0 else SBB
                    col = 0 if d == 0 else 64
                    P.op("pool", lambda e, sb_=sb_, col=col, cin_re=cin_re: e.tensor_copy(out=sb_.ap[:, :, 0, col], in_=cin_re),
                         [cin_u], [sb_.ua])
                    P.op("pool", lambda e, sb_=sb_, col=col, cin_im=cin_im: e.tensor_copy(out=sb_.ap[:, :, 1, col], in_=cin_im),
                         [cin_u], [sb_.ua])
                rot32(WINIT.ap[:, d, 0, :], WINIT.ap[:, d, 1, :], ROT.ap[:, l, d, 0, :], ROT.ap[:, l, d, 1, :], cin_re, cin_im,
                      [cin_u, ROT.ua], [WINIT.ua], "dve")
                for part in range(2):
                    TT_("dve", RW.ap[:, d, part, :], WINIT.ap[:, d, part, :], RTAB.ap[:, l, d, :], ALU.mult, [WINIT.ua, RTAB.ua], [RW.ua])
            for c in range(4):
                rbm = ring_load(s5b[l, 2 + c], rw_s5b[l][2 + c])
                rbt = ring_load(s5t[l, c], rw_s5t[l][c], as_f32=True)
                m2v = rbm.ap.bitcast(BF16).rearrange("p (g d t s) -> p g d t s", g=16, d=2, t=2)
                tv = rbt.ap.rearrange("p (a b q) -> p a b q", a=4, b=8)
                prs = slice(c * 8, (c + 1) * 8)
                for d in dirs:
                    for part in range(2):
                        for pl in range(8):
                            for par in range(2):
                                gl = pl * 2 + par
                                g = c * 16 + gl
                                MM(banks[4 + part][par * 64:(par + 1) * 64, pl * 64:(pl + 1) * 64], m2v[:, gl, d, part, :],
                                   RA.ap[:, g, :], True, True, [rbm.ua, RA.us(g // 8, 8)], [bres[4 + part]])
                    xre = banks[4][:].rearrange("p (a q) -> p a q", a=8)
                    xim = banks[5][:].rearrange("p (a q) -> p a q", a=8)
                    rz = RZ[0]
                    TT_("pool", rz.ap, MASK0.ap, RTAB.ap[:, l, d, prs].rearrange("p (a o) -> p a o", o=1).to_broadcast([128, 8, 64]),
                        ALU.mult, [MASK0.ua, RTAB.ua], [rz.ua])
                    if d == 0:
                        ct = tv[:, 0]
                        st = tv[:, 1]
                        dct, dst_ = ct, st
                        xre_d, xim_d = xre, xim
                    else:
                        ct = tv[:, 2, :, ::-1]
                        st = tv[:, 3, :, ::-1]
                        dct, dst_ = tv[:, 2], tv[:, 3]
                        xre_d, xim_d = xre[:, :, ::-1], xim[:, :, ::-1]
                    TT_("dve", T1.ap, xre_d, dct, ALU.mult, [bres[4], rbt.ua], [T1.ua])
                    TT_("dve", T2.ap, xim_d, dst_, ALU.mult, [bres[5], rbt.ua], [T2.ua])
                    TT_("dve", T3.ap, xim_d, dct, ALU.mult, [bres[5], rbt.ua], [T3.ua])
                    TT_("dve", T4.ap, xre_d, dst_, ALU.mult, [bres[4], rbt.ua], [T4.ua])
                    TT_("dve", Zre.ap, T1.ap, T2.ap, ALU.add, [T1.ua, T2.ua], [Zre.ua])
                    TT_("pool", Zim.ap, T3.ap, T4.ap, ALU.subtract, [T3.ua, T4.ua], [Zim.ua])
                    for part, (Zp, Wp) in enumerate(((Zre, Wre), (Zim, Wim))):
                        TT_("dve", Zp.ap[:, :, 0], Zp.ap[:, :, 0], RW.ap[:, d, part, prs], ALU.add, [Zp.ua, RW.ua], [Zp.ua])
                        P.op("dve", lambda e, Wp=Wp, Zp=Zp, rz=rz: e.tensor_tensor_scan(
                            out=Wp.ap.rearrange("p a q -> p (a q)"), data0=rz.ap.rearrange("p a q -> p (a q)"),
                            data1=Zp.ap.rearrange("p a q -> p (a q)"), initial=0.0, op0=ALU.mult, op1=ALU.add),
                            [Zp.ua, rz.ua], [Wp.ua])
                        P.op("pool", lambda e, Wp=Wp, part=part, d=d, prs=prs: e.tensor_copy(out=WLAST.ap[:, d, part, prs],
                                                                                          in_=Wp.ap[:, :, 63]),
                             [Wp.ua], [WLAST.ua])
                    if pass2:
                        if d == 0:
                            wr_, wi_ = Wre.ap, Wim.ap
                            o_re = SBF.ap[:, prs, 0, 1:65]
                            o_im = SBF.ap[:, prs, 1, 1:65]
                            sbu = SBF.ua
                        else:
                            wr_, wi_ = Wre.ap[:, :, ::-1], Wim.ap[:, :, ::-1]
                            o_re = SBB.ap[:, prs, 0, 0:64]
                            o_im = SBB.ap[:, prs, 1, 0:64]
                            sbu = SBB.ua
                        TT_("dve", T1.ap, wr_, ct, ALU.mult, [Wre.ua, rbt.ua], [T1.ua])
                        TT_("pool", T2.ap, wi_, st, ALU.mult, [Wim.ua, rbt.ua], [T2.ua])
                        TT_("dve", T3.ap, wi_, ct, ALU.mult, [Wim.ua, rbt.ua], [T3.ua])
                        TT_("pool", T4.ap, wr_, st, ALU.mult, [Wre.ua, rbt.ua], [T4.ua])
                        TT_("dve", o_re, T1.ap, T2.ap, ALU.subtract, [T1.ua, T2.ua], [sbu])
                        TT_("dve", o_im, T3.ap, T4.ap, ALU.add, [T3.ua, T4.ua], [sbu])
                    slot[0] += 1
                    tgt = (slot[0] * nfill) // nslots
                    while nfill - len(fillers) < tgt:
                        fillers.pop(0)()
            while fillers:
                fillers.pop(0)()
            for d in dirs:
                if d == 0:
                    rot32(CARF.ap[:, 0, :], CARF.ap[:, 1, :], ROT.ap[:, l, d, 2, :], ROT.ap[:, l, d, 3, :],
                          WLAST.ap[:, d, 0, :], WLAST.ap[:, d, 1, :], [WLAST.ua, ROT.ua], [CARF.ua], "pool")
                elif not pass2 and k >= 1:
                    rot32(SBIN.ap[:, k - 1, 0, :], SBIN.ap[:, k - 1, 1, :], ROT.ap[:, l, d, 2, :], ROT.ap[:, l, d, 3, :],
                          WLAST.ap[:, d, 0, :], WLAST.ap[:, d, 1, :], [WLAST.ua, ROT.ua], [SBIN.ua], "pool")

        def s5_out(l):
            m1rb = None
            m3rb = None

            def back_transposes(kt):
                h_ = pt_next()
                for i in range(8):
                    TR(PTB[h_][:, i * 64:(i + 1) * 64], BIG.ap[0:64, i, kt * 128:(kt + 1) * 128], id64, [BIG.ua, IDB.ua], [PTR[h_]])
                P.op("dve", lambda e, h_=h_, kt=kt: e.tensor_copy(out=RB.ap[:, kt, :].rearrange("p (b i) -> p i b", i=8),
                                                                 in_=PTB[h_].rearrange("p (i b) -> p i b", i=8)),
                     [PTR[h_]], [RB.us(kt, 8)])
            pend = None
            for kt in range(8):
                if kt % 4 == 0:
                    m1rb = ring_load(s5b[l, kt // 4], rw_s5b[l][kt // 4])
                if kt % 2 == 0:
                    m3rb = ring_load(s5b[l, 6 + kt // 2], rw_s5b[l][6 + kt // 2])
                m1v = m1rb.ap.bitcast(BF16).rearrange("p (g n) -> p g n", g=32)
                m3v = m3rb.ap.bitcast(BF16).rearrange("p (a d t n) -> p a d t n", a=8, d=2, t=2)
                ybk = (4, 5) if kt % 2 == 0 else (2, 3)
                for gl in range(8):
                    g = kt * 8 + gl
                    pair, par = g // 2, g % 2
                    rows = slice(par * 64, par * 64 + 64)
                    bk = ybk[par]
                    m_ = gl // 2
                    outp = banks[bk][0:64, m_ * 128:(m_ + 1) * 128]
                    pin = pair % 8
                    MM(outp, RA.ap[:, g, :], m1v[:, g % 32, :], True, False, [RA.us(g // 8, 8), m1rb.ua], [bres[bk]])
                    MM(outp, SBF.ap[rows, pair, 0, 0:64], m3v[rows, pin, 0, 0, :], False, False, [SBF.ua, m3rb.ua], [bres[bk]])
                    MM(outp, SBF.ap[rows, pair, 1, 0:64], m3v[rows, pin, 0, 1, :], False, False, [SBF.ua, m3rb.ua], [bres[bk]])
                    MM(outp, SBB.ap[rows, pair, 0, 1:65], m3v[rows, pin, 1, 0, :], False, False, [SBB.ua, m3rb.ua], [bres[bk]])
                    MM(outp, SBB.ap[rows, pair, 1, 1:65], m3v[rows, pin, 1, 1, :], False, True, [SBB.ua, m3rb.ua], [bres[bk]])
                for par in range(2):
                    bk = ybk[par]
                    in_ = banks[bk][0:64, :].rearrange("p (m i h) -> p m i h", m=4, i=8)
                    o_ = BIG.ap[0:64, :, kt * 128:(kt + 1) * 128].rearrange("p i (m r h) -> p m r i h", m=4, r=2)[:, :, par, :, :]
                    ACT_(o_, in_, AF.Gelu_apprx_tanh, [bres[bk]], [BIG.u(kt * 128, 7 * 1024 + (kt + 1) * 128)])
                if pend is not None:
                    back_transposes(pend)
                pend = kt
            back_transposes(pend)

        def tm_project(l, blocks, lhs, ktn):
            for cb in range(2):
                bks = [pa_list[i] for i in range(4)]
                nkc = len(blocks[cb])
                for kc, bid in enumerate(blocks[cb]):
                    rb = ring_load(wb[l, bid], rw_wb[l][bid])
                    wv = wview(rb)
                    for s_ in range(4):
                        for kt in range(8):
                            MM(banks[bks[s_]][:], lhs.ap[:, kc * 8 + kt, s_ * 128:(s_ + 1) * 128], wv[:, kt, :],
                               kc == 0 and kt == 0, kc == nkc - 1 and kt == 7, [rb.ua, lhs.us(kc * 8 + kt, ktn)], [bres[bks[s_]]])
                for s_ in range(4):
                    ACT_(MIXF.ap[:, s_, cb * 512:(cb + 1) * 512], banks[bks[s_]][:], AF.Copy, [bres[bks[s_]]],
                         [MIXF.u(s_ * 1024 + cb * 512, s_ * 1024 + cb * 512 + 512)])

        def postnorm_add(gidx, X):
            for s_ in range(4):
                ACT_(TMPB.ap, MIXF.ap[:, s_, :], AF.Square, [MIXF.us(s_, 4)], [TMP.ua, STAT.ua], accum=STAT.ap[:, 16 + s_:17 + s_])
            rstd_from(16, 4)
            for s_ in range(4):
                STT_(MIXF.ap[:, s_, :], MIXF.ap[:, s_, :], STAT.ap[:, 24 + s_:25 + s_], GBC.ap[:, gidx, :], ALU.mult, ALU.mult,
                     [MIXF.us(s_, 4), STAT.ua, GBC.ua], [MIXF.us(s_, 4)])
                TT_("dve", X.ap[:, s_, :], X.ap[:, s_, :], MIXF.ap[:, s_, :], ALU.add, [X.us(s_, 4), MIXF.us(s_, 4)], [X.us(s_, 4)])

        def load_layer_consts(l):
            DMA(GFM.ap, gfm[l], [], [GFM.ua], q="sp")
            DMA(GBC.ap, gbc[l].rearrange("i p d -> p i d"), [], [GBC.ua], q="sp")
            DMA(BSB.ap.rearrange("p a b -> p (a b)"), bsb[l], [], [BSB.ua], q="sp")
            DMA(RING[0].ap[:, 0:1024], wst[l], [], [RING[0].ua], q="sp")
            P.op("dve", lambda e: e.tensor_copy(out=WST.ap.rearrange("p a b -> p (a b)"), in_=RING[0].ap[:, 0:1024]), [RING[0].ua], [WST.ua])
            P.op("pool", lambda e: e.memset(CARF.ap, 0.0), [], [CARF.ua])
            P.op("pool", lambda e: e.memset(SBIN.ap[:, NT - 1], 0.0), [], [SBIN.ua])

        import os as _os
        TAPS = _os.environ.get("KTAPS", "") == "1"

        def tap(name, b_, l, k):
            if not (TAPS and l == 0 and k == 0):
                return
            dt_ = F32 if b_.es == 4 else BF16
            o_ = nc.dram_tensor("tap_" + name, [128, b_.n], dt_, kind="ExternalOutput").ap()
            flat = b_.ap
            if len(b_.shape) == 2:
                flat = flat.rearrange("p a b -> p (a b)")
            elif len(b_.shape) == 3:
                flat = flat.rearrange("p a b c -> p (a b c)")
            P.dma("sp", o_, flat, [b_.ua], [Res()])

        xstate = {"i": 0, "pending": None}
        r_uc = [Res() for _ in range(NT)]

        def x_fetch(src_ap, rsrc, k, key):
            if xstate["pending"] == key:
                xstate["i"] += 1
                xstate["pending"] = None
                return XB[xstate["i"] % 2]
            Xb = XB[xstate["i"] % 2]
            P.dma("sp", Xb.ap, src_ap[k * TT:(k + 1) * TT, :].rearrange("(s p) d -> p s d", p=128), [rsrc[k]], [Xb.ua])
            return Xb

        def x_prefetch(src_ap, rsrc, k, key):
            Xn = XB[(xstate["i"] + 1) % 2]
            P.dma("sp", Xn.ap, src_ap[k * TT:(k + 1) * TT, :].rearrange("(s p) d -> p s d", p=128), [rsrc[k]], [Xn.ua])
            xstate["pending"] = key

        def branch_a_items(l, k):
            items = []
            for b in range(4):
                def it(b=b):
                    rb = ring_load(wb[l, 6 + b], rw_wb[l][6 + b])

                    def ev(m, bk):
                        mm = (b % 2) * 4 + m
                        ACT_(SG.ap[:, b // 2, mm, :], banks[bk][:], AF.Sigmoid, [bres[bk]], [SG.us((b // 2) * 8 + mm, 16)])
                    dense_fm(H, rb, None, ev)
                items.append(it)
            for b in (0, 1):
                def it(b=b):
                    rb = ring_load(wb[l, b], rw_wb[l][b])

                    def ev(m, bk):
                        mm = b * 4 + m
                        ACT_(RC.ap[:, mm, :], banks[bk][:], AF.Gelu_apprx_tanh, [bres[bk]], [RC.us(mm, 8)])
                    dense_fm(H, rb, None, ev)
                items.append(it)
            for cb in range(2):
                def it(cb=cb):
                    rb = ring_load(wb[l, 2 + cb], rw_wb[l][2 + cb])
                    wv = wview(rb)
                    for s_ in range(4):
                        bk = pa_next()
                        for kt in range(8):
                            MM(banks[bk][:], H.ap[:, kt, s_ * 128:(s_ + 1) * 128], wv[:, kt, :], kt == 0, kt == 7,
                               [rb.ua, H.us(kt, 8)], [bres[bk]])
                        ACT_(VN.ap[:, s_, cb * 512:(cb + 1) * 512], banks[bk][:], AF.Gelu_apprx_tanh, [bres[bk]],
                             [VN.u(s_ * 1024 + cb * 512, s_ * 1024 + cb * 512 + 512)])
                items.append(it)

            def vnorm():
                for s_ in range(4):
                    ACT_(TMPB.ap, VN.ap[:, s_, :], AF.Square, [VN.us(s_, 4)], [TMP.ua, STAT.ua], accum=STAT.ap[:, 32 + s_:33 + s_])
                rstd_from(32, 4)
                for s_ in range(4):
                    P.op("act", lambda e, s_=s_: e.activation(out=VN.ap[:, s_, :], in_=VN.ap[:, s_, :], func=AF.Copy,
                                                              scale=STAT.ap[:, 40 + s_:41 + s_]),
                         [VN.us(s_, 4), STAT.ua], [VN.us(s_, 4)])
            items.append(vnorm)
            for half in range(2):
                def it(half=half):
                    for g in range(half * 4, half * 4 + 4):
                        bk = pa_next()
                        for s_ in range(4):
                            MM(banks[bk][:, s_ * 128:(s_ + 1) * 128], VN.ap[:, s_, g * 128:(g + 1) * 128], WST.ap[:, g, :], True, True,
                               [VN.us(s_, 4), WST.ua], [bres[bk]])
                        STT_(RD.ap[:, g, :].rearrange("p (s q) -> p s q", s=4), banks[bk][:].rearrange("p (s q) -> p s q", s=4),
                             GFM.ap[:, 2, g:g + 1], BSB.ap[:, g, :].rearrange("p (o q) -> p o q", o=1).to_broadcast([128, 4, 128]),
                             ALU.mult, ALU.add, [bres[bk], GFM.ua, BSB.ua], [RD.us(g, 8)])
                        TT_("pool", RC.ap[:, g, :], RC.ap[:, g, :], RD.ap[:, g, :], ALU.mult, [RC.us(g, 8), RD.us(g, 8)], [RC.us(g, 8)])
                items.append(it)
            for b in (0, 1):
                def it(b=b):
                    rb = ring_load(wb[l, 10 + b], rw_wb[l][10 + b])

                    def ev(m, bk):
                        mm = b * 4 + m
                        TT_("dve", SG.ap[:, 0, mm, :], banks[bk][:], SG.ap[:, 0, mm, :], ALU.mult, [bres[bk], SG.us(mm, 16)],
                            [SG.us(mm, 16)])
                    dense_fm(RC, rb, None, ev)
                items.append(it)
            return items

        def layer(l, src_ap, rsrc, dst_ap, rdst):
            load_layer_consts(l)
            for k in range(NT - 1, 0, -1):
                P.cur_tag = f"L{l}P1T{k}:prenorm"
                X = x_fetch(src_ap, rsrc, k, (l, k))
                x_prefetch(src_ap, rsrc, k - 1, (l, k - 1))
                prenorm(0, X)
                P.cur_tag = f"L{l}P1T{k}:ub"
                ub_and_transposes(l)
                P.dma("sp", ucache[k], RA.ap.rearrange("p a b -> p (a b)"), [RA.ua], [r_uc[k]])
                P.cur_tag = f"L{l}P1T{k}:s5st"
                s5_states(l, k, [1], False)
            for k in range(NT):
                P.cur_tag = f"L{l}P2T{k}:prenorm"
                X = x_fetch(src_ap, rsrc, k, (l, k))
                if k + 1 < NT:
                    x_prefetch(src_ap, rsrc, k + 1, (l, k + 1))
                prenorm(0, X)
                P.cur_tag = f"L{l}P2T{k}:ub"
                if k == 0:
                    ub_and_transposes(l)
                else:
                    P.dma("sp", RA.ap.rearrange("p a b -> p (a b)"), ucache[k], [r_uc[k]], [RA.ua])
                tap("H", H, l, k)
                tap("RA", RA, l, k)
                P.cur_tag = f"L{l}P2T{k}:s5st"
                s5_states(l, k, [0, 1], True, fillers=branch_a_items(l, k))
                tap("SBF", SBF, l, k)
                tap("SBB", SBB, l, k)
                tap("AIN", RC, l, k)
                P.cur_tag = f"L{l}P2T{k}:s5out"
                s5_out(l)
                tap("RB", RB, l, k)
                P.cur_tag = f"L{l}P2T{k}:glu"
                for b in (2, 3):
                    rb = ring_load(wb[l, 12 + b], rw_wb[l][12 + b])

                    def ev(m, bk, b=b):
                        mm = (b - 2) * 4 + m
                        ACT_(RD.ap[:, mm, :], banks[bk][:], AF.Sigmoid, [bres[bk]], [RD.us(mm, 8)])
                        TT_("pool", RD.ap[:, mm, :], RD.ap[:, mm, :], SG.ap[:, 1, mm, :], ALU.mult, [RD.us(mm, 8), SG.us(8 + mm, 16)],
                            [RD.us(mm, 8)])
                    dense_fm(RB, rb, None, ev)
                for b in (0, 1):
                    rb = ring_load(wb[l, 12 + b], rw_wb[l][12 + b])

                    def ev(m, bk, b=b):
                        mm = b * 4 + m
                        TT_("dve", SG.ap[:, 1, mm, :], banks[bk][:], RD.ap[:, mm, :], ALU.mult, [bres[bk], RD.us(mm, 8)],
                            [SG.us(8 + mm, 16)])
                        TT_("pool", SG.ap[:, 0, mm, :], SG.ap[:, 0, mm, :], SG.ap[:, 1, mm, :], ALU.add,
                            [SG.us(mm, 16), SG.us(8 + mm, 16)], [SG.us(mm, 16)])
                    dense_fm(RB, rb, None, ev)
                P.cur_tag = f"L{l}P2T{k}:wo"
                SG0 = Buf(AR, SG.off, BF16, (8, 512))
                tap("MIXIN", SG0, l, k)
                tm_project(l, [[16], [17]], SG0, 8)
                tap("MIXF", MIXF, l, k)
                postnorm_add(0, X)
                tap("X1", X, l, k)
                P.cur_tag = f"L{l}P2T{k}:prenorm2"
                prenorm(1, X)
                P.cur_tag = f"L{l}P2T{k}:ff1"
                for b in range(8):
                    rb = ring_load(wb[l, 18 + b], rw_wb[l][18 + b])

                    def ev(m, bk, b=b):
                        mm = b * 4 + m
                        ACT_(RD.ap[:, mm % 8, :], banks[bk][:], AF.Square, [bres[bk]], [RD.us(mm % 8, 8)])
                        STT_(HID.ap[:, mm, :], banks[bk][:], 0.0, RD.ap[:, mm % 8, :], ALU.is_gt, ALU.mult,
                             [bres[bk], RD.us(mm % 8, 8)], [HID.us(mm, 32)])
                    dense_fm(H, rb, None, ev)
                P.cur_tag = f"L{l}P2T{k}:ff2"
                tap("HID", HID, l, k)
                tm_project(l, [[26, 27, 28, 29], [30, 31, 32, 33]], HID, 32)
                tap("FF", MIXF, l, k)
                postnorm_add(1, X)
                tap("X2", X, l, k)
                P.dma("sp", dst_ap[k * TT:(k + 1) * TT, :].rearrange("(s p) d -> p s d", p=128), X.ap, [X.ua], [rdst[k]])

        r_xin = [Res() for _ in range(NT)]
        layer(0, x_in, r_xin, x_mid, r_xmid)
        layer(1, x_mid, r_xmid, y_out, r_yout)
        print("arena top", AR.top, "of", AR.nbytes)
        print("ops:", P.emit())
    return nc


def host_layouts(p):
    f = np.float32
    out = {}
    wf = np.zeros((L, NWB, 128, 4096), f)

    def blk(w, kc, c0):
        return w[kc * 1024:(kc + 1) * 1024, c0:c0 + 512].reshape(8, 128, 512).transpose(1, 0, 2).reshape(128, 4096)
    for l in range(L):
        for b in range(10):
            wf[l, b] = blk(p["w_in"][l], 0, b * 512)
        for b in range(2):
            wf[l, 10 + b] = blk(p["w_out_a"][l], 0, b * 512)
        for b in range(4):
            wf[l, 12 + b] = blk(p["w_glu"][l], 0, b * 512)
        for b in range(2):
            wf[l, 16 + b] = blk(p["w_o"][l], 0, b * 512)
        for b in range(8):
            wf[l, 18 + b] = blk(p["w_ff1"][l], 0, b * 512)
        for cb in range(2):
            for kc in range(4):
                wf[l, 26 + cb * 4 + kc] = blk(p["w_ff2"][l], kc, cb * 512)
    out["wf"] = wf
    hh = np.arange(128) % 16
    jj = np.arange(128) // 16
    a_lam = np.zeros((L, 2, 128, 2, 4096), f)
    a_b = np.zeros((L, 2, 128, 2, 4096), f)
    a_dt = np.zeros((L, 2, 128, 64), f)
    for l in range(L):
        for d in range(2):
            a_lam[l, d, :, 0] = p["lam_re"][l, d].reshape(1, 4096)
            a_lam[l, d, :, 1] = p["lam_im"][l, d].reshape(1, 4096)
            a_b[l, d, :, 0] = p["b_re"][l, d][:, :, hh].transpose(2, 0, 1).reshape(128, 4096)
            a_b[l, d, :, 1] = p["b_im"][l, d][:, :, hh].transpose(2, 0, 1).reshape(128, 4096)
            a_dt[l, d] = p["log_dt"][l, d][None, :]
    out["a_lam"], out["a_b"], out["a_dt"] = a_lam, a_b, a_dt
    a_pw = np.zeros((128, 2), f)
    a_pw[:, 0] = 7 - jj
    a_pw[:, 1] = jj
    out["a_pw"] = a_pw
    a_dsk = np.zeros((L, 128, 64), f)
    for l in range(L):
        a_dsk[l] = p["d_skip"][l].reshape(64, 16)[:, hh].T
    out["a_dsk"] = a_dsk
    par = np.arange(128) // 64
    ss = np.arange(128) % 64
    b_lam = np.zeros((L, 2, 128, 2, 32), f)
    b_dt = np.zeros((L, 2, 128, 32), f)
    b_c = np.zeros((L, 2, 128, 2, 512), f)
    b_b = np.zeros((L, 2, 128, 2, 512), f)
    for l in range(L):
        for d in range(2):
            for pp in range(2):
                rows = slice(pp * 64, pp * 64 + 64)
                b_lam[l, d, rows, 0] = p["lam_re"][l, d][pp::2].T
                b_lam[l, d, rows, 1] = p["lam_im"][l, d][pp::2].T
                b_dt[l, d, rows] = p["log_dt"][l, d][pp::2][None, :]
                b_c[l, d, rows, 0] = p["c_re"][l, d][pp::2].transpose(2, 0, 1).reshape(64, 512)
                b_c[l, d, rows, 1] = p["c_im"][l, d][pp::2].transpose(2, 0, 1).reshape(64, 512)
                b_b[l, d, rows, 0] = p["b_re"][l, d][pp::2].transpose(1, 0, 2).reshape(64, 512)
                b_b[l, d, rows, 1] = p["b_im"][l, d][pp::2].transpose(1, 0, 2).reshape(64, 512)
    out["b_lam"], out["b_dt"], out["b_c"], out["b_b"] = b_lam, b_dt, b_c, b_b
    cst = np.zeros((128, 480), f)
    i8 = np.arange(8)
    cst[:, 0:8] = i8 + 1
    cst[:, 8:16] = 8 - i8
    cst[:, 16:24] = -(i8 + 1)
    cst[:, 24:32] = i8 - 8
    cst[:, 32:96] = np.arange(64)
    ji = np.arange(128) // 16
    cst[:, 96:224] = (ji[None, :] >= ji[:, None])
    cst[:, 224:352] = (ji[None, :] <= ji[:, None])
    cst[:, 352:480] = np.eye(128)
    out["cst"] = cst
    gfm = np.zeros((L, 128, 3, 8), f)
    gbc = np.zeros((L, 2, 128, 1024), f)
    bsb = np.zeros((L, 128, 1024), f)
    wst = np.zeros((L, 128, 1024), f)
    for l in range(L):
        gfm[l, :, 0] = p["norm_pre_mix"][l].reshape(8, 128).T
        gfm[l, :, 1] = p["norm_pre_ff"][l].reshape(8, 128).T
        gfm[l, :, 2] = p["norm_v"][l].reshape(8, 128).T
        gbc[l, 0] = p["norm_post_mix"][l][None, :]
        gbc[l, 1] = p["norm_post_ff"][l][None, :]
        bsb[l] = p["b_s"][l].reshape(1, 1024)
        wst[l] = p["w_s"][l].transpose(2, 0, 1).reshape(128, 1024)
    out["gfm"], out["gbc"], out["bsb"], out["wst"] = gfm, gbc, bsb, wst
    return out


_NC_CACHE = {}


def kernel(**inputs):
    p = {k: np.asarray(v, dtype=np.float32) for k, v in inputs.items()}
    xs = [p["x_prompt"][i] for i in range(2)] + [p["x_sample"][i] for i in range(4)]
    lay = host_layouts(p)
    if "nc" not in _NC_CACHE:
        _NC_CACHE["nc"] = build()
    nc = _NC_CACHE["nc"]
    in_maps = []
    for c in range(NCORES):
        m = dict(lay)
        m["x"] = np.ascontiguousarray(xs[c % 6])
        in_maps.append(m)
    res = run_bass_kernel_spmd(nc, in_maps, core_ids=list(range(NCORES)))
    ys = [res.results[c]["y"] for c in range(6)]
    y_prompt = np.stack(ys[0:2]).astype(np.float32)
    y_sample = np.stack(ys[2:6]).astype(np.float32)
    return (y_prompt, y_sample)
```

```python
import contextlib
import math
import os
import numpy as np
import concourse.bass as bass
import concourse.mybir as mybir
from concourse.bass_utils import run_bass_kernel_spmd

F32 = mybir.dt.float32
BF16 = mybir.dt.bfloat16
AF = mybir.ActivationFunctionType
ALU = mybir.AluOpType

D = 1024
S = 8192
TT = 512
NT = S // TT
NB = 64
L = 2
NWB = 34
EPS = 1e-6
NCORES = 8
MAGIC = 12582912.0
C1 = 6.28125
C2 = 2.0 * math.pi - 6.28125
SINSCALE = 0.999999


class Res:
    __slots__ = ("last_w", "readers")

    def __init__(self):
        self.last_w = None
        self.readers = []


class Op:
    __slots__ = ("eng", "fn", "deps", "needed", "sig", "dma", "dbg", "tag")

    def __init__(self, eng, fn, dma):
        self.eng = eng
        self.fn = fn
        self.deps = []
        self.needed = False
        self.sig = None
        self.dma = dma


EPOCH = 16000
NDMA = 8


def _flat(x, out):
    for r in x:
        if isinstance(r, Res):
            out.append(r)
        else:
            _flat(r, out)
    return out


class Prog:
    ENG = ("pe", "act", "dve", "pool", "sp")

    def __init__(self, nc):
        self.nc = nc
        self.ops = []

    def op(self, eng, fn, reads=(), writes=(), dma=False):
        o = Op(eng, fn, dma)
        import sys as _s
        fr = _s._getframe(1)
        lines = []
        while fr is not None and len(lines) < 4:
            lines.append(fr.f_lineno)
            fr = fr.f_back
        o.dbg = lines
        o.tag = getattr(self, "cur_tag", "")
        reads = _flat(reads, [])
        writes = _flat(writes, [])
        deps = {}
        for r in reads:
            if r.last_w is not None:
                deps[id(r.last_w)] = r.last_w
        for r in writes:
            if r.last_w is not None:
                deps[id(r.last_w)] = r.last_w
            for q in r.readers:
                deps[id(q)] = q
        for r in reads:
            if not dma:
                r.readers = [q for q in r.readers if q.dma or q.eng != eng]
            r.readers.append(o)
        for r in writes:
            r.last_w = o
            r.readers = []
        deps.pop(id(o), None)
        for d in deps.values():
            if d.eng == "pe" and eng == "pe" and not d.dma and not dma:
                continue
            o.deps.append(d)
            d.needed = True
        self.ops.append(o)
        return o

    def dma(self, q, out, in_, reads=(), writes=()):
        return self.op(q, lambda e: e.dma_start(out=out, in_=in_), reads, writes, dma=True)

    def emit(self):
        nc = self.nc
        import os
        lim = int(os.environ.get("KLIMIT", "0"))
        if lim:
            self.ops = self.ops[:lim]
        cnt = {e: 0 for e in self.ENG}
        dcnt = {e: 0 for e in self.ENG}
        for o in self.ops:
            if o.dma:
                k = dcnt[o.eng]
                dcnt[o.eng] += 1
                o.sig = ("d", o.eng, k % NDMA, (k // NDMA + 1) * 16)
            elif o.needed:
                k = cnt[o.eng]
                cnt[o.eng] += 1
                o.sig = ("c", o.eng, k // EPOCH, k % EPOCH + 1)
        sems = {}
        namemap = {} if os.environ.get("KMAP") else None
        self.namemap = namemap
        with contextlib.ExitStack() as stack:
            for e in self.ENG:
                for ep in range((cnt[e] + EPOCH - 1) // EPOCH):
                    sems[("c", e, ep)] = stack.enter_context(nc.semaphore(f"c_{e}_{ep}"))
                for j in range(min(NDMA, dcnt[e])):
                    sems[("d", e, j)] = stack.enter_context(nc.semaphore(f"d_{e}_{j}"))
            block = stack.enter_context(nc.Block())
            per_eng = {e: [o for o in self.ops if o.eng == e] for e in self.ENG}

            def body(ename):
                def f(eng):
                    waited = {}

                    def wait(key, val):
                        if waited.get(key, 0) >= val:
                            return
                        eng.wait_ge(sems[key], val)
                        waited[key] = val
                    for o in per_eng[ename]:
                        for d in o.deps:
                            s = d.sig
                            wait((s[0], s[1], s[2]), s[3])
                        if o.dma and o.sig[3] > 16:
                            s = o.sig
                            wait((s[0], s[1], s[2]), s[3] - 16)
                        ins = o.fn(eng)
                        if namemap is not None:
                            namemap[ins.ins.name] = o.tag
                        if o.sig is not None:
                            s = o.sig
                            ins.then_inc(sems[(s[0], s[1], s[2])], 16 if o.dma else 1)
                    k = dcnt[ename]
                    for j in range(min(NDMA, k)):
                        n = (k - j + NDMA - 1) // NDMA
                        wait(("d", ename, j), n * 16)
                return f
            for e, meth in (("sp", block.sync), ("act", block.scalar), ("dve", block.vector),
                            ("pool", block.gpsimd), ("pe", block.tensor)):
                if per_eng[e]:
                    meth(body(e))
        if namemap is not None:
            import json as _json
            _json.dump(namemap, open(os.environ["KMAP"], "w"))
        return {e: len(v) for e, v in per_eng.items()}


UNIT = 512


class Arena:
    def __init__(self, nc, stack, nbytes):
        self.t = stack.enter_context(nc.sbuf_tensor("arena", [128, nbytes // 4], F32))
        self.units = [Res() for _ in range((nbytes + UNIT - 1) // UNIT)]
        self.top = 0
        self.nbytes = nbytes

    def alloc(self, nbytes):
        off = self.top
        self.top += (nbytes + UNIT - 1) // UNIT * UNIT
        assert self.top <= self.nbytes, (self.top, self.nbytes)
        return off


class Buf:
    def __init__(self, arena, off, dtype, shape):
        self.arena = arena
        self.off = off
        self.es = 4 if dtype == F32 else 2
        n = 1
        for s in shape:
            n *= s
        self.n = n
        ap = arena.t[:, off // 4:(off + n * self.es + 3) // 4]
        if dtype != F32:
            ap = ap.bitcast(dtype)
        if len(shape) == 2:
            ap = ap.rearrange("p (a b) -> p a b", a=shape[0])
        elif len(shape) == 3:
            ap = ap.rearrange("p (a b c) -> p a b c", a=shape[0], b=shape[1])
        elif len(shape) == 4:
            ap = ap.rearrange("p (a b c d) -> p a b c d", a=shape[0], b=shape[1], c=shape[2])
        self.ap = ap
        self.shape = shape
        self.ua = self.u(0, n)

    def u(self, lo, hi):
        b0 = (self.off + lo * self.es) // UNIT
        b1 = (self.off + hi * self.es + UNIT - 1) // UNIT
        return self.arena.units[b0:b1]

    def us(self, i, n_i):
        w = self.n // n_i
        return self.u(i * w, (i + 1) * w)


def build(dbg=False):
    nc = bass.Bass("TRN2", target_bir_lowering=False)

    def din(name, shape, dt=F32):
        return nc.dram_tensor(name, list(shape), dt, kind="ExternalInput").ap()

    def dint(name, shape, dt):
        return nc.dram_tensor(name, list(shape), dt, kind="Internal").ap()

    x_in = din("x", [S, D])
    wf = din("wf", [L, NWB, 128, 4096])
    a_lam = din("a_lam", [L, 2, 128, 2, 4096])
    a_b = din("a_b", [L, 2, 128, 2, 4096])
    a_dt = din("a_dt", [L, 2, 128, 64])
    a_pw = din("a_pw", [128, 2])
    a_dsk = din("a_dsk", [L, 128, 64])
    b_lam = din("b_lam", [L, 2, 128, 2, 32])
    b_dt = din("b_dt", [L, 2, 128, 32])
    b_c = din("b_c", [L, 2, 128, 2, 512])
    b_b = din("b_b", [L, 2, 128, 2, 512])
    cst = din("cst", [128, 8 + 8 + 8 + 8 + 64 + 128 + 128 + 128])
    gfm = din("gfm", [L, 128, 3, 8])
    gbc = din("gbc", [L, 2, 128, 1024])
    bsb = din("bsb", [L, 128, 1024])
    wst = din("wst", [L, 128, 1024])
    y_out = nc.dram_tensor("y", [S, D], F32, kind="ExternalOutput").ap()
    wb = dint("wb", [L, NWB, 128, 4096], BF16)
    s5b = dint("s5b", [L, 10, 128, 4096], BF16)
    s5t = dint("s5t", [L, 4, 128, 2048], F32)
    x_mid = dint("x_mid", [S, D], F32)
    ucache = dint("ucache", [NT, 128, 4096], BF16)
    dbgo = {}
    if dbg:
        dbgo["s5b"] = nc.dram_tensor("dbg_s5b", [L, 10, 128, 4096], F32, kind="ExternalOutput").ap()
        dbgo["s5t"] = nc.dram_tensor("dbg_s5t", [L, 4, 128, 2048], F32, kind="ExternalOutput").ap()

    P = Prog(nc)
    with contextlib.ExitStack() as es:
        AR = Arena(nc, es, 206 * 1024)
        banks = [es.enter_context(nc.psum_tensor(f"ps{i}", [128, 512], F32)) for i in range(8)]
        bres = [Res() for _ in range(8)]
        bres_h = [[Res(), Res()] for _ in range(8)]

        def buf(dtype, shape):
            n = 1
            for s_ in shape:
                n *= s_
            off = AR.alloc(n * (4 if dtype == F32 else 2))
            return Buf(AR, off, dtype, shape)

        def alias(b, byte_off, dtype, shape):
            return Buf(AR, b.off + byte_off, dtype, shape)

        TMP = buf(BF16, (1024,))
        TMPB = TMP
        IDB = buf(BF16, (128,))
        CST = buf(F32, (480,))
        GFM = buf(F32, (3, 8))
        GBC = buf(F32, (2, 1024))
        BSB = buf(F32, (8, 128))
        WST = buf(BF16, (8, 128))
        STAT = buf(F32, (64,))
        CARF = buf(F32, (2, 32))
        SBIN = buf(F32, (NT, 2, 32))
        RTAB = buf(F32, (L, 2, 32))
        WINIT = buf(F32, (2, 2, 32))
        SML = buf(F32, (8, 8))
        EPSC = buf(F32, (4,))
        PWA = buf(F32, (2,))
        ROT = buf(F32, (L, 2, 4, 32))
        WLAST = buf(F32, (2, 2, 32))
        ZT34 = [buf(F32, (8, 64)) for _ in range(2)]
        ROTT = buf(F32, (4, 32))
        MASK0 = buf(F32, (8, 64))
        RZ = [buf(F32, (8, 64))]
        RW = buf(F32, (2, 2, 32))
        main_base = AR.top
        XB = [buf(F32, (4, 1024)) for _ in range(2)]
        H = buf(BF16, (8, 512))
        SG = buf(BF16, (2, 8, 512))
        BIG = buf(BF16, (8, 1024))
        MIXF = alias(BIG, 0, F32, (4, 1024))
        RA = buf(BF16, (64, 64))
        VN = None
        RB = buf(BF16, (8, 512))
        VN = alias(RB, 0, BF16, (4, 1024))
        RC = buf(BF16, (8, 512))
        RD = buf(BF16, (8, 512))
        HN = alias(RD, 0, BF16, (4, 1024))
        HID = buf(BF16, (32, 512))
        hid_off = HID.off
        ZW = [Buf(AR, hid_off + i * 2048, F32, (8, 64)) for i in range(6)]
        SBF = Buf(AR, hid_off + 12288, BF16, (32, 2, 65))
        SBB = Buf(AR, hid_off + 12288 + 8320, BF16, (32, 2, 65))
        assert 12288 + 2 * 8320 <= 32768
        NRING = 4
        RING = [buf(F32, (2048,)) for _ in range(NRING)]
        assert AR.top - main_base >= 141 * 1024

        def TT_(eng, out, i0, i1, op, r, w):
            P.op(eng, lambda e: e.tensor_tensor(out=out, in0=i0, in1=i1, op=op), r, w)

        def TS_(eng, out, i0, s1, s2, op0, op1, r, w):
            if op1 is None:
                P.op(eng, lambda e: e.tensor_scalar(out=out, in0=i0, scalar1=s1, scalar2=None, op0=op0), r, w)
            else:
                P.op(eng, lambda e: e.tensor_scalar(out=out, in0=i0, scalar1=s1, scalar2=s2, op0=op0, op1=op1), r, w)

        def STT_(out, i0, sc, i1, op0, op1, r, w, eng="dve"):
            P.op(eng, lambda e: e.scalar_tensor_tensor(out=out, in0=i0, scalar=sc, in1=i1, op0=op0, op1=op1), r, w)

        def ACT_(out, in_, func, r, w, scale=1.0, bias=None, accum=None):
            def f(e):
                kw = {}
                if bias is not None:
                    kw["bias"] = bias
                if accum is not None:
                    kw["accum_out"] = accum
                return e.activation(out=out, in_=in_, func=func, scale=scale, **kw)
            P.op("act", f, r, w)

        def MM(out, lhsT, rhs, start, stop, r, w):
            P.op("pe", lambda e: e.matmul(out, lhsT=lhsT, rhs=rhs, start=start, stop=stop), r, w)

        def TR(out, in_, ident, r, w):
            P.op("pe", lambda e: e.transpose(out, in_, ident), r, w)

        dq = [0]

        def DMA(out, in_, r, w, q=None):
            if q is None:
                q = ("sp", "act")[dq[0] % 2]
                dq[0] += 1
            P.dma(q, out, in_, r, w)

        rw_wb = [[Res() for _ in range(NWB)] for _ in range(L)]
        rw_s5b = [[Res() for _ in range(10)] for _ in range(L)]
        rw_s5t = [[Res() for _ in range(4)] for _ in range(L)]
        r_xmid = [Res() for _ in range(NT)]
        r_yout = [Res() for _ in range(NT)]
        DMA(CST.ap, cst, [], [CST.ua])
        DMA(PWA.ap, a_pw, [], [PWA.ua])
        pw3 = [CST.ap[:, 0:8], CST.ap[:, 8:16]]
        pwL = [CST.ap[:, 16:24], CST.ap[:, 24:32]]
        qvec = CST.ap[:, 32:96]
        maskF = CST.ap[:, 96:224]
        maskB = CST.ap[:, 224:352]
        identF = CST.ap[:, 352:480]
        P.op("dve", lambda e: e.tensor_copy(out=IDB.ap, in_=identF), [CST.ua], [IDB.ua])
        P.op("pool", lambda e: e.memset(EPSC.ap[:, 0:1], EPS), [], [EPSC.ua])
        P.op("pool", lambda e: e.memset(EPSC.ap[:, 1:2], SINSCALE * math.pi / 2), [], [EPSC.ua])
        P.op("pool", lambda e: e.memset(EPSC.ap[:, 2:3], 0.0), [], [EPSC.ua])
        P.op("pool", lambda e: e.memset(MASK0.ap, 1.0), [], [MASK0.ua])
        P.op("pool", lambda e: e.memset(MASK0.ap[:, :, 0:1], 0.0), [], [MASK0.ua])
        eps_ap = EPSC.ap[:, 0:1]
        hpi_ap = EPSC.ap[:, 1:2]
        zero_ap = EPSC.ap[:, 2:3]

        for l in range(L):
            for b in list(range(4, 10)) + list(range(0, 4)) + list(range(10, NWB)):
                P.dma("pool", wb[l, b], wf[l, b], [], [rw_wb[l][b]])

        def setup_layer(l):
            base = main_base
            KB = 1024
            nA = 18
            A_ = [Buf(AR, base + i * 4096, F32, (16, 64)) for i in range(nA)]
            stg = Buf(AR, base + 72 * KB, BF16, (16, 2, 2, 64))
            dts = Buf(AR, base + 80 * KB, F32, (2, 64))

            def mul(o, a, b, eng="dve"):
                TT_(eng, o.ap, a.ap, b.ap, ALU.mult, [a.ua, b.ua], [o.ua])

            def sincos(x, c_out, s_out, t1, t2, shape_ap=lambda b: b.ap):
                for (dst, off) in ((s_out, 0.0), (c_out, 0.25)):
                    cur = x
                    for rep in range(2):
                        TS_("dve", shape_ap(t1), shape_ap(cur), 1.0 / (2 * math.pi), off, ALU.mult, ALU.add, [cur.ua], [t1.ua])
                        TS_("dve", shape_ap(t1), shape_ap(t1), MAGIC, -MAGIC, ALU.add, ALU.add, [t1.ua], [t1.ua])
                        STT_(shape_ap(t2), shape_ap(t1), -C1, shape_ap(cur), ALU.mult, ALU.add, [t1.ua, cur.ua], [t2.ua])
                        STT_(shape_ap(t2), shape_ap(t1), -C2, shape_ap(t2), ALU.mult, ALU.add, [t1.ua, t2.ua], [t2.ua])
                        cur = t2
                    ACT_(shape_ap(dst), shape_ap(t2), AF.Sin, [t2.ua, EPSC.ua], [dst.ua], scale=SINSCALE,
                         bias=(zero_ap if off == 0.0 else hpi_ap))

            def cmul(o_r, o_i, a_r, a_i, b_r, b_i, t1, t2, neg_im=False, ap=lambda b: b.ap):
                TT_("dve", ap(t1), ap(a_r), ap(b_r), ALU.mult, [a_r.ua, b_r.ua], [t1.ua])
                TT_("pool", ap(t2), ap(a_i), ap(b_i), ALU.mult, [a_i.ua, b_i.ua], [t2.ua])
                TT_("dve", ap(o_r), ap(t1), ap(t2), ALU.subtract, [t1.ua, t2.ua], [o_r.ua])
                TT_("dve", ap(t1), ap(a_r), ap(b_i), ALU.mult, [a_r.ua, b_i.ua], [t1.ua])
                TT_("pool", ap(t2), ap(a_i), ap(b_r), ALU.mult, [a_i.ua, b_r.ua], [t2.ua])
                TT_("dve", ap(o_i), ap(t1), ap(t2), ALU.add, [t1.ua, t2.ua], [o_i.ua])

            for d in range(2):
                DMA(dts.ap[:, d], a_dt[l, d], [], [dts.ua])
            ACT_(dts.ap, dts.ap, AF.Exp, [dts.ua], [dts.ua])
            for c in range(4):
                for d in range(2):
                    LR, LI, BR, BI, AR_, AI_, EA, CA, SA, T1, T2, KR, KI, QR, QI, PR, PI, T3 = A_
                    g0 = c * 16
                    DMA(LR.ap, a_lam[l, d, :, 0, g0 * 64:(g0 + 16) * 64].rearrange("p (g s) -> p g s", g=16), [], [LR.ua])
                    DMA(LI.ap, a_lam[l, d, :, 1, g0 * 64:(g0 + 16) * 64].rearrange("p (g s) -> p g s", g=16), [], [LI.ua])
                    DMA(BR.ap, a_b[l, d, :, 0, g0 * 64:(g0 + 16) * 64].rearrange("p (g s) -> p g s", g=16), [], [BR.ua])
                    DMA(BI.ap, a_b[l, d, :, 1, g0 * 64:(g0 + 16) * 64].rearrange("p (g s) -> p g s", g=16), [], [BI.ua])
                    dtb = dts.ap[:, d, g0:g0 + 16].rearrange("p (g o) -> p g o", o=1).to_broadcast([128, 16, 64])
                    TT_("dve", AR_.ap, LR.ap, dtb, ALU.mult, [LR.ua, dts.ua], [AR_.ua])
                    TT_("pool", AI_.ap, LI.ap, dtb, ALU.mult, [LI.ua, dts.ua], [AI_.ua])
                    ACT_(EA.ap, AR_.ap, AF.Exp, [AR_.ua], [EA.ua])
                    sincos(AI_, CA, SA, T1, T2)
                    mul(CA, CA, EA)
                    mul(SA, SA, EA)
                    mul(T1, LR, LR)
                    mul(T2, LI, LI, "pool")
                    TT_("dve", T1.ap, T1.ap, T2.ap, ALU.add, [T1.ua, T2.ua], [T1.ua])
                    P.op("dve", lambda e, T1=T1: e.reciprocal(out=T1.ap, in_=T1.ap), [T1.ua], [T1.ua])
                    TS_("dve", EA.ap, CA.ap, -1.0, None, ALU.add, None, [CA.ua], [EA.ua])
                    mul(T2, EA, LR)
                    mul(T3, SA, LI, "pool")
                    TT_("dve", KR.ap, T2.ap, T3.ap, ALU.add, [T2.ua, T3.ua], [KR.ua])
                    mul(T2, SA, LR)
                    mul(T3, EA, LI, "pool")
                    TT_("dve", KI.ap, T2.ap, T3.ap, ALU.subtract, [T2.ua, T3.ua], [KI.ua])
                    mul(KR, KR, T1)
                    mul(KI, KI, T1)
                    cmul(QR, QI, KR, KI, BR, BI, T1, T2)
                    pcol = PWA.ap[:, d:d + 1]
                    TS_("dve", T1.ap, AR_.ap, pcol, None, ALU.mult, None, [AR_.ua, PWA.ua], [T1.ua])
                    ACT_(EA.ap, T1.ap, AF.Exp, [T1.ua], [EA.ua])
                    TS_("dve", T3.ap, AI_.ap, pcol, None, ALU.mult, None, [AI_.ua, PWA.ua], [T3.ua])
                    sincos(T3, PR, PI, T1, T2)
                    mul(PR, PR, EA)
                    mul(PI, PI, EA)
                    TT_("dve", T1.ap, PR.ap, QR.ap, ALU.mult, [PR.ua, QR.ua], [T1.ua])
                    TT_("pool", T2.ap, PI.ap, QI.ap, ALU.mult, [PI.ua, QI.ua], [T2.ua])
                    TT_("dve", stg.ap[:, :, d, 0, :], T1.ap, T2.ap, ALU.subtract, [T1.ua, T2.ua], [stg.ua])
                    TT_("dve", T1.ap, PR.ap, QI.ap, ALU.mult, [PR.ua, QI.ua], [T1.ua])
                    TT_("pool", T2.ap, PI.ap, QR.ap, ALU.mult, [PI.ua, QR.ua], [T2.ua])
                    TT_("dve", stg.ap[:, :, d, 1, :], T1.ap, T2.ap, ALU.add, [T1.ua, T2.ua], [stg.ua])
                DMA(s5b[l, 2 + c], stg.ap.rearrange("p a b c d -> p (a b c d)"), [stg.ua], [rw_s5b[l][2 + c]])

            small = [Buf(AR, base + i * 128, F32, (32,)) for i in range(24)]
            (bLR, bLI, bDT, bAR, bAI, bEA, bCA, bSA, bT1, bT2, bT3, bKR, bKI, bR8, bPH) = small[:15]
            bC = [Buf(AR, base + 4 * KB + i * 2048, F32, (32, 16)) for i in range(2)]
            bB = [Buf(AR, base + 8 * KB + i * 2048, F32, (32, 16)) for i in range(2)]
            bQ = [Buf(AR, base + 12 * KB + i * 2048, F32, (32, 16)) for i in range(2)]
            bTq = [Buf(AR, base + 16 * KB + i * 2048, F32, (32, 16)) for i in range(2)]
            P38 = [Buf(AR, base + 20 * KB + i * 1024, F32, (32, 8)) for i in range(4)]
            P38b = [Buf(AR, base + 24 * KB + i * 1024, F32, (32, 8)) for i in range(4)]
            PL8 = [Buf(AR, base + 28 * KB + i * 1024, F32, (32, 8)) for i in range(4)]
            PL8b = [Buf(AR, base + 32 * KB + i * 1024, F32, (32, 8)) for i in range(4)]
            ANG = [Buf(AR, base + 36 * KB + i * 8192, F32, (32, 64)) for i in range(2)]
            TAB = [Buf(AR, base + 52 * KB + i * 8192, F32, (32, 64)) for i in range(2)]
            M3F = [Buf(AR, base + 68 * KB + d * 8192, F32, (8, 2, 128)) for d in range(2)]
            QLF = [Buf(AR, base + 84 * KB + d * 8192, BF16, (8, 2, 128)) for d in range(2)]
            E4 = [Buf(AR, base + 100 * KB + i * 4096, F32, (8, 8, 16)) for i in range(4)]
            stg3 = Buf(AR, base + 116 * KB, BF16, (8, 2, 2, 128))
            stg1 = Buf(AR, base + 124 * KB, BF16, (32, 128))
            M1T = [Buf(AR, base + 132 * KB + i * 2048, F32, (4, 128)) for i in range(2)]
            M1G = [Buf(AR, base + 136 * KB + i * 2048, F32, (4, 128)) for i in range(2)]
            dsk = Buf(AR, base + 140 * KB, F32, (64,))
            DMA(dsk.ap, a_dsk[l], [], [dsk.ua])

            for d in range(2):
                DMA(bLR.ap, b_lam[l, d, :, 0], [], [bLR.ua])
                DMA(bLI.ap, b_lam[l, d, :, 1], [], [bLI.ua])
                DMA(bDT.ap, b_dt[l, d], [], [bDT.ua])
                for i in range(2):
                    DMA(bC[i].ap, b_c[l, d, :, i].rearrange("p (a b) -> p a b", a=32), [], [bC[i].ua])
                    DMA(bB[i].ap, b_b[l, d, :, i].rearrange("p (a b) -> p a b", a=32), [], [bB[i].ua])
                ACT_(bDT.ap, bDT.ap, AF.Exp, [bDT.ua], [bDT.ua])
                mul(bAR, bLR, bDT)
                mul(bAI, bLI, bDT)
                ACT_(bEA.ap, bAR.ap, AF.Exp, [bAR.ua], [bEA.ua])
                sincos(bAI, bCA, bSA, bT1, bT2)
                mul(bCA, bCA, bEA)
                mul(bSA, bSA, bEA)
                mul(bT1, bLR, bLR)
                mul(bT2, bLI, bLI)
                TT_("dve", bT1.ap, bT1.ap, bT2.ap, ALU.add, [bT1.ua, bT2.ua], [bT1.ua])
                P.op("dve", lambda e: e.reciprocal(out=bT1.ap, in_=bT1.ap), [bT1.ua], [bT1.ua])
                TS_("dve", bEA.ap, bCA.ap, -1.0, None, ALU.add, None, [bCA.ua], [bEA.ua])
                mul(bT2, bEA, bLR)
                mul(bT3, bSA, bLI)
                TT_("dve", bKR.ap, bT2.ap, bT3.ap, ALU.add, [bT2.ua, bT3.ua], [bKR.ua])
                mul(bT2, bSA, bLR)
                mul(bT3, bEA, bLI)
                TT_("dve", bKI.ap, bT2.ap, bT3.ap, ALU.subtract, [bT2.ua, bT3.ua], [bKI.ua])
                mul(bKR, bKR, bT1)
                mul(bKI, bKI, bT1)
                b16 = lambda b_: b_.ap.rearrange("p (a o) -> p a o", o=1).to_broadcast([128, 32, 16])
                TT_("dve", bTq[0].ap, bB[0].ap, b16(bKR), ALU.mult, [bB[0].ua, bKR.ua], [bTq[0].ua])
                TT_("dve", bTq[1].ap, bB[1].ap, b16(bKI), ALU.mult, [bB[1].ua, bKI.ua], [bTq[1].ua])
                TT_("dve", bQ[0].ap, bTq[0].ap, bTq[1].ap, ALU.subtract, [bTq[0].ua, bTq[1].ua], [bQ[0].ua])
                TT_("dve", bTq[0].ap, bB[1].ap, b16(bKR), ALU.mult, [bB[1].ua, bKR.ua], [bTq[0].ua])
                TT_("dve", bTq[1].ap, bB[0].ap, b16(bKI), ALU.mult, [bB[0].ua, bKI.ua], [bTq[1].ua])
                TT_("dve", bQ[1].ap, bTq[0].ap, bTq[1].ap, ALU.add, [bTq[0].ua, bTq[1].ua], [bQ[1].ua])
                ACT_(RTAB.ap[:, l, d, :], bAR.ap, AF.Exp, [bAR.ua], [RTAB.ua], scale=8.0)
                TS_("dve", bPH.ap, bAI.ap, 8.0, None, ALU.mult, None, [bAI.ua], [bPH.ua])
                TT_("dve", ANG[0].ap, bPH.ap.rearrange("p (a o) -> p a o", o=1).to_broadcast([128, 32, 64]),
                    qvec.rearrange("p (o q) -> p o q", o=1).to_broadcast([128, 32, 64]), ALU.mult,
                    [bPH.ua, CST.ua], [ANG[0].ua])
                sincos(ANG[0], TAB[0], TAB[1], ANG[1], Buf(AR, E4[0].off, F32, (32, 64)))
                for cs in range(2):
                    for (ki, col) in ((0, 1), (2, 63)):
                        P.op("dve", lambda e, cs=cs, ki=ki, col=col, d=d: e.tensor_copy(out=ROT.ap[:, l, d, ki + cs, :],
                                                                                       in_=TAB[cs].ap[:, :, col]),
                             [TAB[cs].ua], [ROT.ua])
                for c in range(4):
                    for cs in range(2):
                        DMA(s5t[l, c, :, (d * 2 + cs) * 512:(d * 2 + cs + 1) * 512].rearrange("p (a q) -> p a q", a=8),
                            TAB[cs].ap[:, c * 8:(c + 1) * 8, :], [TAB[cs].ua], [rw_s5t[l][c]])
                for (PP, pv) in ((P38 if d == 0 else P38b, pw3[d]), (PL8 if d == 0 else PL8b, pwL[d])):
                    b8 = lambda b_: b_.ap.rearrange("p (a o) -> p a o", o=1).to_broadcast([128, 32, 8])
                    pvb = pv.rearrange("p (o q) -> p o q", o=1).to_broadcast([128, 32, 8])
                    TT_("dve", PP[3].ap, b8(bAR), pvb, ALU.mult, [bAR.ua, CST.ua], [PP[3].ua])
                    ACT_(PP[0].ap, PP[3].ap, AF.Exp, [PP[3].ua], [PP[0].ua])
                    TT_("dve", PP[3].ap, b8(bAI), pvb, ALU.mult, [bAI.ua, CST.ua], [PP[3].ua])
                    t_a = Buf(AR, E4[1].off, F32, (32, 8))
                    t_b = Buf(AR, E4[1].off + 1024, F32, (32, 8))
                    sincos(PP[3], PP[1], PP[2], t_a, t_b)
                    mul(PP[1], PP[1], PP[0])
                    mul(PP[2], PP[2], PP[0])
                PP3 = P38 if d == 0 else P38b
                PPL = PL8 if d == 0 else PL8b
                for c in range(4):
                    ps_ = slice(c * 8, (c + 1) * 8)

                    def bc_i(b_):
                        return b_.ap[:, ps_, :].rearrange("p a (i o) -> p a i o", o=1).to_broadcast([128, 8, 8, 16])

                    def bc_h(b_):
                        return b_.ap[:, ps_, :].rearrange("p a (o h) -> p a o h", o=1).to_broadcast([128, 8, 8, 16])

                    def o4(b_, part):
                        return b_.ap[:, :, part, :].rearrange("p a (i h) -> p a i h", i=8)
                    TT_("dve", E4[0].ap, bc_h(bC[0]), bc_i(PP3[1]), ALU.mult, [bC[0].ua, PP3[1].ua], [E4[0].ua])
                    TT_("pool", E4[1].ap, bc_h(bC[1]), bc_i(PP3[2]), ALU.mult, [bC[1].ua, PP3[2].ua], [E4[1].ua])
                    TT_("dve", o4(M3F[d], 0), E4[0].ap, E4[1].ap, ALU.subtract, [E4[0].ua, E4[1].ua], [M3F[d].ua])
                    TT_("dve", E4[0].ap, bc_h(bC[0]), bc_i(PP3[2]), ALU.mult, [bC[0].ua, PP3[2].ua], [E4[0].ua])
                    TT_("pool", E4[1].ap, bc_h(bC[1]), bc_i(PP3[1]), ALU.mult, [bC[1].ua, PP3[1].ua], [E4[1].ua])
                    STT_(o4(M3F[d], 1), E4[0].ap, -1.0, E4[1].ap, ALU.mult, ALU.subtract, [E4[0].ua, E4[1].ua], [M3F[d].ua])
                    TT_("dve", E4[2].ap, bc_h(bQ[0]), bc_i(PPL[1]), ALU.mult, [bQ[0].ua, PPL[1].ua], [E4[2].ua])
                    TT_("pool", E4[3].ap, bc_h(bQ[1]), bc_i(PPL[2]), ALU.mult, [bQ[1].ua, PPL[2].ua], [E4[3].ua])
                    TT_("dve", o4(QLF[d], 0), E4[2].ap, E4[3].ap, ALU.subtract, [E4[2].ua, E4[3].ua], [QLF[d].ua])
                    TT_("dve", E4[2].ap, bc_h(bQ[0]), bc_i(PPL[2]), ALU.mult, [bQ[0].ua, PPL[2].ua], [E4[2].ua])
                    TT_("pool", E4[3].ap, bc_h(bQ[1]), bc_i(PPL[1]), ALU.mult, [bQ[1].ua, PPL[1].ua], [E4[3].ua])
                    TT_("dve", o4(QLF[d], 1), E4[2].ap, E4[3].ap, ALU.add, [E4[2].ua, E4[3].ua], [QLF[d].ua])
                    P.op("act", lambda e, d=d: e.activation(out=stg3.ap[:, :, d, :, :], in_=M3F[d].ap, func=AF.Copy),
                         [M3F[d].ua], [stg3.ua])
                    DMA(s5b[l, 6 + c].rearrange("p (a b c e) -> p a b c e", a=8, b=2, c=2)[:, :, d, :, :],
                        stg3.ap[:, :, d, :, :], [stg3.ua], [rw_s5b[l][6 + c]])
                    for half in range(2):
                        for pl4 in range(4):
                            pl = half * 4 + pl4
                            for par in range(2):
                                rows = slice(par * 64, par * 64 + 64)
                                bk = 4 + par
                                outp = banks[bk][:, pl4 * 128:(pl4 + 1) * 128]
                                MM(outp, QLF[d].ap[rows, pl, 0, :], stg3.ap[rows, pl, d, 0, :], True, False,
                                   [QLF[d].ua, stg3.ua], [bres[bk]])
                                MM(outp, QLF[d].ap[rows, pl, 1, :], stg3.ap[rows, pl, d, 1, :], False, True,
                                   [QLF[d].ua, stg3.ua], [bres[bk]])
                        for par in range(2):
                            bk = 4 + par
                            idx = (c * 2 + half) * 2 + par
                            msk = (maskF if d == 0 else maskB).rearrange("p (o n) -> p o n", o=1).to_broadcast([128, 4, 128])
                            TT_("dve", M1T[par].ap, banks[bk][:].rearrange("p (a n) -> p a n", a=4), msk, ALU.mult,
                                [bres[bk], CST.ua], [M1T[par].ua])
                            if d == 0:
                                DMA(m1f[l, idx], M1T[par].ap.rearrange("p a n -> p (a n)"), [M1T[par].ua], [r_m1f[l][idx]])
                            else:
                                DMA(M1G[par].ap.rearrange("p a n -> p (a n)"), m1f[l, idx], [r_m1f[l][idx]], [M1G[par].ua])
                                TT_("dve", M1T[par].ap, M1T[par].ap, M1G[par].ap, ALU.add, [M1T[par].ua, M1G[par].ua], [M1T[par].ua])
                                for m4 in range(4):
                                    gg = 2 * (c * 8 + half * 4 + m4) + par
                                    STT_(stg1.ap[:, gg % 32, :], identF, dsk.ap[:, gg:gg + 1], M1T[par].ap[:, m4, :],
                                         ALU.mult, ALU.add, [CST.ua, dsk.ua, M1T[par].ua], [stg1.ua])
                    if d == 1 and c % 2 == 1:
                        DMA(s5b[l, c // 2], stg1.ap.rearrange("p a b -> p (a b)"), [stg1.ua], [rw_s5b[l][c // 2]])

        m1f = dint("m1f", [L, 16, 128, 512], F32)
        r_m1f = [[Res() for _ in range(16)] for _ in range(L)]

        for l in range(L):
            setup_layer(l)
        if dbg:
            for l in range(L):
                for b in range(10):
                    P.dma("pool", dbgo["s5b"][l, b], s5b[l, b], [rw_s5b[l][b]], [Res()])
                for b in range(4):
                    P.dma("sp", dbgo["s5t"][l, b], s5t[l, b], [rw_s5t[l][b]], [Res()])
            X = XB[0]
            P.op("pool", lambda e: e.memset(X.ap, 0.0), [], [X.ua])
            for k in range(NT):
                DMA(y_out[k * TT:(k + 1) * TT, :].rearrange("(s p) d -> p s d", p=128), X.ap, [X.ua], [r_yout[k]])
            print("ops:", P.emit())
            return nc

        PTB = [banks[6 + h][:].bitcast(BF16)[:, 0:512] for h in range(2)]
        PTR = [bres[6], bres[7]]
        pa_list = [0, 1, 2, 3]
        pa_i = [0]

        def pa_next():
            b_ = pa_list[pa_i[0] % len(pa_list)]
            pa_i[0] += 1
            return b_
        ring_i = [0]

        def ring_load(dram_ap, rres, as_f32=False):
            s_ = ring_i[0] % NRING
            ring_i[0] += 1
            rb = RING[s_]
            if as_f32:
                P.dma("sp", rb.ap, dram_ap, [rres], [rb.ua])
            else:
                P.dma("sp", rb.ap.bitcast(BF16), dram_ap, [rres], [rb.ua])
            return rb

        def wview(rb):
            return rb.ap.bitcast(BF16).rearrange("p (k c) -> p k c", k=8)

        id64 = IDB.ap[0:64, 0:64]
        pt_i = [0]

        def pt_next():
            h_ = pt_i[0] % 2
            pt_i[0] += 1
            return h_

        def rstd_from(ss_cols, n):
            ACT_(STAT.ap[:, ss_cols + 4:ss_cols + 4 + n], STAT.ap[:, ss_cols:ss_cols + n], AF.Sqrt, [STAT.ua, EPSC.ua], [STAT.ua],
                 scale=1.0 / 1024.0, bias=eps_ap)
            P.op("dve", lambda e: e.reciprocal(out=STAT.ap[:, ss_cols + 8:ss_cols + 8 + n], in_=STAT.ap[:, ss_cols + 4:ss_cols + 4 + n]),
                 [STAT.ua], [STAT.ua])

        def prenorm(gidx, X):
            for s_ in range(4):
                ACT_(TMPB.ap, X.ap[:, s_, :], AF.Square, [X.us(s_, 4)], [TMP.ua, STAT.ua], accum=STAT.ap[:, s_:s_ + 1])
            rstd_from(0, 4)
            for s_ in range(4):
                ACT_(HN.ap[:, s_, :], X.ap[:, s_, :], AF.Copy, [X.us(s_, 4), STAT.ua], [HN.us(s_, 4)], scale=STAT.ap[:, 8 + s_:9 + s_])
            for kt in range(8):
                h_ = pt_next()
                for s_ in range(4):
                    TR(PTB[h_][:, s_ * 128:(s_ + 1) * 128], HN.ap[:, s_, kt * 128:(kt + 1) * 128], IDB.ap,
                       [HN.us(s_, 4), IDB.ua], [PTR[h_]])
                TS_("dve", H.ap[:, kt, :], PTB[h_], GFM.ap[:, gidx, kt:kt + 1], None, ALU.mult, None,
                    [PTR[h_], GFM.ua], [H.us(kt, 8)])

        def dense_fm(rhs, rb, rres_w, evac):
            wv = wview(rb)
            for m in range(4):
                bk = pa_next()
                for kt in range(8):
                    MM(banks[bk][:], wv[:, kt, m * 128:(m + 1) * 128], rhs.ap[:, kt, :], kt == 0, kt == 7,
                       [rb.ua, rhs.us(kt, 8)], [bres[bk]])
                evac(m, bk)

        def ub_mm(l, cb):
            rb = ring_load(wb[l, 4 + cb], rw_wb[l][4 + cb])
            wv = wview(rb)
            for j in range(8):
                bk = pa_next()
                for kt in range(8):
                    MM(banks[bk][0:64, :], H.ap[:, kt, j:512:8], wv[:, kt, :], kt == 0, kt == 7,
                       [rb.ua, H.us(kt, 8)], [bres[bk]])
                o_ = BIG.ap.rearrange("p a b -> p (a b)").rearrange("p (g j h) -> p g j h", g=64, j=8)[0:64, cb * 32:(cb + 1) * 32, j, :]
                i_ = banks[bk][0:64, :].rearrange("p (g h) -> p g h", h=16)
                if j % 2 == 0:
                    ACT_(o_, i_, AF.Copy, [bres[bk]], [BIG.u(cb * 4096, (cb + 1) * 4096)])
                else:
                    P.op("dve", lambda e, o_=o_, i_=i_: e.tensor_copy(out=o_, in_=i_), [bres[bk]],
                         [BIG.u(cb * 4096, (cb + 1) * 4096)])

        def u_tr():
            for g8 in range(8):
                h_ = pt_next()
                for gl in range(8):
                    g = g8 * 8 + gl
                    TR(PTB[h_][:, gl * 64:(gl + 1) * 64], BIG.ap.rearrange("p a b -> p (a b)")[0:64, g * 128:(g + 1) * 128], id64,
                       [BIG.u(g * 128, (g + 1) * 128), IDB.ua], [PTR[h_]])
                P.op("dve", lambda e, h_=h_, g8=g8: e.tensor_copy(out=RA.ap[:, g8 * 8:(g8 + 1) * 8, :],
                                                                 in_=PTB[h_].rearrange("p (a b) -> p a b", a=8)),
                     [PTR[h_]], [RA.us(g8, 8)])

        def ub_and_transposes(l):
            ub_mm(l, 0)
            ub_mm(l, 1)
            u_tr()

        def rot_small(o_re, o_im, c_, s_, i_re, i_im, rd, wr):
            t = [SML.ap[:, i, :] for i in range(4)]
            TT_("dve", t[0], c_, i_re, ALU.mult, rd, [SML.ua])
            TT_("dve", t[1], s_, i_im, ALU.mult, rd, [SML.ua])
            TT_("dve", t[2], c_, i_im, ALU.mult, rd, [SML.ua])
            TT_("dve", t[3], s_, i_re, ALU.mult, rd, [SML.ua])
            TT_("dve", o_re, t[0], t[1], ALU.subtract, [SML.ua], wr)
            TT_("dve", o_im, t[2], t[3], ALU.add, [SML.ua], wr)

        def rot32(o_re, o_im, c_, s_, i_re, i_im, rd, wr, eng):
            t = [ROTT.ap[:, i, :] for i in range(4)]
            TT_(eng, t[0], c_, i_re, ALU.mult, rd, [ROTT.ua])
            TT_(eng, t[1], s_, i_im, ALU.mult, rd, [ROTT.ua])
            TT_(eng, t[2], c_, i_im, ALU.mult, rd, [ROTT.ua])
            TT_(eng, t[3], s_, i_re, ALU.mult, rd, [ROTT.ua])
            TT_(eng, o_re, t[0], t[1], ALU.subtract, [ROTT.ua], wr)
            TT_(eng, o_im, t[2], t[3], ALU.add, [ROTT.ua], wr)

        def s5_states(l, k, dirs, pass2, fillers=()):
            Zre, Zim, Wre, Wim, T1, T2 = ZW
            T3, T4 = ZT34
            fillers = list(fillers)
            nslots = 4 * len(dirs)
            slot = [0]
            nfill = len(fillers)
            for d in dirs:
                if d == 0:
                    cin_re, cin_im, cin_u = CARF.ap[:, 0, :], CARF.ap[:, 1, :], CARF.ua
                else:
                    cin_re, cin_im, cin_u = SBIN.ap[:, k, 0, :], SBIN.ap[:, k, 1, :], SBIN.ua
                if pass2:
                    sb_ = SBF if d == 0 else SBB
                    col = 0 if d == 0 else 64
                    P.op("pool", lambda e, sb_=sb_, col=col, cin_re=cin_re: e.tensor_copy(out=sb_.ap[:, :, 0, col], in_=cin_re),
                         [cin_u], [sb_.ua])
                    P.op("pool", lambda e, sb_=sb_, col=col, cin_im=cin_im: e.tensor_copy(out=sb_.ap[:, :, 1, col], in_=cin_im),
                         [cin_u], [sb_.ua])
                rot32(WINIT.ap[:, d, 0, :], WINIT.ap[:, d, 1, :], ROT.ap[:, l, d, 0, :], ROT.ap[:, l, d, 1, :], cin_re, cin_im,
                      [cin_u, ROT.ua], [WINIT.ua], "dve")
                for part in range(2):
                    TT_("dve", RW.ap[:, d, part, :], WINIT.ap[:, d, part, :], RTAB.ap[:, l, d, :], ALU.mult, [WINIT.ua, RTAB.ua], [RW.ua])
            for c in range(4):
                rbm = ring_load(s5b[l, 2 + c], rw_s5b[l][2 + c])
                rbt = ring_load(s5t[l, c], rw_s5t[l][c], as_f32=True)
                m2v = rbm.ap.bitcast(BF16).rearrange("p (g d t s) -> p g d t s", g=16, d=2, t=2)
                tv = rbt.ap.rearrange("p (a b q) -> p a b q", a=4, b=8)
                prs = slice(c * 8, (c + 1) * 8)
                for d in dirs:
                    for part in range(2):
                        for pl in range(8):
                            for par in range(2):
                                gl = pl * 2 + par
                                g = c * 16 + gl
                                MM(banks[4 + part][par * 64:(par + 1) * 64, pl * 64:(pl + 1) * 64], m2v[:, gl, d, part, :],
                                   RA.ap[:, g, :], True, True, [rbm.ua, RA.us(g // 8, 8)], [bres[4 + part]])
                    xre = banks[4][:].rearrange("p (a q) -> p a q", a=8)
                    xim = banks[5][:].rearrange("p (a q) -> p a q", a=8)
                    rz = RZ[0]
                    TT_("pool", rz.ap, MASK0.ap, RTAB.ap[:, l, d, prs].rearrange("p (a o) -> p a o", o=1).to_broadcast([128, 8, 64]),
                        ALU.mult, [MASK0.ua, RTAB.ua], [rz.ua])
                    if d == 0:
                        ct = tv[:, 0]
                        st = tv[:, 1]
                        dct, dst_ = ct, st
                        xre_d, xim_d = xre, xim
                    else:
                        ct = tv[:, 2, :, ::-1]
                        st = tv[:, 3, :, ::-1]
                        dct, dst_ = tv[:, 2], tv[:, 3]
                        xre_d, xim_d = xre[:, :, ::-1], xim[:, :, ::-1]
                    TT_("dve", T1.ap, xre_d, dct, ALU.mult, [bres[4], rbt.ua], [T1.ua])
                    TT_("dve", T2.ap, xim_d, dst_, ALU.mult, [bres[5], rbt.ua], [T2.ua])
                    TT_("dve", T3.ap, xim_d, dct, ALU.mult, [bres[5], rbt.ua], [T3.ua])
                    TT_("dve", T4.ap, xre_d, dst_, ALU.mult, [bres[4], rbt.ua], [T4.ua])
                    TT_("dve", Zre.ap, T1.ap, T2.ap, ALU.add, [T1.ua, T2.ua], [Zre.ua])
                    TT_("pool", Zim.ap, T3.ap, T4.ap, ALU.subtract, [T3.ua, T4.ua], [Zim.ua])
                    for part, (Zp, Wp) in enumerate(((Zre, Wre), (Zim, Wim))):
                        TT_("dve", Zp.ap[:, :, 0], Zp.ap[:, :, 0], RW.ap[:, d, part, prs], ALU.add, [Zp.ua, RW.ua], [Zp.ua])
                        P.op("dve", lambda e, Wp=Wp, Zp=Zp, rz=rz: e.tensor_tensor_scan(
                            out=Wp.ap.rearrange("p a q -> p (a q)"), data0=rz.ap.rearrange("p a q -> p (a q)"),
                            data1=Zp.ap.rearrange("p a q -> p (a q)"), initial=0.0, op0=ALU.mult, op1=ALU.add),
                            [Zp.ua, rz.ua], [Wp.ua])
                        P.op("pool", lambda e, Wp=Wp, part=part, d=d, prs=prs: e.tensor_copy(out=WLAST.ap[:, d, part, prs],
                                                                                          in_=Wp.ap[:, :, 63]),
                             [Wp.ua], [WLAST.ua])
                    if pass2:
                        if d == 0:
                            wr_, wi_ = Wre.ap, Wim.ap
                            o_re = SBF.ap[:, prs, 0, 1:65]
                            o_im = SBF.ap[:, prs, 1, 1:65]
                            sbu = SBF.ua
                        else:
                            wr_, wi_ = Wre.ap[:, :, ::-1], Wim.ap[:, :, ::-1]
                            o_re = SBB.ap[:, prs, 0, 0:64]
                            o_im = SBB.ap[:, prs, 1, 0:64]
                            sbu = SBB.ua
                        TT_("dve", T1.ap, wr_, ct, ALU.mult, [Wre.ua, rbt.ua], [T1.ua])
                        TT_("pool", T2.ap, wi_, st, ALU.mult, [Wim.ua, rbt.ua], [T2.ua])
                        TT_("dve", T3.ap, wi_, ct, ALU.mult, [Wim.ua, rbt.ua], [T3.ua])
                        TT_("pool", T4.ap, wr_, st, ALU.mult, [Wre.ua, rbt.ua], [T4.ua])
                        TT_("dve", o_re, T1.ap, T2.ap, ALU.subtract, [T1.ua, T2.ua], [sbu])
                        TT_("dve", o_im, T3.ap, T4.ap, ALU.add, [T3.ua, T4.ua], [sbu])
                    slot[0] += 1
                    tgt = (slot[0] * nfill) // nslots
                    while nfill - len(fillers) < tgt:
                        fillers.pop(0)()
            while fillers:
                fillers.pop(0)()
            for d in dirs:
                if d == 0:
                    rot32(CARF.ap[:, 0, :], CARF.ap[:, 1, :], ROT.ap[:, l, d, 2, :], ROT.ap[:, l, d, 3, :],
                          WLAST.ap[:, d, 0, :], WLAST.ap[:, d, 1, :], [WLAST.ua, ROT.ua], [CARF.ua], "pool")
                elif not pass2 and k >= 1:
                    rot32(SBIN.ap[:, k - 1, 0, :], SBIN.ap[:, k - 1, 1, :], ROT.ap[:, l, d, 2, :], ROT.ap[:, l, d, 3, :],
                          WLAST.ap[:, d, 0, :], WLAST.ap[:, d, 1, :], [WLAST.ua, ROT.ua], [SBIN.ua], "pool")

        def s5_out(l):
            m1rb = None
            m3rb = None

            def back_transposes(kt):
                h_ = pt_next()
                for i in range(8):
                    TR(PTB[h_][:, i * 64:(i + 1) * 64], BIG.ap[0:64, i, kt * 128:(kt + 1) * 128], id64, [BIG.ua, IDB.ua], [PTR[h_]])
                P.op("dve", lambda e, h_=h_, kt=kt: e.tensor_copy(out=RB.ap[:, kt, :].rearrange("p (b i) -> p i b", i=8),
                                                                 in_=PTB[h_].rearrange("p (i b) -> p i b", i=8)),
                     [PTR[h_]], [RB.us(kt, 8)])
            pend = None
            for kt in range(8):
                if kt % 4 == 0:
                    m1rb = ring_load(s5b[l, kt // 4], rw_s5b[l][kt // 4])
                if kt % 2 == 0:
                    m3rb = ring_load(s5b[l, 6 + kt // 2], rw_s5b[l][6 + kt // 2])
                m1v = m1rb.ap.bitcast(BF16).rearrange("p (g n) -> p g n", g=32)
                m3v = m3rb.ap.bitcast(BF16).rearrange("p (a d t n) -> p a d t n", a=8, d=2, t=2)
                ybk = (4, 5) if kt % 2 == 0 else (2, 3)
                for gl in range(8):
                    g = kt * 8 + gl
                    pair, par = g // 2, g % 2
                    rows = slice(par * 64, par * 64 + 64)
                    bk = ybk[par]
                    m_ = gl // 2
                    outp = banks[bk][0:64, m_ * 128:(m_ + 1) * 128]
                    pin = pair % 8
                    MM(outp, RA.ap[:, g, :], m1v[:, g % 32, :], True, False, [RA.us(g // 8, 8), m1rb.ua], [bres[bk]])
                    MM(outp, SBF.ap[rows, pair, 0, 0:64], m3v[rows, pin, 0, 0, :], False, False, [SBF.ua, m3rb.ua], [bres[bk]])
                    MM(outp, SBF.ap[rows, pair, 1, 0:64], m3v[rows, pin, 0, 1, :], False, False, [SBF.ua, m3rb.ua], [bres[bk]])
                    MM(outp, SBB.ap[rows, pair, 0, 1:65], m3v[rows, pin, 1, 0, :], False, False, [SBB.ua, m3rb.ua], [bres[bk]])
                    MM(outp, SBB.ap[rows, pair, 1, 1:65], m3v[rows, pin, 1, 1, :], False, True, [SBB.ua, m3rb.ua], [bres[bk]])
                for par in range(2):
                    bk = ybk[par]
                    in_ = banks[bk][0:64, :].rearrange("p (m i h) -> p m i h", m=4, i=8)
                    o_ = BIG.ap[0:64, :, kt * 128:(kt + 1) * 128].rearrange("p i (m r h) -> p m r i h", m=4, r=2)[:, :, par, :, :]
                    ACT_(o_, in_, AF.Gelu_apprx_tanh, [bres[bk]], [BIG.u(kt * 128, 7 * 1024 + (kt + 1) * 128)])
                if pend is not None:
                    back_transposes(pend)
                pend = kt
            back_transposes(pend)

        def tm_project(l, blocks, lhs, ktn):
            for cb in range(2):
                bks = [pa_list[i] for i in range(4)]
                nkc = len(blocks[cb])
                for kc, bid in enumerate(blocks[cb]):
                    rb = ring_load(wb[l, bid], rw_wb[l][bid])
                    wv = wview(rb)
                    for s_ in range(4):
                        for kt in range(8):
                            MM(banks[bks[s_]][:], lhs.ap[:, kc * 8 + kt, s_ * 128:(s_ + 1) * 128], wv[:, kt, :],
                               kc == 0 and kt == 0, kc == nkc - 1 and kt == 7, [rb.ua, lhs.us(kc * 8 + kt, ktn)], [bres[bks[s_]]])
                for s_ in range(4):
                    ACT_(MIXF.ap[:, s_, cb * 512:(cb + 1) * 512], banks[bks[s_]][:], AF.Copy, [bres[bks[s_]]],
                         [MIXF.u(s_ * 1024 + cb * 512, s_ * 1024 + cb * 512 + 512)])

        def postnorm_add(gidx, X):
            for s_ in range(4):
                ACT_(TMPB.ap, MIXF.ap[:, s_, :], AF.Square, [MIXF.us(s_, 4)], [TMP.ua, STAT.ua], accum=STAT.ap[:, 16 + s_:17 + s_])
            rstd_from(16, 4)
            for s_ in range(4):
                STT_(MIXF.ap[:, s_, :], MIXF.ap[:, s_, :], STAT.ap[:, 24 + s_:25 + s_], GBC.ap[:, gidx, :], ALU.mult, ALU.mult,
                     [MIXF.us(s_, 4), STAT.ua, GBC.ua], [MIXF.us(s_, 4)])
                TT_("dve", X.ap[:, s_, :], X.ap[:, s_, :], MIXF.ap[:, s_, :], ALU.add, [X.us(s_, 4), MIXF.us(s_, 4)], [X.us(s_, 4)])

        def load_layer_consts(l):
            DMA(GFM.ap, gfm[l], [], [GFM.ua], q="sp")
            DMA(GBC.ap, gbc[l].rearrange("i p d -> p i d"), [], [GBC.ua], q="sp")
            DMA(BSB.ap.rearrange("p a b -> p (a b)"), bsb[l], [], [BSB.ua], q="sp")
            DMA(RING[0].ap[:, 0:1024], wst[l], [], [RING[0].ua], q="sp")
            P.op("dve", lambda e: e.tensor_copy(out=WST.ap.rearrange("p a b -> p (a b)"), in_=RING[0].ap[:, 0:1024]), [RING[0].ua], [WST.ua])
            P.op("pool", lambda e: e.memset(CARF.ap, 0.0), [], [CARF.ua])
            P.op("pool", lambda e: e.memset(SBIN.ap[:, NT - 1], 0.0), [], [SBIN.ua])

        import os as _os
        TAPS = _os.environ.get("KTAPS", "") == "1"

        def tap(name, b_, l, k):
            if not (TAPS and l == 0 and k == 0):
                return
            dt_ = F32 if b_.es == 4 else BF16
            o_ = nc.dram_tensor("tap_" + name, [128, b_.n], dt_, kind="ExternalOutput").ap()
            flat = b_.ap
            if len(b_.shape) == 2:
                flat = flat.rearrange("p a b -> p (a b)")
            elif len(b_.shape) == 3:
                flat = flat.rearrange("p a b c -> p (a b c)")
            P.dma("sp", o_, flat, [b_.ua], [Res()])

        xstate = {"i": 0, "pending": None}
        r_uc = [Res() for _ in range(NT)]

        def x_fetch(src_ap, rsrc, k, key):
            if xstate["pending"] == key:
                xstate["i"] += 1
                xstate["pending"] = None
                return XB[xstate["i"] % 2]
            Xb = XB[xstate["i"] % 2]
            P.dma("sp", Xb.ap, src_ap[k * TT:(k + 1) * TT, :].rearrange("(s p) d -> p s d", p=128), [rsrc[k]], [Xb.ua])
            return Xb

        def x_prefetch(src_ap, rsrc, k, key):
            Xn = XB[(xstate["i"] + 1) % 2]
            P.dma("sp", Xn.ap, src_ap[k * TT:(k + 1) * TT, :].rearrange("(s p) d -> p s d", p=128), [rsrc[k]], [Xn.ua])
            xstate["pending"] = key

        def branch_a_items(l, k):
            items = []
            for b in range(4):
                def it(b=b):
                    rb = ring_load(wb[l, 6 + b], rw_wb[l][6 + b])

                    def ev(m, bk):
                        mm = (b % 2) * 4 + m
                        ACT_(SG.ap[:, b // 2, mm, :], banks[bk][:], AF.Sigmoid, [bres[bk]], [SG.us((b // 2) * 8 + mm, 16)])
                    dense_fm(H, rb, None, ev)
                items.append(it)
            for b in (0, 1):
                def it(b=b):
                    rb = ring_load(wb[l, b], rw_wb[l][b])

                    def ev(m, bk):
                        mm = b * 4 + m
                        ACT_(RC.ap[:, mm, :], banks[bk][:], AF.Gelu_apprx_tanh, [bres[bk]], [RC.us(mm, 8)])
                    dense_fm(H, rb, None, ev)
                items.append(it)
            for cb in range(2):
                def it(cb=cb):
                    rb = ring_load(wb[l, 2 + cb], rw_wb[l][2 + cb])
                    wv = wview(rb)
                    for s_ in range(4):
                        bk = pa_next()
                        for kt in range(8):
                            MM(banks[bk][:], H.ap[:, kt, s_ * 128:(s_ + 1) * 128], wv[:, kt, :], kt == 0, kt == 7,
                               [rb.ua, H.us(kt, 8)], [bres[bk]])
                        ACT_(VN.ap[:, s_, cb * 512:(cb + 1) * 512], banks[bk][:], AF.Gelu_apprx_tanh, [bres[bk]],
                             [VN.u(s_ * 1024 + cb * 512, s_ * 1024 + cb * 512 + 512)])
                items.append(it)

            def vnorm():
                for s_ in range(4):
                    ACT_(TMPB.ap, VN.ap[:, s_, :], AF.Square, [VN.us(s_, 4)], [TMP.ua, STAT.ua], accum=STAT.ap[:, 32 + s_:33 + s_])
                rstd_from(32, 4)
                for s_ in range(4):
                    P.op("act", lambda e, s_=s_: e.activation(out=VN.ap[:, s_, :], in_=VN.ap[:, s_, :], func=AF.Copy,
                                                              scale=STAT.ap[:, 40 + s_:41 + s_]),
                         [VN.us(s_, 4), STAT.ua], [VN.us(s_, 4)])
            items.append(vnorm)
            for half in range(2):
                def it(half=half):
                    for g in range(half * 4, half * 4 + 4):
                        bk = pa_next()
                        for s_ in range(4):
                            MM(banks[bk][:, s_ * 128:(s_ + 1) * 128], VN.ap[:, s_, g * 128:(g + 1) * 128], WST.ap[:, g, :], True, True,
                               [VN.us(s_, 4), WST.ua], [bres[bk]])
                        STT_(RD.ap[:, g, :].rearrange("p (s q) -> p s q", s=4), banks[bk][:].rearrange("p (s q) -> p s q", s=4),
                             GFM.ap[:, 2, g:g + 1], BSB.ap[:, g, :].rearrange("p (o q) -> p o q", o=1).to_broadcast([128, 4, 128]),
                             ALU.mult, ALU.add, [bres[bk], GFM.ua, BSB.ua], [RD.us(g, 8)])
                        TT_("pool", RC.ap[:, g, :], RC.ap[:, g, :], RD.ap[:, g, :], ALU.mult, [RC.us(g, 8), RD.us(g, 8)], [RC.us(g, 8)])
                items.append(it)
            for b in (0, 1):
                def it(b=b):
                    rb = ring_load(wb[l, 10 + b], rw_wb[l][10 + b])

                    def ev(m, bk):
                        mm = b * 4 + m
                        TT_("dve", SG.ap[:, 0, mm, :], banks[bk][:], SG.ap[:, 0, mm, :], ALU.mult, [bres[bk], SG.us(mm, 16)],
                            [SG.us(mm, 16)])
                    dense_fm(RC, rb, None, ev)
                items.append(it)
            return items

        def layer(l, src_ap, rsrc, dst_ap, rdst):
            load_layer_consts(l)
            if NT > 1:
                k0 = NT - 1
                P.cur_tag = f"L{l}P1T{k0}:prenorm"
                X = x_fetch(src_ap, rsrc, k0, (l, k0))
                x_prefetch(src_ap, rsrc, k0 - 1, (l, k0 - 1))
                prenorm(0, X)
                P.cur_tag = f"L{l}P1T{k0}:ub"
                ub_mm(l, 0)
                ub_mm(l, 1)
            for k in range(NT - 1, 0, -1):
                P.cur_tag = f"L{l}P1T{k}:ub"
                u_tr()
                P.dma("sp", ucache[k], RA.ap.rearrange("p a b -> p (a b)"), [RA.ua], [r_uc[k]])
                fl = []
                if k - 1 >= 1:
                    Xn = x_fetch(src_ap, rsrc, k - 1, (l, k - 1))
                    x_prefetch(src_ap, rsrc, k - 2, (l, k - 2))
                    fl = [lambda Xn=Xn: prenorm(0, Xn), lambda: ub_mm(l, 0), lambda: ub_mm(l, 1)]
                P.cur_tag = f"L{l}P1T{k}:s5st"
                s5_states(l, k, [1], False, fillers=fl)
            for k in range(NT):
                P.cur_tag = f"L{l}P2T{k}:prenorm"
                X = x_fetch(src_ap, rsrc, k, (l, k))
                if k + 1 < NT:
                    x_prefetch(src_ap, rsrc, k + 1, (l, k + 1))
                if k == 0:
                    prenorm(0, X)
                    P.cur_tag = f"L{l}P2T{k}:ub"
                    ub_and_transposes(l)
                    fl = branch_a_items(l, k)
                else:
                    fl = [lambda X=X: prenorm(0, X)] + branch_a_items(l, k)
                tap("RA", RA, l, k)
                P.cur_tag = f"L{l}P2T{k}:s5st"
                s5_states(l, k, [0, 1], True, fillers=fl)
                tap("H", H, l, k)
                tap("SBF", SBF, l, k)
                tap("SBB", SBB, l, k)
                tap("AIN", RC, l, k)
                P.cur_tag = f"L{l}P2T{k}:s5out"
                s5_out(l)
                if k + 1 < NT:
                    P.dma("sp", RA.ap.rearrange("p a b -> p (a b)"), ucache[k + 1], [r_uc[k + 1]], [RA.ua])
                tap("RB", RB, l, k)
                P.cur_tag = f"L{l}P2T{k}:glu"
                for b in (2, 3):
                    rb = ring_load(wb[l, 12 + b], rw_wb[l][12 + b])

                    def ev(m, bk, b=b):
                        mm = (b - 2) * 4 + m
                        ACT_(RD.ap[:, mm, :], banks[bk][:], AF.Sigmoid, [bres[bk]], [RD.us(mm, 8)])
                        TT_("pool", RD.ap[:, mm, :], RD.ap[:, mm, :], SG.ap[:, 1, mm, :], ALU.mult, [RD.us(mm, 8), SG.us(8 + mm, 16)],
                            [RD.us(mm, 8)])
                    dense_fm(RB, rb, None, ev)
                for b in (0, 1):
                    rb = ring_load(wb[l, 12 + b], rw_wb[l][12 + b])

                    def ev(m, bk, b=b):
                        mm = b * 4 + m
                        TT_("dve", SG.ap[:, 1, mm, :], banks[bk][:], RD.ap[:, mm, :], ALU.mult, [bres[bk], RD.us(mm, 8)],
                            [SG.us(8 + mm, 16)])
                        TT_("pool", SG.ap[:, 0, mm, :], SG.ap[:, 0, mm, :], SG.ap[:, 1, mm, :], ALU.add,
                            [SG.us(mm, 16), SG.us(8 + mm, 16)], [SG.us(mm, 16)])
                    dense_fm(RB, rb, None, ev)
                P.cur_tag = f"L{l}P2T{k}:wo"
                SG0 = Buf(AR, SG.off, BF16, (8, 512))
                tap("MIXIN", SG0, l, k)
                tm_project(l, [[16], [17]], SG0, 8)
                tap("MIXF", MIXF, l, k)
                postnorm_add(0, X)
                tap("X1", X, l, k)
                P.cur_tag = f"L{l}P2T{k}:prenorm2"
                prenorm(1, X)
                P.cur_tag = f"L{l}P2T{k}:ff1"
                for b in range(8):
                    rb = ring_load(wb[l, 18 + b], rw_wb[l][18 + b])

                    def ev(m, bk, b=b):
                        mm = b * 4 + m
                        ACT_(RD.ap[:, mm % 8, :], banks[bk][:], AF.Square, [bres[bk]], [RD.us(mm % 8, 8)])
                        STT_(HID.ap[:, mm, :], banks[bk][:], 0.0, RD.ap[:, mm % 8, :], ALU.is_gt, ALU.mult,
                             [bres[bk], RD.us(mm % 8, 8)], [HID.us(mm, 32)])
                    dense_fm(H, rb, None, ev)
                P.cur_tag = f"L{l}P2T{k}:ff2"
                tap("HID", HID, l, k)
                tm_project(l, [[26, 27, 28, 29], [30, 31, 32, 33]], HID, 32)
                tap("FF", MIXF, l, k)
                postnorm_add(1, X)
                tap("X2", X, l, k)
                P.dma("sp", dst_ap[k * TT:(k + 1) * TT, :].rearrange("(s p) d -> p s d", p=128), X.ap, [X.ua], [rdst[k]])

        r_xin = [Res() for _ in range(NT)]
        layer(0, x_in, r_xin, x_mid, r_xmid)
        layer(1, x_mid, r_xmid, y_out, r_yout)
        print("arena top", AR.top, "of", AR.nbytes)
        print("ops:", P.emit())
    return nc


def host_layouts(p):
    f = np.float32
    out = {}
    wf = np.zeros((L, NWB, 128, 4096), f)

    def blk(w, kc, c0):
        return w[kc * 1024:(kc + 1) * 1024, c0:c0 + 512].reshape(8, 128, 512).transpose(1, 0, 2).reshape(128, 4096)
    for l in range(L):
        for b in range(10):
            wf[l, b] = blk(p["w_in"][l], 0, b * 512)
        for b in range(2):
            wf[l, 10 + b] = blk(p["w_out_a"][l], 0, b * 512)
        for b in range(4):
            wf[l, 12 + b] = blk(p["w_glu"][l], 0, b * 512)
        for b in range(2):
            wf[l, 16 + b] = blk(p["w_o"][l], 0, b * 512)
        for b in range(8):
            wf[l, 18 + b] = blk(p["w_ff1"][l], 0, b * 512)
        for cb in range(2):
            for kc in range(4):
                wf[l, 26 + cb * 4 + kc] = blk(p["w_ff2"][l], kc, cb * 512)
    out["wf"] = wf
    hh = np.arange(128) % 16
    jj = np.arange(128) // 16
    a_lam = np.zeros((L, 2, 128, 2, 4096), f)
    a_b = np.zeros((L, 2, 128, 2, 4096), f)
    a_dt = np.zeros((L, 2, 128, 64), f)
    for l in range(L):
        for d in range(2):
            a_lam[l, d, :, 0] = p["lam_re"][l, d].reshape(1, 4096)
            a_lam[l, d, :, 1] = p["lam_im"][l, d].reshape(1, 4096)
            a_b[l, d, :, 0] = p["b_re"][l, d][:, :, hh].transpose(2, 0, 1).reshape(128, 4096)
            a_b[l, d, :, 1] = p["b_im"][l, d][:, :, hh].transpose(2, 0, 1).reshape(128, 4096)
            a_dt[l, d] = p["log_dt"][l, d][None, :]
    out["a_lam"], out["a_b"], out["a_dt"] = a_lam, a_b, a_dt
    a_pw = np.zeros((128, 2), f)
    a_pw[:, 0] = 7 - jj
    a_pw[:, 1] = jj
    out["a_pw"] = a_pw
    a_dsk = np.zeros((L, 128, 64), f)
    for l in range(L):
        a_dsk[l] = p["d_skip"][l].reshape(64, 16)[:, hh].T
    out["a_dsk"] = a_dsk
    par = np.arange(128) // 64
    ss = np.arange(128) % 64
    b_lam = np.zeros((L, 2, 128, 2, 32), f)
    b_dt = np.zeros((L, 2, 128, 32), f)
    b_c = np.zeros((L, 2, 128, 2, 512), f)
    b_b = np.zeros((L, 2, 128, 2, 512), f)
    for l in range(L):
        for d in range(2):
            for pp in range(2):
                rows = slice(pp * 64, pp * 64 + 64)
                b_lam[l, d, rows, 0] = p["lam_re"][l, d][pp::2].T
                b_lam[l, d, rows, 1] = p["lam_im"][l, d][pp::2].T
                b_dt[l, d, rows] = p["log_dt"][l, d][pp::2][None, :]
                b_c[l, d, rows, 0] = p["c_re"][l, d][pp::2].transpose(2, 0, 1).reshape(64, 512)
                b_c[l, d, rows, 1] = p["c_im"][l, d][pp::2].transpose(2, 0, 1).reshape(64, 512)
                b_b[l, d, rows, 0] = p["b_re"][l, d][pp::2].transpose(1, 0, 2).reshape(64, 512)
                b_b[l, d, rows, 1] = p["b_im"][l, d][pp::2].transpose(1, 0, 2).reshape(64, 512)
    out["b_lam"], out["b_dt"], out["b_c"], out["b_b"] = b_lam, b_dt, b_c, b_b
    cst = np.zeros((128, 480), f)
    i8 = np.arange(8)
    cst[:, 0:8] = i8 + 1
    cst[:, 8:16] = 8 - i8
    cst[:, 16:24] = -(i8 + 1)
    cst[:, 24:32] = i8 - 8
    cst[:, 32:96] = np.arange(64)
    ji = np.arange(128) // 16
    cst[:, 96:224] = (ji[None, :] >= ji[:, None])
    cst[:, 224:352] = (ji[None, :] <= ji[:, None])
    cst[:, 352:480] = np.eye(128)
    out["cst"] = cst
    gfm = np.zeros((L, 128, 3, 8), f)
    gbc = np.zeros((L, 2, 128, 1024), f)
    bsb = np.zeros((L, 128, 1024), f)
    wst = np.zeros((L, 128, 1024), f)
    for l in range(L):
        gfm[l, :, 0] = p["norm_pre_mix"][l].reshape(8, 128).T
        gfm[l, :, 1] = p["norm_pre_ff"][l].reshape(8, 128).T
        gfm[l, :, 2] = p["norm_v"][l].reshape(8, 128).T
        gbc[l, 0] = p["norm_post_mix"][l][None, :]
        gbc[l, 1] = p["norm_post_ff"][l][None, :]
        bsb[l] = p["b_s"][l].reshape(1, 1024)
        wst[l] = p["w_s"][l].transpose(2, 0, 1).reshape(128, 1024)
    out["gfm"], out["gbc"], out["bsb"], out["wst"] = gfm, gbc, bsb, wst
    return out


_NC_CACHE = {}


def kernel(**inputs):
    p = {k: np.asarray(v, dtype=np.float32) for k, v in inputs.items()}
    xs = [p["x_prompt"][i] for i in range(2)] + [p["x_sample"][i] for i in range(4)]
    lay = host_layouts(p)
    if "nc" not in _NC_CACHE:
        _NC_CACHE["nc"] = build()
    nc = _NC_CACHE["nc"]
    in_maps = []
    for c in range(NCORES):
        m = dict(lay)
        m["x"] = np.ascontiguousarray(xs[c % 6])
        in_maps.append(m)
    res = run_bass_kernel_spmd(nc, in_maps, core_ids=list(range(NCORES)))
    ys = [res.results[c]["y"] for c in range(6)]
    y_prompt = np.stack(ys[0:2]).astype(np.float32)
    y_sample = np.stack(ys[2:6]).astype(np.float32)
    return (y_prompt, y_sample)
```

```python
import contextlib
import math
import os
import numpy as np
import concourse.bass as bass
import concourse.mybir as mybir
from concourse.bass_utils import run_bass_kernel_spmd

F32 = mybir.dt.float32
BF16 = mybir.dt.bfloat16
AF = mybir.ActivationFunctionType
ALU = mybir.AluOpType

D = 1024
S = 8192
TT = 512
NT = S // TT
NB = 64
L = 2
NWB = 34
EPS = 1e-6
NCORES = 8
MAGIC = 12582912.0
C1 = 6.28125
C2 = 2.0 * math.pi - 6.28125
SINSCALE = 0.999996


class Res:
    __slots__ = ("last_w", "readers")

    def __init__(self):
        self.last_w = None
        self.readers = []


class Op:
    __slots__ = ("eng", "fn", "deps", "needed", "sig", "dma", "dbg", "tag")

    def __init__(self, eng, fn, dma):
        self.eng = eng
        self.fn = fn
        self.deps = []
        self.needed = False
        self.sig = None
        self.dma = dma


EPOCH = 16000
NDMA = 8


def _flat(x, out):
    for r in x:
        if isinstance(r, Res):
            out.append(r)
        else:
            _flat(r, out)
    return out


class Prog:
    ENG = ("pe", "act", "dve", "pool", "sp")

    def __init__(self, nc):
        self.nc = nc
        self.ops = []

    def op(self, eng, fn, reads=(), writes=(), dma=False):
        o = Op(eng, fn, dma)
        import sys as _s
        fr = _s._getframe(1)
        lines = []
        while fr is not None and len(lines) < 4:
            lines.append(fr.f_lineno)
            fr = fr.f_back
        o.dbg = lines
        o.tag = getattr(self, "cur_tag", "")
        reads = _flat(reads, [])
        writes = _flat(writes, [])
        deps = {}
        for r in reads:
            if r.last_w is not None:
                deps[id(r.last_w)] = r.last_w
        for r in writes:
            if r.last_w is not None:
                deps[id(r.last_w)] = r.last_w
            for q in r.readers:
                deps[id(q)] = q
        for r in reads:
            if not dma:
                r.readers = [q for q in r.readers if q.dma or q.eng != eng]
            r.readers.append(o)
        for r in writes:
            r.last_w = o
            r.readers = []
        deps.pop(id(o), None)
        for d in deps.values():
            if d.eng == "pe" and eng == "pe" and not d.dma and not dma:
                continue
            o.deps.append(d)
            d.needed = True
        self.ops.append(o)
        return o

    def dma(self, q, out, in_, reads=(), writes=()):
        return self.op(q, lambda e: e.dma_start(out=out, in_=in_), reads, writes, dma=True)

    def emit(self):
        nc = self.nc
        import os
        lim = int(os.environ.get("KLIMIT", "0"))
        if lim:
            self.ops = self.ops[:lim]
        cnt = {e: 0 for e in self.ENG}
        dcnt = {e: 0 for e in self.ENG}
        for o in self.ops:
            if o.dma:
                k = dcnt[o.eng]
                dcnt[o.eng] += 1
                o.sig = ("d", o.eng, k % NDMA, (k // NDMA + 1) * 16)
            elif o.needed:
                k = cnt[o.eng]
                cnt[o.eng] += 1
                o.sig = ("c", o.eng, k // EPOCH, k % EPOCH + 1)
        sems = {}
        namemap = {} if os.environ.get("KMAP") else None
        self.namemap = namemap
        with contextlib.ExitStack() as stack:
            for e in self.ENG:
                for ep in range((cnt[e] + EPOCH - 1) // EPOCH):
                    sems[("c", e, ep)] = stack.enter_context(nc.semaphore(f"c_{e}_{ep}"))
                for j in range(min(NDMA, dcnt[e])):
                    sems[("d", e, j)] = stack.enter_context(nc.semaphore(f"d_{e}_{j}"))
            block = stack.enter_context(nc.Block())
            per_eng = {e: [o for o in self.ops if o.eng == e] for e in self.ENG}

            def body(ename):
                def f(eng):
                    waited = {}

                    def wait(key, val):
                        if waited.get(key, 0) >= val:
                            return
                        eng.wait_ge(sems[key], val)
                        waited[key] = val
                    for o in per_eng[ename]:
                        for d in o.deps:
                            s = d.sig
                            wait((s[0], s[1], s[2]), s[3])
                        if o.dma and o.sig[3] > 16:
                            s = o.sig
                            wait((s[0], s[1], s[2]), s[3] - 16)
                        ins = o.fn(eng)
                        if namemap is not None:
                            namemap[ins.ins.name] = o.tag
                        if o.sig is not None:
                            s = o.sig
                            ins.then_inc(sems[(s[0], s[1], s[2])], 16 if o.dma else 1)
                    k = dcnt[ename]
                    for j in range(min(NDMA, k)):
                        n = (k - j + NDMA - 1) // NDMA
                        wait(("d", ename, j), n * 16)
                return f
            for e, meth in (("sp", block.sync), ("act", block.scalar), ("dve", block.vector),
                            ("pool", block.gpsimd), ("pe", block.tensor)):
                if per_eng[e]:
                    meth(body(e))
        if namemap is not None:
            import json as _json
            _json.dump(namemap, open(os.environ["KMAP"], "w"))
        return {e: len(v) for e, v in per_eng.items()}


UNIT = 512


class Arena:
    def __init__(self, nc, stack, nbytes):
        self.t = stack.enter_context(nc.sbuf_tensor("arena", [128, nbytes // 4], F32))
        self.units = [Res() for _ in range((nbytes + UNIT - 1) // UNIT)]
        self.top = 0
        self.nbytes = nbytes

    def alloc(self, nbytes):
        off = self.top
        self.top += (nbytes + UNIT - 1) // UNIT * UNIT
        assert self.top <= self.nbytes, (self.top, self.nbytes)
        return off


class Buf:
    def __init__(self, arena, off, dtype, shape):
        self.arena = arena
        self.off = off
        self.es = 4 if dtype == F32 else 2
        n = 1
        for s in shape:
            n *= s
        self.n = n
        ap = arena.t[:, off // 4:(off + n * self.es + 3) // 4]
        if dtype != F32:
            ap = ap.bitcast(dtype)
        if len(shape) == 2:
            ap = ap.rearrange("p (a b) -> p a b", a=shape[0])
        elif len(shape) == 3:
            ap = ap.rearrange("p (a b c) -> p a b c", a=shape[0], b=shape[1])
        elif len(shape) == 4:
            ap = ap.rearrange("p (a b c d) -> p a b c d", a=shape[0], b=shape[1], c=shape[2])
        self.ap = ap
        self.shape = shape
        self.ua = self.u(0, n)

    def u(self, lo, hi):
        b0 = (self.off + lo * self.es) // UNIT
        b1 = (self.off + hi * self.es + UNIT - 1) // UNIT
        return self.arena.units[b0:b1]

    def us(self, i, n_i):
        w = self.n // n_i
        return self.u(i * w, (i + 1) * w)


def build(dbg=False):
    nc = bass.Bass("TRN2", target_bir_lowering=False)

    def din(name, shape, dt=F32):
        return nc.dram_tensor(name, list(shape), dt, kind="ExternalInput").ap()

    def dint(name, shape, dt):
        return nc.dram_tensor(name, list(shape), dt, kind="Internal").ap()

    x_in = din("x", [S, D])
    wf = din("wf", [L, NWB, 128, 4096])
    a_lam = din("a_lam", [L, 2, 128, 2, 4096])
    a_b = din("a_b", [L, 2, 128, 2, 4096])
    a_dt = din("a_dt", [L, 2, 128, 64])
    a_pw = din("a_pw", [128, 2])
    a_dsk = din("a_dsk", [L, 128, 64])
    b_lam = din("b_lam", [L, 2, 128, 2, 32])
    b_dt = din("b_dt", [L, 2, 128, 32])
    b_c = din("b_c", [L, 2, 128, 2, 512])
    b_b = din("b_b", [L, 2, 128, 2, 512])
    cst = din("cst", [128, 8 + 8 + 8 + 8 + 64 + 128 + 128 + 128])
    gfm = din("gfm", [L, 128, 3, 8])
    gbc = din("gbc", [L, 2, 128, 1024])
    bsb = din("bsb", [L, 128, 1024])
    wst = din("wst", [L, 128, 1024])
    y_out = nc.dram_tensor("y", [S, D], F32, kind="ExternalOutput").ap()
    wb = dint("wb", [L, NWB, 128, 4096], BF16)
    s5b = dint("s5b", [L, 10, 128, 4096], BF16)
    s5t = dint("s5t", [L, 4, 128, 2048], F32)
    x_mid = dint("x_mid", [S, D], F32)
    ucache = dint("ucache", [NT, 128, 4096], BF16)
    dbgo = {}
    if dbg:
        dbgo["s5b"] = nc.dram_tensor("dbg_s5b", [L, 10, 128, 4096], F32, kind="ExternalOutput").ap()
        dbgo["s5t"] = nc.dram_tensor("dbg_s5t", [L, 4, 128, 2048], F32, kind="ExternalOutput").ap()

    P = Prog(nc)
    with contextlib.ExitStack() as es:
        AR = Arena(nc, es, 206 * 1024)
        banks = [es.enter_context(nc.psum_tensor(f"ps{i}", [128, 512], F32)) for i in range(8)]
        bres = [Res() for _ in range(8)]
        bres_h = [[Res(), Res()] for _ in range(8)]

        def buf(dtype, shape):
            n = 1
            for s_ in shape:
                n *= s_
            off = AR.alloc(n * (4 if dtype == F32 else 2))
            return Buf(AR, off, dtype, shape)

        def alias(b, byte_off, dtype, shape):
            return Buf(AR, b.off + byte_off, dtype, shape)

        TMP = buf(BF16, (1024,))
        TMPB = TMP
        IDB = buf(BF16, (128,))
        CST = buf(F32, (480,))
        GFM = buf(F32, (3, 8))
        GBC = buf(F32, (2, 1024))
        BSB = buf(F32, (8, 128))
        WST = buf(BF16, (8, 128))
        STAT = buf(F32, (64,))
        CARF = buf(F32, (2, 32))
        SBIN = buf(F32, (NT, 2, 32))
        RTAB = buf(F32, (L, 2, 32))
        WINIT = buf(F32, (2, 2, 32))
        SML = buf(F32, (8, 8))
        EPSC = buf(F32, (4,))
        PWA = buf(F32, (2,))
        ROT = buf(F32, (L, 2, 4, 32))
        WLAST = buf(F32, (2, 2, 32))
        ZT34 = [buf(F32, (8, 64)) for _ in range(2)]
        ROTT = buf(F32, (4, 32))
        MASK0 = buf(F32, (8, 64))
        RZ = [buf(F32, (8, 64))]
        RW = buf(F32, (2, 2, 32))
        main_base = AR.top
        XB = [buf(F32, (4, 1024)) for _ in range(2)]
        H = buf(BF16, (8, 512))
        SG = buf(BF16, (2, 8, 512))
        BIG = buf(BF16, (8, 1024))
        MIXF = alias(BIG, 0, F32, (4, 1024))
        RA = buf(BF16, (64, 64))
        VN = None
        RB = buf(BF16, (8, 512))
        VN = alias(RB, 0, BF16, (4, 1024))
        RC = buf(BF16, (8, 512))
        RD = buf(BF16, (8, 512))
        HN = alias(RD, 0, BF16, (4, 1024))
        HID = buf(BF16, (32, 512))
        hid_off = HID.off
        ZW = [Buf(AR, hid_off + i * 2048, F32, (8, 64)) for i in range(6)]
        SBF = Buf(AR, hid_off + 12288, BF16, (32, 2, 65))
        SBB = Buf(AR, hid_off + 12288 + 8320, BF16, (32, 2, 65))
        assert 12288 + 2 * 8320 <= 32768
        NRING = 4
        RING = [buf(F32, (2048,)) for _ in range(NRING)]
        assert AR.top - main_base >= 141 * 1024

        def TT_(eng, out, i0, i1, op, r, w):
            P.op(eng, lambda e: e.tensor_tensor(out=out, in0=i0, in1=i1, op=op), r, w)

        def TS_(eng, out, i0, s1, s2, op0, op1, r, w):
            if op1 is None:
                P.op(eng, lambda e: e.tensor_scalar(out=out, in0=i0, scalar1=s1, scalar2=None, op0=op0), r, w)
            else:
                P.op(eng, lambda e: e.tensor_scalar(out=out, in0=i0, scalar1=s1, scalar2=s2, op0=op0, op1=op1), r, w)

        def STT_(out, i0, sc, i1, op0, op1, r, w, eng="dve"):
            P.op(eng, lambda e: e.scalar_tensor_tensor(out=out, in0=i0, scalar=sc, in1=i1, op0=op0, op1=op1), r, w)

        def ACT_(out, in_, func, r, w, scale=1.0, bias=None, accum=None):
            def f(e):
                kw = {}
                if bias is not None:
                    kw["bias"] = bias
                if accum is not None:
                    kw["accum_out"] = accum
                return e.activation(out=out, in_=in_, func=func, scale=scale, **kw)
            P.op("act", f, r, w)

        def MM(out, lhsT, rhs, start, stop, r, w):
            P.op("pe", lambda e: e.matmul(out, lhsT=lhsT, rhs=rhs, start=start, stop=stop), r, w)

        def TR(out, in_, ident, r, w):
            P.op("pe", lambda e: e.transpose(out, in_, ident), r, w)

        dq = [0]

        def DMA(out, in_, r, w, q=None):
            if q is None:
                q = ("sp", "act")[dq[0] % 2]
                dq[0] += 1
            P.dma(q, out, in_, r, w)

        rw_wb = [[Res() for _ in range(NWB)] for _ in range(L)]
        rw_s5b = [[Res() for _ in range(10)] for _ in range(L)]
        rw_s5t = [[Res() for _ in range(4)] for _ in range(L)]
        r_xmid = [Res() for _ in range(NT)]
        r_yout = [Res() for _ in range(NT)]
        DMA(CST.ap, cst, [], [CST.ua])
        DMA(PWA.ap, a_pw, [], [PWA.ua])
        pw3 = [CST.ap[:, 0:8], CST.ap[:, 8:16]]
        pwL = [CST.ap[:, 16:24], CST.ap[:, 24:32]]
        qvec = CST.ap[:, 32:96]
        maskF = CST.ap[:, 96:224]
        maskB = CST.ap[:, 224:352]
        identF = CST.ap[:, 352:480]
        P.op("dve", lambda e: e.tensor_copy(out=IDB.ap, in_=identF), [CST.ua], [IDB.ua])
        P.op("pool", lambda e: e.memset(EPSC.ap[:, 0:1], EPS), [], [EPSC.ua])
        P.op("pool", lambda e: e.memset(EPSC.ap[:, 1:2], SINSCALE * math.pi / 2), [], [EPSC.ua])
        P.op("pool", lambda e: e.memset(EPSC.ap[:, 2:3], 0.0), [], [EPSC.ua])
        P.op("pool", lambda e: e.memset(MASK0.ap, 1.0), [], [MASK0.ua])
        P.op("pool", lambda e: e.memset(MASK0.ap[:, :, 0:1], 0.0), [], [MASK0.ua])
        eps_ap = EPSC.ap[:, 0:1]
        hpi_ap = EPSC.ap[:, 1:2]
        zero_ap = EPSC.ap[:, 2:3]

        for l in range(L):
            for b in list(range(4, 10)) + list(range(0, 4)) + list(range(10, NWB)):
                P.dma("pool", wb[l, b], wf[l, b], [], [rw_wb[l][b]])

        def setup_layer(l):
            base = main_base
            KB = 1024
            nA = 18
            A_ = [Buf(AR, base + i * 4096, F32, (16, 64)) for i in range(nA)]
            stg = Buf(AR, base + 72 * KB, BF16, (16, 2, 2, 64))
            dts = Buf(AR, base + 80 * KB, F32, (2, 64))

            def mul(o, a, b, eng="dve"):
                TT_(eng, o.ap, a.ap, b.ap, ALU.mult, [a.ua, b.ua], [o.ua])

            def sincos(x, c_out, s_out, t1, t2, shape_ap=lambda b: b.ap, reps=1):
                for (dst, off) in ((s_out, 0.0), (c_out, 0.25)):
                    cur = x
                    for rep in range(reps):
                        TS_("dve", shape_ap(t1), shape_ap(cur), 1.0 / (2 * math.pi), off, ALU.mult, ALU.add, [cur.ua], [t1.ua])
                        TS_("dve", shape_ap(t1), shape_ap(t1), MAGIC, -MAGIC, ALU.add, ALU.add, [t1.ua], [t1.ua])
                        STT_(shape_ap(t2), shape_ap(t1), -C1, shape_ap(cur), ALU.mult, ALU.add, [t1.ua, cur.ua], [t2.ua])
                        STT_(shape_ap(t2), shape_ap(t1), -C2, shape_ap(t2), ALU.mult, ALU.add, [t1.ua, t2.ua], [t2.ua])
                        cur = t2
                    ACT_(shape_ap(dst), shape_ap(t2), AF.Sin, [t2.ua, EPSC.ua], [dst.ua], scale=SINSCALE,
                         bias=(zero_ap if off == 0.0 else hpi_ap))

            def cmul(o_r, o_i, a_r, a_i, b_r, b_i, t1, t2, neg_im=False, ap=lambda b: b.ap):
                TT_("dve", ap(t1), ap(a_r), ap(b_r), ALU.mult, [a_r.ua, b_r.ua], [t1.ua])
                TT_("dve", ap(t2), ap(a_i), ap(b_i), ALU.mult, [a_i.ua, b_i.ua], [t2.ua])
                TT_("dve", ap(o_r), ap(t1), ap(t2), ALU.subtract, [t1.ua, t2.ua], [o_r.ua])
                TT_("dve", ap(t1), ap(a_r), ap(b_i), ALU.mult, [a_r.ua, b_i.ua], [t1.ua])
                TT_("dve", ap(t2), ap(a_i), ap(b_r), ALU.mult, [a_i.ua, b_r.ua], [t2.ua])
                TT_("dve", ap(o_i), ap(t1), ap(t2), ALU.add, [t1.ua, t2.ua], [o_i.ua])

            for d in range(2):
                DMA(dts.ap[:, d], a_dt[l, d], [], [dts.ua])
            ACT_(dts.ap, dts.ap, AF.Exp, [dts.ua], [dts.ua])
            for c in range(4):
                for d in range(2):
                    LR, LI, BR, BI, AR_, AI_, EA, CA, SA, T1, T2, KR, KI, QR, QI, PR, PI, T3 = A_
                    g0 = c * 16
                    DMA(LR.ap, a_lam[l, d, :, 0, g0 * 64:(g0 + 16) * 64].rearrange("p (g s) -> p g s", g=16), [], [LR.ua])
                    DMA(LI.ap, a_lam[l, d, :, 1, g0 * 64:(g0 + 16) * 64].rearrange("p (g s) -> p g s", g=16), [], [LI.ua])
                    DMA(BR.ap, a_b[l, d, :, 0, g0 * 64:(g0 + 16) * 64].rearrange("p (g s) -> p g s", g=16), [], [BR.ua])
                    DMA(BI.ap, a_b[l, d, :, 1, g0 * 64:(g0 + 16) * 64].rearrange("p (g s) -> p g s", g=16), [], [BI.ua])
                    dtb = dts.ap[:, d, g0:g0 + 16].rearrange("p (g o) -> p g o", o=1).to_broadcast([128, 16, 64])
                    TT_("dve", AR_.ap, LR.ap, dtb, ALU.mult, [LR.ua, dts.ua], [AR_.ua])
                    TT_("dve", AI_.ap, LI.ap, dtb, ALU.mult, [LI.ua, dts.ua], [AI_.ua])
                    ACT_(EA.ap, AR_.ap, AF.Exp, [AR_.ua], [EA.ua])
                    sincos(AI_, CA, SA, T1, T2)
                    mul(CA, CA, EA)
                    mul(SA, SA, EA)
                    mul(T1, LR, LR)
                    mul(T2, LI, LI, "dve")
                    TT_("dve", T1.ap, T1.ap, T2.ap, ALU.add, [T1.ua, T2.ua], [T1.ua])
                    P.op("dve", lambda e, T1=T1: e.reciprocal(out=T1.ap, in_=T1.ap), [T1.ua], [T1.ua])
                    TS_("dve", EA.ap, CA.ap, -1.0, None, ALU.add, None, [CA.ua], [EA.ua])
                    mul(T2, EA, LR)
                    mul(T3, SA, LI, "dve")
                    TT_("dve", KR.ap, T2.ap, T3.ap, ALU.add, [T2.ua, T3.ua], [KR.ua])
                    mul(T2, SA, LR)
                    mul(T3, EA, LI, "dve")
                    TT_("dve", KI.ap, T2.ap, T3.ap, ALU.subtract, [T2.ua, T3.ua], [KI.ua])
                    mul(KR, KR, T1)
                    mul(KI, KI, T1)
                    cmul(QR, QI, KR, KI, BR, BI, T1, T2)
                    pcol = PWA.ap[:, d:d + 1]
                    TS_("dve", T1.ap, AR_.ap, pcol, None, ALU.mult, None, [AR_.ua, PWA.ua], [T1.ua])
                    ACT_(EA.ap, T1.ap, AF.Exp, [T1.ua], [EA.ua])
                    TS_("dve", T3.ap, AI_.ap, pcol, None, ALU.mult, None, [AI_.ua, PWA.ua], [T3.ua])
                    sincos(T3, PR, PI, T1, T2)
                    mul(PR, PR, EA)
                    mul(PI, PI, EA)
                    TT_("dve", T1.ap, PR.ap, QR.ap, ALU.mult, [PR.ua, QR.ua], [T1.ua])
                    TT_("dve", T2.ap, PI.ap, QI.ap, ALU.mult, [PI.ua, QI.ua], [T2.ua])
                    TT_("dve", stg.ap[:, :, d, 0, :], T1.ap, T2.ap, ALU.subtract, [T1.ua, T2.ua], [stg.ua])
                    TT_("dve", T1.ap, PR.ap, QI.ap, ALU.mult, [PR.ua, QI.ua], [T1.ua])
                    TT_("dve", T2.ap, PI.ap, QR.ap, ALU.mult, [PI.ua, QR.ua], [T2.ua])
                    TT_("dve", stg.ap[:, :, d, 1, :], T1.ap, T2.ap, ALU.add, [T1.ua, T2.ua], [stg.ua])
                DMA(s5b[l, 2 + c], stg.ap.rearrange("p a b c d -> p (a b c d)"), [stg.ua], [rw_s5b[l][2 + c]])

            small = [Buf(AR, base + i * 128, F32, (32,)) for i in range(24)]
            (bLR, bLI, bDT, bAR, bAI, bEA, bCA, bSA, bT1, bT2, bT3, bKR, bKI, bR8, bPH) = small[:15]
            bC = [Buf(AR, base + 4 * KB + i * 2048, F32, (32, 16)) for i in range(2)]
            bB = [Buf(AR, base + 8 * KB + i * 2048, F32, (32, 16)) for i in range(2)]
            bQ = [Buf(AR, base + 12 * KB + i * 2048, F32, (32, 16)) for i in range(2)]
            bTq = [Buf(AR, base + 16 * KB + i * 2048, F32, (32, 16)) for i in range(2)]
            P38 = [Buf(AR, base + 20 * KB + i * 1024, F32, (32, 8)) for i in range(4)]
            P38b = [Buf(AR, base + 24 * KB + i * 1024, F32, (32, 8)) for i in range(4)]
            PL8 = [Buf(AR, base + 28 * KB + i * 1024, F32, (32, 8)) for i in range(4)]
            PL8b = [Buf(AR, base + 32 * KB + i * 1024, F32, (32, 8)) for i in range(4)]
            ANG = [Buf(AR, base + 36 * KB + i * 8192, F32, (32, 64)) for i in range(2)]
            TAB = [Buf(AR, base + 52 * KB + i * 8192, F32, (32, 64)) for i in range(2)]
            M3F = [Buf(AR, base + 68 * KB + d * 8192, F32, (8, 2, 128)) for d in range(2)]
            QLF = [Buf(AR, base + 84 * KB + d * 8192, BF16, (8, 2, 128)) for d in range(2)]
            E4 = [Buf(AR, base + 100 * KB + i * 4096, F32, (8, 8, 16)) for i in range(4)]
            stg3 = Buf(AR, base + 116 * KB, BF16, (8, 2, 2, 128))
            stg1 = Buf(AR, base + 124 * KB, BF16, (32, 128))
            M1T = [Buf(AR, base + 132 * KB + i * 2048, F32, (4, 128)) for i in range(2)]
            M1G = [Buf(AR, base + 136 * KB + i * 2048, F32, (4, 128)) for i in range(2)]
            dsk = Buf(AR, base + 140 * KB, F32, (64,))
            DMA(dsk.ap, a_dsk[l], [], [dsk.ua])

            for d in range(2):
                DMA(bLR.ap, b_lam[l, d, :, 0], [], [bLR.ua])
                DMA(bLI.ap, b_lam[l, d, :, 1], [], [bLI.ua])
                DMA(bDT.ap, b_dt[l, d], [], [bDT.ua])
                for i in range(2):
                    DMA(bC[i].ap, b_c[l, d, :, i].rearrange("p (a b) -> p a b", a=32), [], [bC[i].ua])
                    DMA(bB[i].ap, b_b[l, d, :, i].rearrange("p (a b) -> p a b", a=32), [], [bB[i].ua])
                ACT_(bDT.ap, bDT.ap, AF.Exp, [bDT.ua], [bDT.ua])
                mul(bAR, bLR, bDT)
                mul(bAI, bLI, bDT)
                ACT_(bEA.ap, bAR.ap, AF.Exp, [bAR.ua], [bEA.ua])
                sincos(bAI, bCA, bSA, bT1, bT2)
                mul(bCA, bCA, bEA)
                mul(bSA, bSA, bEA)
                mul(bT1, bLR, bLR)
                mul(bT2, bLI, bLI)
                TT_("dve", bT1.ap, bT1.ap, bT2.ap, ALU.add, [bT1.ua, bT2.ua], [bT1.ua])
                P.op("dve", lambda e: e.reciprocal(out=bT1.ap, in_=bT1.ap), [bT1.ua], [bT1.ua])
                TS_("dve", bEA.ap, bCA.ap, -1.0, None, ALU.add, None, [bCA.ua], [bEA.ua])
                mul(bT2, bEA, bLR)
                mul(bT3, bSA, bLI)
                TT_("dve", bKR.ap, bT2.ap, bT3.ap, ALU.add, [bT2.ua, bT3.ua], [bKR.ua])
                mul(bT2, bSA, bLR)
                mul(bT3, bEA, bLI)
                TT_("dve", bKI.ap, bT2.ap, bT3.ap, ALU.subtract, [bT2.ua, bT3.ua], [bKI.ua])
                mul(bKR, bKR, bT1)
                mul(bKI, bKI, bT1)
                b16 = lambda b_: b_.ap.rearrange("p (a o) -> p a o", o=1).to_broadcast([128, 32, 16])
                TT_("dve", bTq[0].ap, bB[0].ap, b16(bKR), ALU.mult, [bB[0].ua, bKR.ua], [bTq[0].ua])
                TT_("dve", bTq[1].ap, bB[1].ap, b16(bKI), ALU.mult, [bB[1].ua, bKI.ua], [bTq[1].ua])
                TT_("dve", bQ[0].ap, bTq[0].ap, bTq[1].ap, ALU.subtract, [bTq[0].ua, bTq[1].ua], [bQ[0].ua])
                TT_("dve", bTq[0].ap, bB[1].ap, b16(bKR), ALU.mult, [bB[1].ua, bKR.ua], [bTq[0].ua])
                TT_("dve", bTq[1].ap, bB[0].ap, b16(bKI), ALU.mult, [bB[0].ua, bKI.ua], [bTq[1].ua])
                TT_("dve", bQ[1].ap, bTq[0].ap, bTq[1].ap, ALU.add, [bTq[0].ua, bTq[1].ua], [bQ[1].ua])
                ACT_(RTAB.ap[:, l, d, :], bAR.ap, AF.Exp, [bAR.ua], [RTAB.ua], scale=8.0)
                TS_("dve", bPH.ap, bAI.ap, 8.0, None, ALU.mult, None, [bAI.ua], [bPH.ua])
                TT_("dve", ANG[0].ap, bPH.ap.rearrange("p (a o) -> p a o", o=1).to_broadcast([128, 32, 64]),
                    qvec.rearrange("p (o q) -> p o q", o=1).to_broadcast([128, 32, 64]), ALU.mult,
                    [bPH.ua, CST.ua], [ANG[0].ua])
                sincos(ANG[0], TAB[0], TAB[1], ANG[1], Buf(AR, E4[0].off, F32, (32, 64)), reps=2)
                for cs in range(2):
                    for (ki, col) in ((0, 1), (2, 63)):
                        P.op("dve", lambda e, cs=cs, ki=ki, col=col, d=d: e.tensor_copy(out=ROT.ap[:, l, d, ki + cs, :],
                                                                                       in_=TAB[cs].ap[:, :, col]),
                             [TAB[cs].ua], [ROT.ua])
                for c in range(4):
                    for cs in range(2):
                        DMA(s5t[l, c, :, (d * 2 + cs) * 512:(d * 2 + cs + 1) * 512].rearrange("p (a q) -> p a q", a=8),
                            TAB[cs].ap[:, c * 8:(c + 1) * 8, :], [TAB[cs].ua], [rw_s5t[l][c]])
                for (PP, pv) in ((P38 if d == 0 else P38b, pw3[d]), (PL8 if d == 0 else PL8b, pwL[d])):
                    b8 = lambda b_: b_.ap.rearrange("p (a o) -> p a o", o=1).to_broadcast([128, 32, 8])
                    pvb = pv.rearrange("p (o q) -> p o q", o=1).to_broadcast([128, 32, 8])
                    TT_("dve", PP[3].ap, b8(bAR), pvb, ALU.mult, [bAR.ua, CST.ua], [PP[3].ua])
                    ACT_(PP[0].ap, PP[3].ap, AF.Exp, [PP[3].ua], [PP[0].ua])
                    TT_("dve", PP[3].ap, b8(bAI), pvb, ALU.mult, [bAI.ua, CST.ua], [PP[3].ua])
                    t_a = Buf(AR, E4[1].off, F32, (32, 8))
                    t_b = Buf(AR, E4[1].off + 1024, F32, (32, 8))
                    sincos(PP[3], PP[1], PP[2], t_a, t_b)
                    mul(PP[1], PP[1], PP[0])
                    mul(PP[2], PP[2], PP[0])
                PP3 = P38 if d == 0 else P38b
                PPL = PL8 if d == 0 else PL8b
                for c in range(4):
                    ps_ = slice(c * 8, (c + 1) * 8)

                    def bc_i(b_):
                        return b_.ap[:, ps_, :].rearrange("p a (i o) -> p a i o", o=1).to_broadcast([128, 8, 8, 16])

                    def bc_h(b_):
                        return b_.ap[:, ps_, :].rearrange("p a (o h) -> p a o h", o=1).to_broadcast([128, 8, 8, 16])

                    def o4(b_, part):
                        return b_.ap[:, :, part, :].rearrange("p a (i h) -> p a i h", i=8)
                    TT_("dve", E4[0].ap, bc_h(bC[0]), bc_i(PP3[1]), ALU.mult, [bC[0].ua, PP3[1].ua], [E4[0].ua])
                    TT_("dve", E4[1].ap, bc_h(bC[1]), bc_i(PP3[2]), ALU.mult, [bC[1].ua, PP3[2].ua], [E4[1].ua])
                    TT_("dve", o4(M3F[d], 0), E4[0].ap, E4[1].ap, ALU.subtract, [E4[0].ua, E4[1].ua], [M3F[d].ua])
                    TT_("dve", E4[0].ap, bc_h(bC[0]), bc_i(PP3[2]), ALU.mult, [bC[0].ua, PP3[2].ua], [E4[0].ua])
                    TT_("dve", E4[1].ap, bc_h(bC[1]), bc_i(PP3[1]), ALU.mult, [bC[1].ua, PP3[1].ua], [E4[1].ua])
                    STT_(o4(M3F[d], 1), E4[0].ap, -1.0, E4[1].ap, ALU.mult, ALU.subtract, [E4[0].ua, E4[1].ua], [M3F[d].ua])
                    TT_("dve", E4[2].ap, bc_h(bQ[0]), bc_i(PPL[1]), ALU.mult, [bQ[0].ua, PPL[1].ua], [E4[2].ua])
                    TT_("dve", E4[3].ap, bc_h(bQ[1]), bc_i(PPL[2]), ALU.mult, [bQ[1].ua, PPL[2].ua], [E4[3].ua])
                    TT_("dve", o4(QLF[d], 0), E4[2].ap, E4[3].ap, ALU.subtract, [E4[2].ua, E4[3].ua], [QLF[d].ua])
                    TT_("dve", E4[2].ap, bc_h(bQ[0]), bc_i(PPL[2]), ALU.mult, [bQ[0].ua, PPL[2].ua], [E4[2].ua])
                    TT_("dve", E4[3].ap, bc_h(bQ[1]), bc_i(PPL[1]), ALU.mult, [bQ[1].ua, PPL[1].ua], [E4[3].ua])
                    TT_("dve", o4(QLF[d], 1), E4[2].ap, E4[3].ap, ALU.add, [E4[2].ua, E4[3].ua], [QLF[d].ua])
                    P.op("act", lambda e, d=d: e.activation(out=stg3.ap[:, :, d, :, :], in_=M3F[d].ap, func=AF.Copy),
                         [M3F[d].ua], [stg3.ua])
                    DMA(s5b[l, 6 + c].rearrange("p (a b c e) -> p a b c e", a=8, b=2, c=2)[:, :, d, :, :],
                        stg3.ap[:, :, d, :, :], [stg3.ua], [rw_s5b[l][6 + c]])
                    for half in range(2):
                        for pl4 in range(4):
                            pl = half * 4 + pl4
                            for par in range(2):
                                rows = slice(par * 64, par * 64 + 64)
                                bk = 4 + par
                                outp = banks[bk][:, pl4 * 128:(pl4 + 1) * 128]
                                MM(outp, QLF[d].ap[rows, pl, 0, :], stg3.ap[rows, pl, d, 0, :], True, False,
                                   [QLF[d].ua, stg3.ua], [bres[bk]])
                                MM(outp, QLF[d].ap[rows, pl, 1, :], stg3.ap[rows, pl, d, 1, :], False, True,
                                   [QLF[d].ua, stg3.ua], [bres[bk]])
                        for par in range(2):
                            bk = 4 + par
                            idx = (c * 2 + half) * 2 + par
                            msk = (maskF if d == 0 else maskB).rearrange("p (o n) -> p o n", o=1).to_broadcast([128, 4, 128])
                            TT_("dve", M1T[par].ap, banks[bk][:].rearrange("p (a n) -> p a n", a=4), msk, ALU.mult,
                                [bres[bk], CST.ua], [M1T[par].ua])
                            if d == 0:
                                DMA(m1f[l, idx], M1T[par].ap.rearrange("p a n -> p (a n)"), [M1T[par].ua], [r_m1f[l][idx]])
                            else:
                                DMA(M1G[par].ap.rearrange("p a n -> p (a n)"), m1f[l, idx], [r_m1f[l][idx]], [M1G[par].ua])
                                TT_("dve", M1T[par].ap, M1T[par].ap, M1G[par].ap, ALU.add, [M1T[par].ua, M1G[par].ua], [M1T[par].ua])
                                for m4 in range(4):
                                    gg = 2 * (c * 8 + half * 4 + m4) + par
                                    STT_(stg1.ap[:, gg % 32, :], identF, dsk.ap[:, gg:gg + 1], M1T[par].ap[:, m4, :],
                                         ALU.mult, ALU.add, [CST.ua, dsk.ua, M1T[par].ua], [stg1.ua])
                    if d == 1 and c % 2 == 1:
                        DMA(s5b[l, c // 2], stg1.ap.rearrange("p a b -> p (a b)"), [stg1.ua], [rw_s5b[l][c // 2]])

        m1f = dint("m1f", [L, 16, 128, 512], F32)
        r_m1f = [[Res() for _ in range(16)] for _ in range(L)]

        for l in range(L):
            setup_layer(l)
        if dbg:
            for l in range(L):
                for b in range(10):
                    P.dma("pool", dbgo["s5b"][l, b], s5b[l, b], [rw_s5b[l][b]], [Res()])
                for b in range(4):
                    P.dma("sp", dbgo["s5t"][l, b], s5t[l, b], [rw_s5t[l][b]], [Res()])
            X = XB[0]
            P.op("pool", lambda e: e.memset(X.ap, 0.0), [], [X.ua])
            for k in range(NT):
                DMA(y_out[k * TT:(k + 1) * TT, :].rearrange("(s p) d -> p s d", p=128), X.ap, [X.ua], [r_yout[k]])
            print("ops:", P.emit())
            return nc

        PTB = [banks[6 + h][:].bitcast(BF16)[:, 0:512] for h in range(2)]
        PTR = [bres[6], bres[7]]
        pa_list = [0, 1, 2, 3]
        pa_i = [0]

        def pa_next():
            b_ = pa_list[pa_i[0] % len(pa_list)]
            pa_i[0] += 1
            return b_
        ring_i = [0]

        def ring_load(dram_ap, rres, as_f32=False):
            s_ = ring_i[0] % NRING
            ring_i[0] += 1
            rb = RING[s_]
            if as_f32:
                P.dma("sp", rb.ap, dram_ap, [rres], [rb.ua])
            else:
                P.dma("sp", rb.ap.bitcast(BF16), dram_ap, [rres], [rb.ua])
            return rb

        def wview(rb):
            return rb.ap.bitcast(BF16).rearrange("p (k c) -> p k c", k=8)

        id64 = IDB.ap[0:64, 0:64]
        pt_i = [0]

        def pt_next():
            h_ = pt_i[0] % 2
            pt_i[0] += 1
            return h_

        def rstd_from(ss_cols, n):
            ACT_(STAT.ap[:, ss_cols + 4:ss_cols + 4 + n], STAT.ap[:, ss_cols:ss_cols + n], AF.Sqrt, [STAT.ua, EPSC.ua], [STAT.ua],
                 scale=1.0 / 1024.0, bias=eps_ap)
            P.op("dve", lambda e: e.reciprocal(out=STAT.ap[:, ss_cols + 8:ss_cols + 8 + n], in_=STAT.ap[:, ss_cols + 4:ss_cols + 4 + n]),
                 [STAT.ua], [STAT.ua])

        def prenorm(gidx, X):
            for s_ in range(4):
                ACT_(TMPB.ap, X.ap[:, s_, :], AF.Square, [X.us(s_, 4)], [TMP.ua, STAT.ua], accum=STAT.ap[:, s_:s_ + 1])
            rstd_from(0, 4)
            for s_ in range(4):
                ACT_(HN.ap[:, s_, :], X.ap[:, s_, :], AF.Copy, [X.us(s_, 4), STAT.ua], [HN.us(s_, 4)], scale=STAT.ap[:, 8 + s_:9 + s_])
            for kt in range(8):
                h_ = pt_next()
                for s_ in range(4):
                    TR(PTB[h_][:, s_ * 128:(s_ + 1) * 128], HN.ap[:, s_, kt * 128:(kt + 1) * 128], IDB.ap,
                       [HN.us(s_, 4), IDB.ua], [PTR[h_]])
                ACT_(H.ap[:, kt, :], PTB[h_], AF.Copy, [PTR[h_], GFM.ua], [H.us(kt, 8)], scale=GFM.ap[:, gidx, kt:kt + 1])

        def dense_fm(rhs, rb, rres_w, evac):
            wv = wview(rb)
            for m in range(4):
                bk = pa_next()
                for kt in range(8):
                    MM(banks[bk][:], wv[:, kt, m * 128:(m + 1) * 128], rhs.ap[:, kt, :], kt == 0, kt == 7,
                       [rb.ua, rhs.us(kt, 8)], [bres[bk]])
                evac(m, bk)

        def ub_mm(l, cb):
            rb = ring_load(wb[l, 4 + cb], rw_wb[l][4 + cb])
            wv = wview(rb)
            for j in range(8):
                bk = pa_next()
                for kt in range(8):
                    MM(banks[bk][0:64, :], H.ap[:, kt, j:512:8], wv[:, kt, :], kt == 0, kt == 7,
                       [rb.ua, H.us(kt, 8)], [bres[bk]])
                o_ = BIG.ap.rearrange("p a b -> p (a b)").rearrange("p (g j h) -> p g j h", g=64, j=8)[0:64, cb * 32:(cb + 1) * 32, j, :]
                i_ = banks[bk][0:64, :].rearrange("p (g h) -> p g h", h=16)
                if j % 2 == 0:
                    ACT_(o_, i_, AF.Copy, [bres[bk]], [BIG.u(cb * 4096, (cb + 1) * 4096)])
                else:
                    P.op("dve", lambda e, o_=o_, i_=i_: e.tensor_copy(out=o_, in_=i_), [bres[bk]],
                         [BIG.u(cb * 4096, (cb + 1) * 4096)])

        def u_tr():
            for g8 in range(8):
                h_ = pt_next()
                for gl in range(8):
                    g = g8 * 8 + gl
                    TR(PTB[h_][:, gl * 64:(gl + 1) * 64], BIG.ap.rearrange("p a b -> p (a b)")[0:64, g * 128:(g + 1) * 128], id64,
                       [BIG.u(g * 128, (g + 1) * 128), IDB.ua], [PTR[h_]])
                P.op("dve", lambda e, h_=h_, g8=g8: e.tensor_copy(out=RA.ap[:, g8 * 8:(g8 + 1) * 8, :],
                                                                 in_=PTB[h_].rearrange("p (a b) -> p a b", a=8)),
                     [PTR[h_]], [RA.us(g8, 8)])

        def ub_and_transposes(l):
            ub_mm(l, 0)
            ub_mm(l, 1)
            u_tr()

        def rot_small(o_re, o_im, c_, s_, i_re, i_im, rd, wr):
            t = [SML.ap[:, i, :] for i in range(4)]
            TT_("dve", t[0], c_, i_re, ALU.mult, rd, [SML.ua])
            TT_("dve", t[1], s_, i_im, ALU.mult, rd, [SML.ua])
            TT_("dve", t[2], c_, i_im, ALU.mult, rd, [SML.ua])
            TT_("dve", t[3], s_, i_re, ALU.mult, rd, [SML.ua])
            TT_("dve", o_re, t[0], t[1], ALU.subtract, [SML.ua], wr)
            TT_("dve", o_im, t[2], t[3], ALU.add, [SML.ua], wr)

        def rot32(o_re, o_im, c_, s_, i_re, i_im, rd, wr, eng):
            t = [ROTT.ap[:, i, :] for i in range(4)]
            TT_(eng, t[0], c_, i_re, ALU.mult, rd, [ROTT.ua])
            TT_(eng, t[1], s_, i_im, ALU.mult, rd, [ROTT.ua])
            TT_(eng, t[2], c_, i_im, ALU.mult, rd, [ROTT.ua])
            TT_(eng, t[3], s_, i_re, ALU.mult, rd, [ROTT.ua])
            TT_(eng, o_re, t[0], t[1], ALU.subtract, [ROTT.ua], wr)
            TT_(eng, o_im, t[2], t[3], ALU.add, [ROTT.ua], wr)

        def s5_states(l, k, dirs, pass2, fillers=()):
            Zre, Zim, Wre, Wim, T1, T2 = ZW
            T3, T4 = ZT34
            fillers = list(fillers)
            nslots = 4 * len(dirs)
            slot = [0]
            nfill = len(fillers)
            for d in dirs:
                if d == 0:
                    cin_re, cin_im, cin_u = CARF.ap[:, 0, :], CARF.ap[:, 1, :], CARF.ua
                else:
                    cin_re, cin_im, cin_u = SBIN.ap[:, k, 0, :], SBIN.ap[:, k, 1, :], SBIN.ua
                if pass2:
                    sb_ = SBF if d == 0 else SBB
                    col = 0 if d == 0 else 64
                    P.op("pool", lambda e, sb_=sb_, col=col, cin_re=cin_re: e.tensor_copy(out=sb_.ap[:, :, 0, col], in_=cin_re),
                         [cin_u], [sb_.ua])
                    P.op("pool", lambda e, sb_=sb_, col=col, cin_im=cin_im: e.tensor_copy(out=sb_.ap[:, :, 1, col], in_=cin_im),
                         [cin_u], [sb_.ua])
                rot32(WINIT.ap[:, d, 0, :], WINIT.ap[:, d, 1, :], ROT.ap[:, l, d, 0, :], ROT.ap[:, l, d, 1, :], cin_re, cin_im,
                      [cin_u, ROT.ua], [WINIT.ua], "dve")
                for part in range(2):
                    TT_("dve", RW.ap[:, d, part, :], WINIT.ap[:, d, part, :], RTAB.ap[:, l, d, :], ALU.mult, [WINIT.ua, RTAB.ua], [RW.ua])
            for c in range(4):
                rbm = ring_load(s5b[l, 2 + c], rw_s5b[l][2 + c])
                rbt = ring_load(s5t[l, c], rw_s5t[l][c], as_f32=True)
                m2v = rbm.ap.bitcast(BF16).rearrange("p (g d t s) -> p g d t s", g=16, d=2, t=2)
                tv = rbt.ap.rearrange("p (a b q) -> p a b q", a=4, b=8)
                prs = slice(c * 8, (c + 1) * 8)
                for d in dirs:
                    for part in range(2):
                        for pl in range(8):
                            for par in range(2):
                                gl = pl * 2 + par
                                g = c * 16 + gl
                                MM(banks[4 + part][par * 64:(par + 1) * 64, pl * 64:(pl + 1) * 64], m2v[:, gl, d, part, :],
                                   RA.ap[:, g, :], True, True, [rbm.ua, RA.us(g // 8, 8)], [bres[4 + part]])
                    xre = banks[4][:].rearrange("p (a q) -> p a q", a=8)
                    xim = banks[5][:].rearrange("p (a q) -> p a q", a=8)
                    rz = RZ[0]
                    TT_("pool", rz.ap, MASK0.ap, RTAB.ap[:, l, d, prs].rearrange("p (a o) -> p a o", o=1).to_broadcast([128, 8, 64]),
                        ALU.mult, [MASK0.ua, RTAB.ua], [rz.ua])
                    if d == 0:
                        ct = tv[:, 0]
                        st = tv[:, 1]
                        dct, dst_ = ct, st
                        xre_d, xim_d = xre, xim
                    else:
                        ct = tv[:, 2, :, ::-1]
                        st = tv[:, 3, :, ::-1]
                        dct, dst_ = tv[:, 2], tv[:, 3]
                        xre_d, xim_d = xre[:, :, ::-1], xim[:, :, ::-1]
                    TT_("dve", T1.ap, xre_d, dct, ALU.mult, [bres[4], rbt.ua], [T1.ua])
                    TT_("dve", T2.ap, xim_d, dst_, ALU.mult, [bres[5], rbt.ua], [T2.ua])
                    TT_("dve", T3.ap, xim_d, dct, ALU.mult, [bres[5], rbt.ua], [T3.ua])
                    TT_("dve", T4.ap, xre_d, dst_, ALU.mult, [bres[4], rbt.ua], [T4.ua])
                    TT_("dve", Zre.ap, T1.ap, T2.ap, ALU.add, [T1.ua, T2.ua], [Zre.ua])
                    TT_("pool", Zim.ap, T3.ap, T4.ap, ALU.subtract, [T3.ua, T4.ua], [Zim.ua])
                    for part, (Zp, Wp) in enumerate(((Zre, Wre), (Zim, Wim))):
                        TT_("dve", Zp.ap[:, :, 0], Zp.ap[:, :, 0], RW.ap[:, d, part, prs], ALU.add, [Zp.ua, RW.ua], [Zp.ua])
                        P.op("dve", lambda e, Wp=Wp, Zp=Zp, rz=rz: e.tensor_tensor_scan(
                            out=Wp.ap.rearrange("p a q -> p (a q)"), data0=rz.ap.rearrange("p a q -> p (a q)"),
                            data1=Zp.ap.rearrange("p a q -> p (a q)"), initial=0.0, op0=ALU.mult, op1=ALU.add),
                            [Zp.ua, rz.ua], [Wp.ua])
                        P.op("pool", lambda e, Wp=Wp, part=part, d=d, prs=prs: e.tensor_copy(out=WLAST.ap[:, d, part, prs],
                                                                                          in_=Wp.ap[:, :, 63]),
                             [Wp.ua], [WLAST.ua])
                    if pass2:
                        if d == 0:
                            wr_, wi_ = Wre.ap, Wim.ap
                            o_re = SBF.ap[:, prs, 0, 1:65]
                            o_im = SBF.ap[:, prs, 1, 1:65]
                            sbu = SBF.ua
                        else:
                            wr_, wi_ = Wre.ap[:, :, ::-1], Wim.ap[:, :, ::-1]
                            o_re = SBB.ap[:, prs, 0, 0:64]
                            o_im = SBB.ap[:, prs, 1, 0:64]
                            sbu = SBB.ua
                        TT_("dve", T1.ap, wr_, ct, ALU.mult, [Wre.ua, rbt.ua], [T1.ua])
                        TT_("pool", T2.ap, wi_, st, ALU.mult, [Wim.ua, rbt.ua], [T2.ua])
                        TT_("dve", T3.ap, wi_, ct, ALU.mult, [Wim.ua, rbt.ua], [T3.ua])
                        TT_("pool", T4.ap, wr_, st, ALU.mult, [Wre.ua, rbt.ua], [T4.ua])
                        TT_("dve", o_re, T1.ap, T2.ap, ALU.subtract, [T1.ua, T2.ua], [sbu])
                        TT_("dve", o_im, T3.ap, T4.ap, ALU.add, [T3.ua, T4.ua], [sbu])
                    slot[0] += 1
                    tgt = (slot[0] * nfill) // nslots
                    while nfill - len(fillers) < tgt:
                        fillers.pop(0)()
            while fillers:
                fillers.pop(0)()
            for d in dirs:
                if d == 0:
                    rot32(CARF.ap[:, 0, :], CARF.ap[:, 1, :], ROT.ap[:, l, d, 2, :], ROT.ap[:, l, d, 3, :],
                          WLAST.ap[:, d, 0, :], WLAST.ap[:, d, 1, :], [WLAST.ua, ROT.ua], [CARF.ua], "pool")
                elif not pass2 and k >= 1:
                    rot32(SBIN.ap[:, k - 1, 0, :], SBIN.ap[:, k - 1, 1, :], ROT.ap[:, l, d, 2, :], ROT.ap[:, l, d, 3, :],
                          WLAST.ap[:, d, 0, :], WLAST.ap[:, d, 1, :], [WLAST.ua, ROT.ua], [SBIN.ua], "pool")

        def s5_out(l):
            m1rb = None
            m3rb = None

            def back_transposes(kt):
                h_ = pt_next()
                for i in range(8):
                    TR(PTB[h_][:, i * 64:(i + 1) * 64], BIG.ap[0:64, i, kt * 128:(kt + 1) * 128], id64, [BIG.ua, IDB.ua], [PTR[h_]])
                ACT_(RB.ap[:, kt, :].rearrange("p (b i) -> p i b", i=8), PTB[h_].rearrange("p (i b) -> p i b", i=8), AF.Copy,
                     [PTR[h_]], [RB.us(kt, 8)])
            pend = None
            for kt in range(8):
                if kt % 4 == 0:
                    m1rb = ring_load(s5b[l, kt // 4], rw_s5b[l][kt // 4])
                if kt % 2 == 0:
                    m3rb = ring_load(s5b[l, 6 + kt // 2], rw_s5b[l][6 + kt // 2])
                m1v = m1rb.ap.bitcast(BF16).rearrange("p (g n) -> p g n", g=32)
                m3v = m3rb.ap.bitcast(BF16).rearrange("p (a d t n) -> p a d t n", a=8, d=2, t=2)
                ybk = (4, 5) if kt % 2 == 0 else (2, 3)
                for gl in range(8):
                    g = kt * 8 + gl
                    pair, par = g // 2, g % 2
                    rows = slice(par * 64, par * 64 + 64)
                    bk = ybk[par]
                    m_ = gl // 2
                    outp = banks[bk][0:64, m_ * 128:(m_ + 1) * 128]
                    pin = pair % 8
                    MM(outp, RA.ap[:, g, :], m1v[:, g % 32, :], True, False, [RA.us(g // 8, 8), m1rb.ua], [bres[bk]])
                    MM(outp, SBF.ap[rows, pair, 0, 0:64], m3v[rows, pin, 0, 0, :], False, False, [SBF.ua, m3rb.ua], [bres[bk]])
                    MM(outp, SBF.ap[rows, pair, 1, 0:64], m3v[rows, pin, 0, 1, :], False, False, [SBF.ua, m3rb.ua], [bres[bk]])
                    MM(outp, SBB.ap[rows, pair, 0, 1:65], m3v[rows, pin, 1, 0, :], False, False, [SBB.ua, m3rb.ua], [bres[bk]])
                    MM(outp, SBB.ap[rows, pair, 1, 1:65], m3v[rows, pin, 1, 1, :], False, True, [SBB.ua, m3rb.ua], [bres[bk]])
                for par in range(2):
                    bk = ybk[par]
                    in_ = banks[bk][0:64, :].rearrange("p (m i h) -> p m i h", m=4, i=8)
                    o_ = BIG.ap[0:64, :, kt * 128:(kt + 1) * 128].rearrange("p i (m r h) -> p m r i h", m=4, r=2)[:, :, par, :, :]
                    ACT_(o_, in_, AF.Gelu_apprx_tanh, [bres[bk]], [BIG.u(kt * 128, 7 * 1024 + (kt + 1) * 128)])
                if pend is not None:
                    back_transposes(pend)
                pend = kt
            back_transposes(pend)

        def tm_project(l, blocks, lhs, ktn):
            for cb in range(2):
                bks = [pa_list[i] for i in range(4)]
                nkc = len(blocks[cb])
                for kc, bid in enumerate(blocks[cb]):
                    rb = ring_load(wb[l, bid], rw_wb[l][bid])
                    wv = wview(rb)
                    for s_ in range(4):
                        for kt in range(8):
                            MM(banks[bks[s_]][:], lhs.ap[:, kc * 8 + kt, s_ * 128:(s_ + 1) * 128], wv[:, kt, :],
                               kc == 0 and kt == 0, kc == nkc - 1 and kt == 7, [rb.ua, lhs.us(kc * 8 + kt, ktn)], [bres[bks[s_]]])
                for s_ in range(4):
                    ACT_(MIXF.ap[:, s_, cb * 512:(cb + 1) * 512], banks[bks[s_]][:], AF.Copy, [bres[bks[s_]]],
                         [MIXF.u(s_ * 1024 + cb * 512, s_ * 1024 + cb * 512 + 512)])

        def postnorm_add(gidx, X):
            for s_ in range(4):
                ACT_(TMPB.ap, MIXF.ap[:, s_, :], AF.Square, [MIXF.us(s_, 4)], [TMP.ua, STAT.ua], accum=STAT.ap[:, 16 + s_:17 + s_])
            rstd_from(16, 4)
            for s_ in range(4):
                STT_(MIXF.ap[:, s_, :], MIXF.ap[:, s_, :], STAT.ap[:, 24 + s_:25 + s_], GBC.ap[:, gidx, :], ALU.mult, ALU.mult,
                     [MIXF.us(s_, 4), STAT.ua, GBC.ua], [MIXF.us(s_, 4)])
                TT_("dve", X.ap[:, s_, :], X.ap[:, s_, :], MIXF.ap[:, s_, :], ALU.add, [X.us(s_, 4), MIXF.us(s_, 4)], [X.us(s_, 4)])

        def load_layer_consts(l):
            DMA(GFM.ap, gfm[l], [], [GFM.ua], q="sp")
            DMA(GBC.ap, gbc[l].rearrange("i p d -> p i d"), [], [GBC.ua], q="sp")
            DMA(BSB.ap.rearrange("p a b -> p (a b)"), bsb[l], [], [BSB.ua], q="sp")
            DMA(RING[0].ap[:, 0:1024], wst[l], [], [RING[0].ua], q="sp")
            P.op("dve", lambda e: e.tensor_copy(out=WST.ap.rearrange("p a b -> p (a b)"), in_=RING[0].ap[:, 0:1024]), [RING[0].ua], [WST.ua])
            P.op("pool", lambda e: e.memset(CARF.ap, 0.0), [], [CARF.ua])
            P.op("pool", lambda e: e.memset(SBIN.ap[:, NT - 1], 0.0), [], [SBIN.ua])

        import os as _os
        TAPS = _os.environ.get("KTAPS", "") == "1"

        def tap(name, b_, l, k):
            if not (TAPS and l == 0 and k == 0):
                return
            dt_ = F32 if b_.es == 4 else BF16
            o_ = nc.dram_tensor("tap_" + name, [128, b_.n], dt_, kind="ExternalOutput").ap()
            flat = b_.ap
            if len(b_.shape) == 2:
                flat = flat.rearrange("p a b -> p (a b)")
            elif len(b_.shape) == 3:
                flat = flat.rearrange("p a b c -> p (a b c)")
            P.dma("sp", o_, flat, [b_.ua], [Res()])

        xstate = {"i": 0, "pending": None}
        r_uc = [Res() for _ in range(NT)]

        def x_fetch(src_ap, rsrc, k, key):
            if xstate["pending"] == key:
                xstate["i"] += 1
                xstate["pending"] = None
                return XB[xstate["i"] % 2]
            Xb = XB[xstate["i"] % 2]
            P.dma("sp", Xb.ap, src_ap[k * TT:(k + 1) * TT, :].rearrange("(s p) d -> p s d", p=128), [rsrc[k]], [Xb.ua])
            return Xb

        def x_prefetch(src_ap, rsrc, k, key):
            Xn = XB[(xstate["i"] + 1) % 2]
            P.dma("sp", Xn.ap, src_ap[k * TT:(k + 1) * TT, :].rearrange("(s p) d -> p s d", p=128), [rsrc[k]], [Xn.ua])
            xstate["pending"] = key

        def branch_a_items(l, k):
            items = []
            for b in range(4):
                def it(b=b):
                    rb = ring_load(wb[l, 6 + b], rw_wb[l][6 + b])

                    def ev(m, bk):
                        mm = (b % 2) * 4 + m
                        ACT_(SG.ap[:, b // 2, mm, :], banks[bk][:], AF.Sigmoid, [bres[bk]], [SG.us((b // 2) * 8 + mm, 16)])
                    dense_fm(H, rb, None, ev)
                items.append(it)
            for b in (0, 1):
                def it(b=b):
                    rb = ring_load(wb[l, b], rw_wb[l][b])

                    def ev(m, bk):
                        mm = b * 4 + m
                        ACT_(RC.ap[:, mm, :], banks[bk][:], AF.Gelu_apprx_tanh, [bres[bk]], [RC.us(mm, 8)])
                    dense_fm(H, rb, None, ev)
                items.append(it)
            for cb in range(2):
                def it(cb=cb):
                    rb = ring_load(wb[l, 2 + cb], rw_wb[l][2 + cb])
                    wv = wview(rb)
                    for s_ in range(4):
                        bk = pa_next()
                        for kt in range(8):
                            MM(banks[bk][:], H.ap[:, kt, s_ * 128:(s_ + 1) * 128], wv[:, kt, :], kt == 0, kt == 7,
                               [rb.ua, H.us(kt, 8)], [bres[bk]])
                        ACT_(VN.ap[:, s_, cb * 512:(cb + 1) * 512], banks[bk][:], AF.Gelu_apprx_tanh, [bres[bk]],
                             [VN.u(s_ * 1024 + cb * 512, s_ * 1024 + cb * 512 + 512)])
                items.append(it)

            def vnorm():
                for s_ in range(4):
                    ACT_(TMPB.ap, VN.ap[:, s_, :], AF.Square, [VN.us(s_, 4)], [TMP.ua, STAT.ua], accum=STAT.ap[:, 32 + s_:33 + s_])
                rstd_from(32, 4)
                for s_ in range(4):
                    P.op("act", lambda e, s_=s_: e.activation(out=VN.ap[:, s_, :], in_=VN.ap[:, s_, :], func=AF.Copy,
                                                              scale=STAT.ap[:, 40 + s_:41 + s_]),
                         [VN.us(s_, 4), STAT.ua], [VN.us(s_, 4)])
            items.append(vnorm)
            for half in range(2):
                def it(half=half):
                    for g in range(half * 4, half * 4 + 4):
                        bk = pa_next()
                        for s_ in range(4):
                            MM(banks[bk][:, s_ * 128:(s_ + 1) * 128], VN.ap[:, s_, g * 128:(g + 1) * 128], WST.ap[:, g, :], True, True,
                               [VN.us(s_, 4), WST.ua], [bres[bk]])
                        STT_(RD.ap[:, g, :].rearrange("p (s q) -> p s q", s=4), banks[bk][:].rearrange("p (s q) -> p s q", s=4),
                             GFM.ap[:, 2, g:g + 1], BSB.ap[:, g, :].rearrange("p (o q) -> p o q", o=1).to_broadcast([128, 4, 128]),
                             ALU.mult, ALU.add, [bres[bk], GFM.ua, BSB.ua], [RD.us(g, 8)])
                        TT_("pool", RC.ap[:, g, :], RC.ap[:, g, :], RD.ap[:, g, :], ALU.mult, [RC.us(g, 8), RD.us(g, 8)], [RC.us(g, 8)])
                items.append(it)
            for b in (0, 1):
                def it(b=b):
                    rb = ring_load(wb[l, 10 + b], rw_wb[l][10 + b])

                    def ev(m, bk):
                        mm = b * 4 + m
                        TT_("dve", SG.ap[:, 0, mm, :], banks[bk][:], SG.ap[:, 0, mm, :], ALU.mult, [bres[bk], SG.us(mm, 16)],
                            [SG.us(mm, 16)])
                    dense_fm(RC, rb, None, ev)
                items.append(it)
            return items

        def layer(l, src_ap, rsrc, dst_ap, rdst):
            load_layer_consts(l)
            if NT > 1:
                k0 = NT - 1
                P.cur_tag = f"L{l}P1T{k0}:prenorm"
                X = x_fetch(src_ap, rsrc, k0, (l, k0))
                x_prefetch(src_ap, rsrc, k0 - 1, (l, k0 - 1))
                prenorm(0, X)
                P.cur_tag = f"L{l}P1T{k0}:ub"
                ub_mm(l, 0)
                ub_mm(l, 1)
            for k in range(NT - 1, 0, -1):
                P.cur_tag = f"L{l}P1T{k}:ub"
                u_tr()
                P.dma("sp", ucache[k], RA.ap.rearrange("p a b -> p (a b)"), [RA.ua], [r_uc[k]])
                fl = []
                if k - 1 >= 1:
                    Xn = x_fetch(src_ap, rsrc, k - 1, (l, k - 1))
                    x_prefetch(src_ap, rsrc, k - 2, (l, k - 2))
                    fl = [lambda Xn=Xn: prenorm(0, Xn), lambda: ub_mm(l, 0), lambda: ub_mm(l, 1)]
                P.cur_tag = f"L{l}P1T{k}:s5st"
                s5_states(l, k, [1], False, fillers=fl)
            pend_store = [None]

            def flush_store():
                if pend_store[0] is not None:
                    Xs, ks = pend_store[0]
                    P.dma("sp", dst_ap[ks * TT:(ks + 1) * TT, :].rearrange("(s p) d -> p s d", p=128), Xs.ap, [Xs.ua], [rdst[ks]])
                    pend_store[0] = None
            for k in range(NT):
                P.cur_tag = f"L{l}P2T{k}:prenorm"
                X = x_fetch(src_ap, rsrc, k, (l, k))

                def st_pf(k=k):
                    flush_store()
                    if k + 1 < NT:
                        x_prefetch(src_ap, rsrc, k + 1, (l, k + 1))
                if k == 0:
                    st_pf()
                    prenorm(0, X)
                    P.cur_tag = f"L{l}P2T{k}:ub"
                    ub_and_transposes(l)
                    fl = branch_a_items(l, k)
                else:
                    fl = [st_pf, lambda X=X: prenorm(0, X)] + branch_a_items(l, k)
                tap("RA", RA, l, k)
                P.cur_tag = f"L{l}P2T{k}:s5st"
                s5_states(l, k, [0, 1], True, fillers=fl)
                tap("H", H, l, k)
                tap("SBF", SBF, l, k)
                tap("SBB", SBB, l, k)
                tap("AIN", RC, l, k)
                P.cur_tag = f"L{l}P2T{k}:s5out"
                s5_out(l)
                tap("RB", RB, l, k)
                P.cur_tag = f"L{l}P2T{k}:glu"
                for b in (2, 3):
                    rb = ring_load(wb[l, 12 + b], rw_wb[l][12 + b])

                    def ev(m, bk, b=b):
                        mm = (b - 2) * 4 + m
                        ACT_(RD.ap[:, mm, :], banks[bk][:], AF.Sigmoid, [bres[bk]], [RD.us(mm, 8)])
                        TT_("pool", RD.ap[:, mm, :], RD.ap[:, mm, :], SG.ap[:, 1, mm, :], ALU.mult, [RD.us(mm, 8), SG.us(8 + mm, 16)],
                            [RD.us(mm, 8)])
                    dense_fm(RB, rb, None, ev)
                for b in (0, 1):
                    rb = ring_load(wb[l, 12 + b], rw_wb[l][12 + b])

                    def ev(m, bk, b=b):
                        mm = b * 4 + m
                        TT_("dve", SG.ap[:, 1, mm, :], banks[bk][:], RD.ap[:, mm, :], ALU.mult, [bres[bk], RD.us(mm, 8)],
                            [SG.us(8 + mm, 16)])
                        TT_("pool", SG.ap[:, 0, mm, :], SG.ap[:, 0, mm, :], SG.ap[:, 1, mm, :], ALU.add,
                            [SG.us(mm, 16), SG.us(8 + mm, 16)], [SG.us(mm, 16)])
                    dense_fm(RB, rb, None, ev)
                    if b == 0 and k + 1 < NT:
                        P.dma("sp", RA.ap.rearrange("p a b -> p (a b)"), ucache[k + 1], [r_uc[k + 1]], [RA.ua])
                P.cur_tag = f"L{l}P2T{k}:wo"
                SG0 = Buf(AR, SG.off, BF16, (8, 512))
                tap("MIXIN", SG0, l, k)
                tm_project(l, [[16], [17]], SG0, 8)
                tap("MIXF", MIXF, l, k)
                postnorm_add(0, X)
                tap("X1", X, l, k)
                P.cur_tag = f"L{l}P2T{k}:prenorm2"
                prenorm(1, X)
                P.cur_tag = f"L{l}P2T{k}:ff1"
                for b in range(8):
                    rb = ring_load(wb[l, 18 + b], rw_wb[l][18 + b])

                    def ev(m, bk, b=b):
                        mm = b * 4 + m
                        ACT_(RD.ap[:, mm % 8, :], banks[bk][:], AF.Square, [bres[bk]], [RD.us(mm % 8, 8)])
                        STT_(HID.ap[:, mm, :], banks[bk][:], 0.0, RD.ap[:, mm % 8, :], ALU.is_gt, ALU.mult,
                             [bres[bk], RD.us(mm % 8, 8)], [HID.us(mm, 32)])
                    dense_fm(H, rb, None, ev)
                P.cur_tag = f"L{l}P2T{k}:ff2"
                tap("HID", HID, l, k)
                tm_project(l, [[26, 27, 28, 29], [30, 31, 32, 33]], HID, 32)
                tap("FF", MIXF, l, k)
                postnorm_add(1, X)
                tap("X2", X, l, k)
                pend_store[0] = (X, k)
            flush_store()

        r_xin = [Res() for _ in range(NT)]
        layer(0, x_in, r_xin, x_mid, r_xmid)
        layer(1, x_mid, r_xmid, y_out, r_yout)
        print("arena top", AR.top, "of", AR.nbytes)
        print("ops:", P.emit())
    return nc


def host_layouts(p):
    f = np.float32
    out = {}
    wf = np.zeros((L, NWB, 128, 4096), f)

    def blk(w, kc, c0):
        return w[kc * 1024:(kc + 1) * 1024, c0:c0 + 512].reshape(8, 128, 512).transpose(1, 0, 2).reshape(128, 4096)
    for l in range(L):
        for b in range(10):
            wf[l, b] = blk(p["w_in"][l], 0, b * 512)
        for b in range(2):
            wf[l, 10 + b] = blk(p["w_out_a"][l], 0, b * 512)
        for b in range(4):
            wf[l, 12 + b] = blk(p["w_glu"][l], 0, b * 512)
        for b in range(2):
            wf[l, 16 + b] = blk(p["w_o"][l], 0, b * 512)
        for b in range(8):
            wf[l, 18 + b] = blk(p["w_ff1"][l], 0, b * 512)
        for cb in range(2):
            for kc in range(4):
                wf[l, 26 + cb * 4 + kc] = blk(p["w_ff2"][l], kc, cb * 512)
    out["wf"] = wf
    hh = np.arange(128) % 16
    jj = np.arange(128) // 16
    a_lam = np.zeros((L, 2, 128, 2, 4096), f)
    a_b = np.zeros((L, 2, 128, 2, 4096), f)
    a_dt = np.zeros((L, 2, 128, 64), f)
    for l in range(L):
        for d in range(2):
            a_lam[l, d, :, 0] = p["lam_re"][l, d].reshape(1, 4096)
            a_lam[l, d, :, 1] = p["lam_im"][l, d].reshape(1, 4096)
            a_b[l, d, :, 0] = p["b_re"][l, d][:, :, hh].transpose(2, 0, 1).reshape(128, 4096)
            a_b[l, d, :, 1] = p["b_im"][l, d][:, :, hh].transpose(2, 0, 1).reshape(128, 4096)
            a_dt[l, d] = p["log_dt"][l, d][None, :]
    out["a_lam"], out["a_b"], out["a_dt"] = a_lam, a_b, a_dt
    a_pw = np.zeros((128, 2), f)
    a_pw[:, 0] = 7 - jj
    a_pw[:, 1] = jj
    out["a_pw"] = a_pw
    a_dsk = np.zeros((L, 128, 64), f)
    for l in range(L):
        a_dsk[l] = p["d_skip"][l].reshape(64, 16)[:, hh].T
    out["a_dsk"] = a_dsk
    par = np.arange(128) // 64
    ss = np.arange(128) % 64
    b_lam = np.zeros((L, 2, 128, 2, 32), f)
    b_dt = np.zeros((L, 2, 128, 32), f)
    b_c = np.zeros((L, 2, 128, 2, 512), f)
    b_b = np.zeros((L, 2, 128, 2, 512), f)
    for l in range(L):
        for d in range(2):
            for pp in range(2):
                rows = slice(pp * 64, pp * 64 + 64)
                b_lam[l, d, rows, 0] = p["lam_re"][l, d][pp::2].T
                b_lam[l, d, rows, 1] = p["lam_im"][l, d][pp::2].T
                b_dt[l, d, rows] = p["log_dt"][l, d][pp::2][None, :]
                b_c[l, d, rows, 0] = p["c_re"][l, d][pp::2].transpose(2, 0, 1).reshape(64, 512)
                b_c[l, d, rows, 1] = p["c_im"][l, d][pp::2].transpose(2, 0, 1).reshape(64, 512)
                b_b[l, d, rows, 0] = p["b_re"][l, d][pp::2].transpose(1, 0, 2).reshape(64, 512)
                b_b[l, d, rows, 1] = p["b_im"][l, d][pp::2].transpose(1, 0, 2).reshape(64, 512)
    out["b_lam"], out["b_dt"], out["b_c"], out["b_b"] = b_lam, b_dt, b_c, b_b
    cst = np.zeros((128, 480), f)
    i8 = np.arange(8)
    cst[:, 0:8] = i8 + 1
    cst[:, 8:16] = 8 - i8
    cst[:, 16:24] = -(i8 + 1)
    cst[:, 24:32] = i8 - 8
    cst[:, 32:96] = np.arange(64)
    ji = np.arange(128) // 16
    cst[:, 96:224] = (ji[None, :] >= ji[:, None])
    cst[:, 224:352] = (ji[None, :] <= ji[:, None])
    cst[:, 352:480] = np.eye(128)
    out["cst"] = cst
    gfm = np.zeros((L, 128, 3, 8), f)
    gbc = np.zeros((L, 2, 128, 1024), f)
    bsb = np.zeros((L, 128, 1024), f)
    wst = np.zeros((L, 128, 1024), f)
    for l in range(L):
        gfm[l, :, 0] = p["norm_pre_mix"][l].reshape(8, 128).T
        gfm[l, :, 1] = p["norm_pre_ff"][l].reshape(8, 128).T
        gfm[l, :, 2] = p["norm_v"][l].reshape(8, 128).T
        gbc[l, 0] = p["norm_post_mix"][l][None, :]
        gbc[l, 1] = p["norm_post_ff"][l][None, :]
        bsb[l] = p["b_s"][l].reshape(1, 1024)
        wst[l] = p["w_s"][l].transpose(2, 0, 1).reshape(128, 1024)
    out["gfm"], out["gbc"], out["bsb"], out["wst"] = gfm, gbc, bsb, wst
    return out


_NC_CACHE = {}


def kernel(**inputs):
    p = {k: np.asarray(v, dtype=np.float32) for k, v in inputs.items()}
    xs = [p["x_prompt"][i] for i in range(2)] + [p["x_sample"][i] for i in range(4)]
    lay = host_layouts(p)
    if "nc" not in _NC_CACHE:
        _NC_CACHE["nc"] = build()
    nc = _NC_CACHE["nc"]
    in_maps = []
    for c in range(NCORES):
        m = dict(lay)
        m["x"] = np.ascontiguousarray(xs[c % 6])
        in_maps.append(m)
    res = run_bass_kernel_spmd(nc, in_maps, core_ids=list(range(NCORES)))
    ys = [res.results[c]["y"] for c in range(6)]
    y_prompt = np.stack(ys[0:2]).astype(np.float32)
    y_sample = np.stack(ys[2:6]).astype(np.float32)
    return (y_prompt, y_sample)
```

```python
import contextlib
import math
import os
import numpy as np
import concourse.bass as bass
import concourse.mybir as mybir
from concourse.bass_utils import run_bass_kernel_spmd

F32 = mybir.dt.float32
BF16 = mybir.dt.bfloat16
AF = mybir.ActivationFunctionType
ALU = mybir.AluOpType

D = 1024
S = 8192
TT = 512
NT = S // TT
NB = 64
L = 2
NWB = 34
EPS = 1e-6
NCORES = 8
MAGIC = 12582912.0
C1 = 6.28125
C2 = 2.0 * math.pi - 6.28125
SINSCALE = 0.999999


class Res:
    __slots__ = ("last_w", "readers")

    def __init__(self):
        self.last_w = None
        self.readers = []


class Op:
    __slots__ = ("eng", "fn", "deps", "needed", "sig", "dma", "dbg", "tag")

    def __init__(self, eng, fn, dma):
        self.eng = eng
        self.fn = fn
        self.deps = []
        self.needed = False
        self.sig = None
        self.dma = dma


EPOCH = 16000
NDMA = 8


def _flat(x, out):
    for r in x:
        if isinstance(r, Res):
            out.append(r)
        else:
            _flat(r, out)
    return out


class Prog:
    ENG = ("pe", "act", "dve", "pool", "sp")

    def __init__(self, nc):
        self.nc = nc
        self.ops = []

    def op(self, eng, fn, reads=(), writes=(), dma=False):
        o = Op(eng, fn, dma)
        import sys as _s
        fr = _s._getframe(1)
        lines = []
        while fr is not None and len(lines) < 4:
            lines.append(fr.f_lineno)
            fr = fr.f_back
        o.dbg = lines
        o.tag = getattr(self, "cur_tag", "")
        reads = _flat(reads, [])
        writes = _flat(writes, [])
        deps = {}
        for r in reads:
            if r.last_w is not None:
                deps[id(r.last_w)] = r.last_w
        for r in writes:
            if r.last_w is not None:
                deps[id(r.last_w)] = r.last_w
            for q in r.readers:
                deps[id(q)] = q
        for r in reads:
            if not dma:
                r.readers = [q for q in r.readers if q.dma or q.eng != eng]
            r.readers.append(o)
        for r in writes:
            r.last_w = o
            r.readers = []
        deps.pop(id(o), None)
        for d in deps.values():
            if d.eng == "pe" and eng == "pe" and not d.dma and not dma:
                continue
            o.deps.append(d)
            d.needed = True
        self.ops.append(o)
        return o

    def dma(self, q, out, in_, reads=(), writes=()):
        return self.op(q, lambda e: e.dma_start(out=out, in_=in_), reads, writes, dma=True)

    def emit(self):
        nc = self.nc
        import os
        lim = int(os.environ.get("KLIMIT", "0"))
        if lim:
            self.ops = self.ops[:lim]
        cnt = {e: 0 for e in self.ENG}
        dcnt = {e: 0 for e in self.ENG}
        for o in self.ops:
            if o.dma:
                k = dcnt[o.eng]
                dcnt[o.eng] += 1
                o.sig = ("d", o.eng, k % NDMA, (k // NDMA + 1) * 16)
            elif o.needed:
                k = cnt[o.eng]
                cnt[o.eng] += 1
                o.sig = ("c", o.eng, k // EPOCH, k % EPOCH + 1)
        sems = {}
        namemap = {} if os.environ.get("KMAP") else None
        self.namemap = namemap
        with contextlib.ExitStack() as stack:
            for e in self.ENG:
                for ep in range((cnt[e] + EPOCH - 1) // EPOCH):
                    sems[("c", e, ep)] = stack.enter_context(nc.semaphore(f"c_{e}_{ep}"))
                for j in range(min(NDMA, dcnt[e])):
                    sems[("d", e, j)] = stack.enter_context(nc.semaphore(f"d_{e}_{j}"))
            block = stack.enter_context(nc.Block())
            per_eng = {e: [o for o in self.ops if o.eng == e] for e in self.ENG}

            def body(ename):
                def f(eng):
                    waited = {}

                    def wait(key, val):
                        if waited.get(key, 0) >= val:
                            return
                        eng.wait_ge(sems[key], val)
                        waited[key] = val
                    for o in per_eng[ename]:
                        for d in o.deps:
                            s = d.sig
                            wait((s[0], s[1], s[2]), s[3])
                        if o.dma and o.sig[3] > 16:
                            s = o.sig
                            wait((s[0], s[1], s[2]), s[3] - 16)
                        ins = o.fn(eng)
                        if namemap is not None:
                            namemap[ins.ins.name] = o.tag
                        if o.sig is not None:
                            s = o.sig
                            ins.then_inc(sems[(s[0], s[1], s[2])], 16 if o.dma else 1)
                    k = dcnt[ename]
                    for j in range(min(NDMA, k)):
                        n = (k - j + NDMA - 1) // NDMA
                        wait(("d", ename, j), n * 16)
                return f
            for e, meth in (("sp", block.sync), ("act", block.scalar), ("dve", block.vector),
                            ("pool", block.gpsimd), ("pe", block.tensor)):
                if per_eng[e]:
                    meth(body(e))
        if namemap is not None:
            import json as _json
            _json.dump(namemap, open(os.environ["KMAP"], "w"))
        return {e: len(v) for e, v in per_eng.items()}


UNIT = 512


class Arena:
    def __init__(self, nc, stack, nbytes):
        self.t = stack.enter_context(nc.sbuf_tensor("arena", [128, nbytes // 4], F32))
        self.units = [Res() for _ in range((nbytes + UNIT - 1) // UNIT)]
        self.top = 0
        self.nbytes = nbytes

    def alloc(self, nbytes):
        off = self.top
        self.top += (nbytes + UNIT - 1) // UNIT * UNIT
        assert self.top <= self.nbytes, (self.top, self.nbytes)
        return off


class Buf:
    def __init__(self, arena, off, dtype, shape):
        self.arena = arena
        self.off = off
        self.es = 4 if dtype == F32 else 2
        n = 1
        for s in shape:
            n *= s
        self.n = n
        ap = arena.t[:, off // 4:(off + n * self.es + 3) // 4]
        if dtype != F32:
            ap = ap.bitcast(dtype)
        if len(shape) == 2:
            ap = ap.rearrange("p (a b) -> p a b", a=shape[0])
        elif len(shape) == 3:
            ap = ap.rearrange("p (a b c) -> p a b c", a=shape[0], b=shape[1])
        elif len(shape) == 4:
            ap = ap.rearrange("p (a b c d) -> p a b c d", a=shape[0], b=shape[1], c=shape[2])
        self.ap = ap
        self.shape = shape
        self.ua = self.u(0, n)

    def u(self, lo, hi):
        b0 = (self.off + lo * self.es) // UNIT
        b1 = (self.off + hi * self.es + UNIT - 1) // UNIT
        return self.arena.units[b0:b1]

    def us(self, i, n_i):
        w = self.n // n_i
        return self.u(i * w, (i + 1) * w)


def build(dbg=False):
    nc = bass.Bass("TRN2", target_bir_lowering=False)

    def din(name, shape, dt=F32):
        return nc.dram_tensor(name, list(shape), dt, kind="ExternalInput").ap()

    def dint(name, shape, dt):
        return nc.dram_tensor(name, list(shape), dt, kind="Internal").ap()

    x_in = din("x", [S, D])
    wf = din("wf", [L, NWB, 128, 4096])
    a_lam = din("a_lam", [L, 2, 128, 2, 4096])
    a_b = din("a_b", [L, 2, 128, 2, 4096])
    a_dt = din("a_dt", [L, 2, 128, 64])
    a_pw = din("a_pw", [128, 2])
    a_dsk = din("a_dsk", [L, 128, 64])
    b_lam = din("b_lam", [L, 2, 128, 2, 32])
    b_dt = din("b_dt", [L, 2, 128, 32])
    b_c = din("b_c", [L, 2, 128, 2, 512])
    b_b = din("b_b", [L, 2, 128, 2, 512])
    cst = din("cst", [128, 8 + 8 + 8 + 8 + 64 + 128 + 128 + 128])
    gfm = din("gfm", [L, 128, 3, 8])
    gbc = din("gbc", [L, 2, 128, 1024])
    bsb = din("bsb", [L, 128, 1024])
    wst = din("wst", [L, 128, 1024])
    y_out = nc.dram_tensor("y", [S, D], F32, kind="ExternalOutput").ap()
    wb = dint("wb", [L, NWB, 128, 4096], BF16)
    s5b = dint("s5b", [L, 10, 128, 4096], BF16)
    s5t = dint("s5t", [L, 4, 128, 2048], F32)
    x_mid = dint("x_mid", [S, D], F32)
    ucache = dint("ucache", [NT, 128, 4096], BF16)
    dbgo = {}
    if dbg:
        dbgo["s5b"] = nc.dram_tensor("dbg_s5b", [L, 10, 128, 4096], F32, kind="ExternalOutput").ap()
        dbgo["s5t"] = nc.dram_tensor("dbg_s5t", [L, 4, 128, 2048], F32, kind="ExternalOutput").ap()

    P = Prog(nc)
    with contextlib.ExitStack() as es:
        AR = Arena(nc, es, 206 * 1024)
        banks = [es.enter_context(nc.psum_tensor(f"ps{i}", [128, 512], F32)) for i in range(8)]
        bres = [Res() for _ in range(8)]
        bres_h = [[Res(), Res()] for _ in range(8)]

        def buf(dtype, shape):
            n = 1
            for s_ in shape:
                n *= s_
            off = AR.alloc(n * (4 if dtype == F32 else 2))
            return Buf(AR, off, dtype, shape)

        def alias(b, byte_off, dtype, shape):
            return Buf(AR, b.off + byte_off, dtype, shape)

        TMP = buf(BF16, (1024,))
        TMPB = TMP
        IDB = buf(BF16, (128,))
        CST = buf(F32, (480,))
        GFM = buf(F32, (3, 8))
        GBC = buf(F32, (2, 1024))
        BSB = buf(F32, (8, 128))
        WST = buf(BF16, (8, 128))
        STAT = buf(F32, (64,))
        CARF = buf(F32, (2, 32))
        SBIN = buf(F32, (NT, 2, 32))
        RTAB = buf(F32, (L, 2, 32))
        WINIT = buf(F32, (2, 2, 32))
        SML = buf(F32, (8, 8))
        EPSC = buf(F32, (4,))
        PWA = buf(F32, (2,))
        ROT = buf(F32, (L, 2, 4, 32))
        WLAST = buf(F32, (2, 2, 32))
        ZT34 = [buf(F32, (8, 64)) for _ in range(2)]
        ROTT = buf(F32, (4, 32))
        MASK0 = buf(F32, (8, 64))
        RZ = [buf(F32, (8, 64))]
        RW = buf(F32, (2, 2, 32))
        main_base = AR.top
        XB = [buf(F32, (4, 1024)) for _ in range(2)]
        H = buf(BF16, (8, 512))
        SG = buf(BF16, (2, 8, 512))
        BIG = buf(BF16, (8, 1024))
        MIXF = alias(BIG, 0, F32, (4, 1024))
        RA = buf(BF16, (64, 64))
        VN = None
        RB = buf(BF16, (8, 512))
        VN = alias(RB, 0, BF16, (4, 1024))
        RC = buf(BF16, (8, 512))
        RD = buf(BF16, (8, 512))
        HN = alias(RD, 0, BF16, (4, 1024))
        HID = buf(BF16, (32, 512))
        hid_off = HID.off
        ZW = [Buf(AR, hid_off + i * 2048, F32, (8, 64)) for i in range(6)]
        SBF = Buf(AR, hid_off + 12288, BF16, (32, 2, 65))
        SBB = Buf(AR, hid_off + 12288 + 8320, BF16, (32, 2, 65))
        assert 12288 + 2 * 8320 <= 32768
        NRING = 4
        RING = [buf(F32, (2048,)) for _ in range(NRING)]
        assert AR.top - main_base >= 141 * 1024

        def TT_(eng, out, i0, i1, op, r, w):
            P.op(eng, lambda e: e.tensor_tensor(out=out, in0=i0, in1=i1, op=op), r, w)

        def TS_(eng, out, i0, s1, s2, op0, op1, r, w):
            if op1 is None:
                P.op(eng, lambda e: e.tensor_scalar(out=out, in0=i0, scalar1=s1, scalar2=None, op0=op0), r, w)
            else:
                P.op(eng, lambda e: e.tensor_scalar(out=out, in0=i0, scalar1=s1, scalar2=s2, op0=op0, op1=op1), r, w)

        def STT_(out, i0, sc, i1, op0, op1, r, w, eng="dve"):
            P.op(eng, lambda e: e.scalar_tensor_tensor(out=out, in0=i0, scalar=sc, in1=i1, op0=op0, op1=op1), r, w)

        def ACT_(out, in_, func, r, w, scale=1.0, bias=None, accum=None):
            def f(e):
                kw = {}
                if bias is not None:
                    kw["bias"] = bias
                if accum is not None:
                    kw["accum_out"] = accum
                return e.activation(out=out, in_=in_, func=func, scale=scale, **kw)
            P.op("act", f, r, w)

        def MM(out, lhsT, rhs, start, stop, r, w):
            P.op("pe", lambda e: e.matmul(out, lhsT=lhsT, rhs=rhs, start=start, stop=stop), r, w)

        def TR(out, in_, ident, r, w):
            P.op("pe", lambda e: e.transpose(out, in_, ident), r, w)

        dq = [0]

        def DMA(out, in_, r, w, q=None):
            if q is None:
                q = ("sp", "act")[dq[0] % 2]
                dq[0] += 1
            P.dma(q, out, in_, r, w)

        rw_wb = [[Res() for _ in range(NWB)] for _ in range(L)]
        rw_s5b = [[Res() for _ in range(10)] for _ in range(L)]
        rw_s5t = [[Res() for _ in range(4)] for _ in range(L)]
        r_xmid = [Res() for _ in range(NT)]
        r_yout = [Res() for _ in range(NT)]
        DMA(CST.ap, cst, [], [CST.ua])
        DMA(PWA.ap, a_pw, [], [PWA.ua])
        pw3 = [CST.ap[:, 0:8], CST.ap[:, 8:16]]
        pwL = [CST.ap[:, 16:24], CST.ap[:, 24:32]]
        qvec = CST.ap[:, 32:96]
        maskF = CST.ap[:, 96:224]
        maskB = CST.ap[:, 224:352]
        identF = CST.ap[:, 352:480]
        P.op("dve", lambda e: e.tensor_copy(out=IDB.ap, in_=identF), [CST.ua], [IDB.ua])
        P.op("pool", lambda e: e.memset(EPSC.ap[:, 0:1], EPS), [], [EPSC.ua])
        P.op("pool", lambda e: e.memset(EPSC.ap[:, 1:2], SINSCALE * math.pi / 2), [], [EPSC.ua])
        P.op("pool", lambda e: e.memset(EPSC.ap[:, 2:3], 0.0), [], [EPSC.ua])
        P.op("pool", lambda e: e.memset(MASK0.ap, 1.0), [], [MASK0.ua])
        P.op("pool", lambda e: e.memset(MASK0.ap[:, :, 0:1], 0.0), [], [MASK0.ua])
        eps_ap = EPSC.ap[:, 0:1]
        hpi_ap = EPSC.ap[:, 1:2]
        zero_ap = EPSC.ap[:, 2:3]

        for l in range(L):
            for b in list(range(4, 10)) + list(range(0, 4)) + list(range(10, NWB)):
                P.dma("pool", wb[l, b], wf[l, b], [], [rw_wb[l][b]])

        def setup_layer(l):
            base = main_base
            KB = 1024
            nA = 18
            A_ = [Buf(AR, base + i * 4096, F32, (16, 64)) for i in range(nA)]
            stg = Buf(AR, base + 72 * KB, BF16, (16, 2, 2, 64))
            dts = Buf(AR, base + 80 * KB, F32, (2, 64))

            def mul(o, a, b, eng="dve"):
                TT_(eng, o.ap, a.ap, b.ap, ALU.mult, [a.ua, b.ua], [o.ua])

            def sincos(x, c_out, s_out, t1, t2, shape_ap=lambda b: b.ap):
                for (dst, off) in ((s_out, 0.0), (c_out, 0.25)):
                    cur = x
                    for rep in range(2):
                        TS_("dve", shape_ap(t1), shape_ap(cur), 1.0 / (2 * math.pi), off, ALU.mult, ALU.add, [cur.ua], [t1.ua])
                        TS_("dve", shape_ap(t1), shape_ap(t1), MAGIC, -MAGIC, ALU.add, ALU.add, [t1.ua], [t1.ua])
                        STT_(shape_ap(t2), shape_ap(t1), -C1, shape_ap(cur), ALU.mult, ALU.add, [t1.ua, cur.ua], [t2.ua])
                        STT_(shape_ap(t2), shape_ap(t1), -C2, shape_ap(t2), ALU.mult, ALU.add, [t1.ua, t2.ua], [t2.ua])
                        cur = t2
                    ACT_(shape_ap(dst), shape_ap(t2), AF.Sin, [t2.ua, EPSC.ua], [dst.ua], scale=SINSCALE,
                         bias=(zero_ap if off == 0.0 else hpi_ap))

            def cmul(o_r, o_i, a_r, a_i, b_r, b_i, t1, t2, neg_im=False, ap=lambda b: b.ap):
                TT_("dve", ap(t1), ap(a_r), ap(b_r), ALU.mult, [a_r.ua, b_r.ua], [t1.ua])
                TT_("pool", ap(t2), ap(a_i), ap(b_i), ALU.mult, [a_i.ua, b_i.ua], [t2.ua])
                TT_("dve", ap(o_r), ap(t1), ap(t2), ALU.subtract, [t1.ua, t2.ua], [o_r.ua])
                TT_("dve", ap(t1), ap(a_r), ap(b_i), ALU.mult, [a_r.ua, b_i.ua], [t1.ua])
                TT_("pool", ap(t2), ap(a_i), ap(b_r), ALU.mult, [a_i.ua, b_r.ua], [t2.ua])
                TT_("dve", ap(o_i), ap(t1), ap(t2), ALU.add, [t1.ua, t2.ua], [o_i.ua])

            for d in range(2):
                DMA(dts.ap[:, d], a_dt[l, d], [], [dts.ua])
            ACT_(dts.ap, dts.ap, AF.Exp, [dts.ua], [dts.ua])
            for c in range(4):
                for d in range(2):
                    LR, LI, BR, BI, AR_, AI_, EA, CA, SA, T1, T2, KR, KI, QR, QI, PR, PI, T3 = A_
                    g0 = c * 16
                    DMA(LR.ap, a_lam[l, d, :, 0, g0 * 64:(g0 + 16) * 64].rearrange("p (g s) -> p g s", g=16), [], [LR.ua])
                    DMA(LI.ap, a_lam[l, d, :, 1, g0 * 64:(g0 + 16) * 64].rearrange("p (g s) -> p g s", g=16), [], [LI.ua])
                    DMA(BR.ap, a_b[l, d, :, 0, g0 * 64:(g0 + 16) * 64].rearrange("p (g s) -> p g s", g=16), [], [BR.ua])
                    DMA(BI.ap, a_b[l, d, :, 1, g0 * 64:(g0 + 16) * 64].rearrange("p (g s) -> p g s", g=16), [], [BI.ua])
                    dtb = dts.ap[:, d, g0:g0 + 16].rearrange("p (g o) -> p g o", o=1).to_broadcast([128, 16, 64])
                    TT_("dve", AR_.ap, LR.ap, dtb, ALU.mult, [LR.ua, dts.ua], [AR_.ua])
                    TT_("pool", AI_.ap, LI.ap, dtb, ALU.mult, [LI.ua, dts.ua], [AI_.ua])
                    ACT_(EA.ap, AR_.ap, AF.Exp, [AR_.ua], [EA.ua])
                    sincos(AI_, CA, SA, T1, T2)
                    mul(CA, CA, EA)
                    mul(SA, SA, EA)
                    mul(T1, LR, LR)
                    mul(T2, LI, LI, "pool")
                    TT_("dve", T1.ap, T1.ap, T2.ap, ALU.add, [T1.ua, T2.ua], [T1.ua])
                    P.op("dve", lambda e, T1=T1: e.reciprocal(out=T1.ap, in_=T1.ap), [T1.ua], [T1.ua])
                    TS_("dve", EA.ap, CA.ap, -1.0, None, ALU.add, None, [CA.ua], [EA.ua])
                    mul(T2, EA, LR)
                    mul(T3, SA, LI, "pool")
                    TT_("dve", KR.ap, T2.ap, T3.ap, ALU.add, [T2.ua, T3.ua], [KR.ua])
                    mul(T2, SA, LR)
                    mul(T3, EA, LI, "pool")
                    TT_("dve", KI.ap, T2.ap, T3.ap, ALU.subtract, [T2.ua, T3.ua], [KI.ua])
                    mul(KR, KR, T1)
                    mul(KI, KI, T1)
                    cmul(QR, QI, KR, KI, BR, BI, T1, T2)
                    pcol = PWA.ap[:, d:d + 1]
                    TS_("dve", T1.ap, AR_.ap, pcol, None, ALU.mult, None, [AR_.ua, PWA.ua], [T1.ua])
                    ACT_(EA.ap, T1.ap, AF.Exp, [T1.ua], [EA.ua])
                    TS_("dve", T3.ap, AI_.ap, pcol, None, ALU.mult, None, [AI_.ua, PWA.ua], [T3.ua])
                    sincos(T3, PR, PI, T1, T2)
                    mul(PR, PR, EA)
                    mul(PI, PI, EA)
                    TT_("dve", T1.ap, PR.ap, QR.ap, ALU.mult, [PR.ua, QR.ua], [T1.ua])
                    TT_("pool", T2.ap, PI.ap, QI.ap, ALU.mult, [PI.ua, QI.ua], [T2.ua])
                    TT_("dve", stg.ap[:, :, d, 0, :], T1.ap, T2.ap, ALU.subtract, [T1.ua, T2.ua], [stg.ua])
                    TT_("dve", T1.ap, PR.ap, QI.ap, ALU.mult, [PR.ua, QI.ua], [T1.ua])
                    TT_("pool", T2.ap, PI.ap, QR.ap, ALU.mult, [PI.ua, QR.ua], [T2.ua])
                    TT_("dve", stg.ap[:, :, d, 1, :], T1.ap, T2.ap, ALU.add, [T1.ua, T2.ua], [stg.ua])
                DMA(s5b[l, 2 + c], stg.ap.rearrange("p a b c d -> p (a b c d)"), [stg.ua], [rw_s5b[l][2 + c]])

            small = [Buf(AR, base + i * 128, F32, (32,)) for i in range(24)]
            (bLR, bLI, bDT, bAR, bAI, bEA, bCA, bSA, bT1, bT2, bT3, bKR, bKI, bR8, bPH) = small[:15]
            bC = [Buf(AR, base + 4 * KB + i * 2048, F32, (32, 16)) for i in range(2)]
            bB = [Buf(AR, base + 8 * KB + i * 2048, F32, (32, 16)) for i in range(2)]
            bQ = [Buf(AR, base + 12 * KB + i * 2048, F32, (32, 16)) for i in range(2)]
            bTq = [Buf(AR, base + 16 * KB + i * 2048, F32, (32, 16)) for i in range(2)]
            P38 = [Buf(AR, base + 20 * KB + i * 1024, F32, (32, 8)) for i in range(4)]
            P38b = [Buf(AR, base + 24 * KB + i * 1024, F32, (32, 8)) for i in range(4)]
            PL8 = [Buf(AR, base + 28 * KB + i * 1024, F32, (32, 8)) for i in range(4)]
            PL8b = [Buf(AR, base + 32 * KB + i * 1024, F32, (32, 8)) for i in range(4)]
            ANG = [Buf(AR, base + 36 * KB + i * 8192, F32, (32, 64)) for i in range(2)]
            TAB = [Buf(AR, base + 52 * KB + i * 8192, F32, (32, 64)) for i in range(2)]
            M3F = [Buf(AR, base + 68 * KB + d * 8192, F32, (8, 2, 128)) for d in range(2)]
            QLF = [Buf(AR, base + 84 * KB + d * 8192, BF16, (8, 2, 128)) for d in range(2)]
            E4 = [Buf(AR, base + 100 * KB + i * 4096, F32, (8, 8, 16)) for i in range(4)]
            stg3 = Buf(AR, base + 116 * KB, BF16, (8, 2, 2, 128))
            stg1 = Buf(AR, base + 124 * KB, BF16, (32, 128))
            M1T = [Buf(AR, base + 132 * KB + i * 2048, F32, (4, 128)) for i in range(2)]
            M1G = [Buf(AR, base + 136 * KB + i * 2048, F32, (4, 128)) for i in range(2)]
            dsk = Buf(AR, base + 140 * KB, F32, (64,))
            DMA(dsk.ap, a_dsk[l], [], [dsk.ua])

            for d in range(2):
                DMA(bLR.ap, b_lam[l, d, :, 0], [], [bLR.ua])
                DMA(bLI.ap, b_lam[l, d, :, 1], [], [bLI.ua])
                DMA(bDT.ap, b_dt[l, d], [], [bDT.ua])
                for i in range(2):
                    DMA(bC[i].ap, b_c[l, d, :, i].rearrange("p (a b) -> p a b", a=32), [], [bC[i].ua])
                    DMA(bB[i].ap, b_b[l, d, :, i].rearrange("p (a b) -> p a b", a=32), [], [bB[i].ua])
                ACT_(bDT.ap, bDT.ap, AF.Exp, [bDT.ua], [bDT.ua])
                mul(bAR, bLR, bDT)
                mul(bAI, bLI, bDT)
                ACT_(bEA.ap, bAR.ap, AF.Exp, [bAR.ua], [bEA.ua])
                sincos(bAI, bCA, bSA, bT1, bT2)
                mul(bCA, bCA, bEA)
                mul(bSA, bSA, bEA)
                mul(bT1, bLR, bLR)
                mul(bT2, bLI, bLI)
                TT_("dve", bT1.ap, bT1.ap, bT2.ap, ALU.add, [bT1.ua, bT2.ua], [bT1.ua])
                P.op("dve", lambda e: e.reciprocal(out=bT1.ap, in_=bT1.ap), [bT1.ua], [bT1.ua])
                TS_("dve", bEA.ap, bCA.ap, -1.0, None, ALU.add, None, [bCA.ua], [bEA.ua])
                mul(bT2, bEA, bLR)
                mul(bT3, bSA, bLI)
                TT_("dve", bKR.ap, bT2.ap, bT3.ap, ALU.add, [bT2.ua, bT3.ua], [bKR.ua])
                mul(bT2, bSA, bLR)
                mul(bT3, bEA, bLI)
                TT_("dve", bKI.ap, bT2.ap, bT3.ap, ALU.subtract, [bT2.ua, bT3.ua], [bKI.ua])
                mul(bKR, bKR, bT1)
                mul(bKI, bKI, bT1)
                b16 = lambda b_: b_.ap.rearrange("p (a o) -> p a o", o=1).to_broadcast([128, 32, 16])
                TT_("dve", bTq[0].ap, bB[0].ap, b16(bKR), ALU.mult, [bB[0].ua, bKR.ua], [bTq[0].ua])
                TT_("dve", bTq[1].ap, bB[1].ap, b16(bKI), ALU.mult, [bB[1].ua, bKI.ua], [bTq[1].ua])
                TT_("dve", bQ[0].ap, bTq[0].ap, bTq[1].ap, ALU.subtract, [bTq[0].ua, bTq[1].ua], [bQ[0].ua])
                TT_("dve", bTq[0].ap, bB[1].ap, b16(bKR), ALU.mult, [bB[1].ua, bKR.ua], [bTq[0].ua])
                TT_("dve", bTq[1].ap, bB[0].ap, b16(bKI), ALU.mult, [bB[0].ua, bKI.ua], [bTq[1].ua])
                TT_("dve", bQ[1].ap, bTq[0].ap, bTq[1].ap, ALU.add, [bTq[0].ua, bTq[1].ua], [bQ[1].ua])
                ACT_(RTAB.ap[:, l, d, :], bAR.ap, AF.Exp, [bAR.ua], [RTAB.ua], scale=8.0)
                TS_("dve", bPH.ap, bAI.ap, 8.0, None, ALU.mult, None, [bAI.ua], [bPH.ua])
                TT_("dve", ANG[0].ap, bPH.ap.rearrange("p (a o) -> p a o", o=1).to_broadcast([128, 32, 64]),
                    qvec.rearrange("p (o q) -> p o q", o=1).to_broadcast([128, 32, 64]), ALU.mult,
                    [bPH.ua, CST.ua], [ANG[0].ua])
                sincos(ANG[0], TAB[0], TAB[1], ANG[1], Buf(AR, E4[0].off, F32, (32, 64)))
                for cs in range(2):
                    for (ki, col) in ((0, 1), (2, 63)):
                        P.op("dve", lambda e, cs=cs, ki=ki, col=col, d=d: e.tensor_copy(out=ROT.ap[:, l, d, ki + cs, :],
                                                                                       in_=TAB[cs].ap[:, :, col]),
                             [TAB[cs].ua], [ROT.ua])
                for c in range(4):
                    for cs in range(2):
                        DMA(s5t[l, c, :, (d * 2 + cs) * 512:(d * 2 + cs + 1) * 512].rearrange("p (a q) -> p a q", a=8),
                            TAB[cs].ap[:, c * 8:(c + 1) * 8, :], [TAB[cs].ua], [rw_s5t[l][c]])
                for (PP, pv) in ((P38 if d == 0 else P38b, pw3[d]), (PL8 if d == 0 else PL8b, pwL[d])):
                    b8 = lambda b_: b_.ap.rearrange("p (a o) -> p a o", o=1).to_broadcast([128, 32, 8])
                    pvb = pv.rearrange("p (o q) -> p o q", o=1).to_broadcast([128, 32, 8])
                    TT_("dve", PP[3].ap, b8(bAR), pvb, ALU.mult, [bAR.ua, CST.ua], [PP[3].ua])
                    ACT_(PP[0].ap, PP[3].ap, AF.Exp, [PP[3].ua], [PP[0].ua])
                    TT_("dve", PP[3].ap, b8(bAI), pvb, ALU.mult, [bAI.ua, CST.ua], [PP[3].ua])
                    t_a = Buf(AR, E4[1].off, F32, (32, 8))
                    t_b = Buf(AR, E4[1].off + 1024, F32, (32, 8))
                    sincos(PP[3], PP[1], PP[2], t_a, t_b)
                    mul(PP[1], PP[1], PP[0])
                    mul(PP[2], PP[2], PP[0])
                PP3 = P38 if d == 0 else P38b
                PPL = PL8 if d == 0 else PL8b
                for c in range(4):
                    ps_ = slice(c * 8, (c + 1) * 8)

                    def bc_i(b_):
                        return b_.ap[:, ps_, :].rearrange("p a (i o) -> p a i o", o=1).to_broadcast([128, 8, 8, 16])

                    def bc_h(b_):
                        return b_.ap[:, ps_, :].rearrange("p a (o h) -> p a o h", o=1).to_broadcast([128, 8, 8, 16])

                    def o4(b_, part):
                        return b_.ap[:, :, part, :].rearrange("p a (i h) -> p a i h", i=8)
                    TT_("dve", E4[0].ap, bc_h(bC[0]), bc_i(PP3[1]), ALU.mult, [bC[0].ua, PP3[1].ua], [E4[0].ua])
                    TT_("pool", E4[1].ap, bc_h(bC[1]), bc_i(PP3[2]), ALU.mult, [bC[1].ua, PP3[2].ua], [E4[1].ua])
                    TT_("dve", o4(M3F[d], 0), E4[0].ap, E4[1].ap, ALU.subtract, [E4[0].ua, E4[1].ua], [M3F[d].ua])
                    TT_("dve", E4[0].ap, bc_h(bC[0]), bc_i(PP3[2]), ALU.mult, [bC[0].ua, PP3[2].ua], [E4[0].ua])
                    TT_("pool", E4[1].ap, bc_h(bC[1]), bc_i(PP3[1]), ALU.mult, [bC[1].ua, PP3[1].ua], [E4[1].ua])
                    STT_(o4(M3F[d], 1), E4[0].ap, -1.0, E4[1].ap, ALU.mult, ALU.subtract, [E4[0].ua, E4[1].ua], [M3F[d].ua])
                    TT_("dve", E4[2].ap, bc_h(bQ[0]), bc_i(PPL[1]), ALU.mult, [bQ[0].ua, PPL[1].ua], [E4[2].ua])
                    TT_("pool", E4[3].ap, bc_h(bQ[1]), bc_i(PPL[2]), ALU.mult, [bQ[1].ua, PPL[2].ua], [E4[3].ua])
                    TT_("dve", o4(QLF[d], 0), E4[2].ap, E4[3].ap, ALU.subtract, [E4[2].ua, E4[3].ua], [QLF[d].ua])
                    TT_("dve", E4[2].ap, bc_h(bQ[0]), bc_i(PPL[2]), ALU.mult, [bQ[0].ua, PPL[2].ua], [E4[2].ua])
                    TT_("pool", E4[3].ap, bc_h(bQ[1]), bc_i(PPL[1]), ALU.mult, [bQ[1].ua, PPL[1].ua], [E4[3].ua])
                    TT_("dve", o4(QLF[d], 1), E4[2].ap, E4[3].ap, ALU.add, [E4[2].ua, E4[3].ua], [QLF[d].ua])
                    P.op("act", lambda e, d=d: e.activation(out=stg3.ap[:, :, d, :, :], in_=M3F[d].ap, func=AF.Copy),
                         [M3F[d].ua], [stg3.ua])
                    DMA(s5b[l, 6 + c].rearrange("p (a b c e) -> p a b c e", a=8, b=2, c=2)[:, :, d, :, :],
                        stg3.ap[:, :, d, :, :], [stg3.ua], [rw_s5b[l][6 + c]])
                    for half in range(2):
                        for pl4 in range(4):
                            pl = half * 4 + pl4
                            for par in range(2):
                                rows = slice(par * 64, par * 64 + 64)
                                bk = 4 + par
                                outp = banks[bk][:, pl4 * 128:(pl4 + 1) * 128]
                                MM(outp, QLF[d].ap[rows, pl, 0, :], stg3.ap[rows, pl, d, 0, :], True, False,
                                   [QLF[d].ua, stg3.ua], [bres[bk]])
                                MM(outp, QLF[d].ap[rows, pl, 1, :], stg3.ap[rows, pl, d, 1, :], False, True,
                                   [QLF[d].ua, stg3.ua], [bres[bk]])
                        for par in range(2):
                            bk = 4 + par
                            idx = (c * 2 + half) * 2 + par
                            msk = (maskF if d == 0 else maskB).rearrange("p (o n) -> p o n", o=1).to_broadcast([128, 4, 128])
                            TT_("dve", M1T[par].ap, banks[bk][:].rearrange("p (a n) -> p a n", a=4), msk, ALU.mult,
                                [bres[bk], CST.ua], [M1T[par].ua])
                            if d == 0:
                                DMA(m1f[l, idx], M1T[par].ap.rearrange("p a n -> p (a n)"), [M1T[par].ua], [r_m1f[l][idx]])
                            else:
                                DMA(M1G[par].ap.rearrange("p a n -> p (a n)"), m1f[l, idx], [r_m1f[l][idx]], [M1G[par].ua])
                                TT_("dve", M1T[par].ap, M1T[par].ap, M1G[par].ap, ALU.add, [M1T[par].ua, M1G[par].ua], [M1T[par].ua])
                                for m4 in range(4):
                                    gg = 2 * (c * 8 + half * 4 + m4) + par
                                    STT_(stg1.ap[:, gg % 32, :], identF, dsk.ap[:, gg:gg + 1], M1T[par].ap[:, m4, :],
                                         ALU.mult, ALU.add, [CST.ua, dsk.ua, M1T[par].ua], [stg1.ua])
                    if d == 1 and c % 2 == 1:
                        DMA(s5b[l, c // 2], stg1.ap.rearrange("p a b -> p (a b)"), [stg1.ua], [rw_s5b[l][c // 2]])

        m1f = dint("m1f", [L, 16, 128, 512], F32)
        r_m1f = [[Res() for _ in range(16)] for _ in range(L)]

        for l in range(L):
            setup_layer(l)
        if dbg:
            for l in range(L):
                for b in range(10):
                    P.dma("pool", dbgo["s5b"][l, b], s5b[l, b], [rw_s5b[l][b]], [Res()])
                for b in range(4):
                    P.dma("sp", dbgo["s5t"][l, b], s5t[l, b], [rw_s5t[l][b]], [Res()])
            X = XB[0]
            P.op("pool", lambda e: e.memset(X.ap, 0.0), [], [X.ua])
            for k in range(NT):
                DMA(y_out[k * TT:(k + 1) * TT, :].rearrange("(s p) d -> p s d", p=128), X.ap, [X.ua], [r_yout[k]])
            print("ops:", P.emit())
            return nc

        PTB = [banks[6 + h][:].bitcast(BF16)[:, 0:512] for h in range(2)]
        PTR = [bres[6], bres[7]]
        pa_list = [0, 1, 2, 3]
        pa_i = [0]

        def pa_next():
            b_ = pa_list[pa_i[0] % len(pa_list)]
            pa_i[0] += 1
            return b_
        ring_i = [0]

        def ring_load(dram_ap, rres, as_f32=False):
            s_ = ring_i[0] % NRING
            ring_i[0] += 1
            rb = RING[s_]
            if as_f32:
                P.dma("sp", rb.ap, dram_ap, [rres], [rb.ua])
            else:
                P.dma("sp", rb.ap.bitcast(BF16), dram_ap, [rres], [rb.ua])
            return rb

        def wview(rb):
            return rb.ap.bitcast(BF16).rearrange("p (k c) -> p k c", k=8)

        id64 = IDB.ap[0:64, 0:64]
        pt_i = [0]

        def pt_next():
            h_ = pt_i[0] % 2
            pt_i[0] += 1
            return h_

        def rstd_from(ss_cols, n):
            ACT_(STAT.ap[:, ss_cols + 4:ss_cols + 4 + n], STAT.ap[:, ss_cols:ss_cols + n], AF.Sqrt, [STAT.ua, EPSC.ua], [STAT.ua],
                 scale=1.0 / 1024.0, bias=eps_ap)
            P.op("dve", lambda e: e.reciprocal(out=STAT.ap[:, ss_cols + 8:ss_cols + 8 + n], in_=STAT.ap[:, ss_cols + 4:ss_cols + 4 + n]),
                 [STAT.ua], [STAT.ua])

        def prenorm(gidx, X):
            for s_ in range(4):
                ACT_(TMPB.ap, X.ap[:, s_, :], AF.Square, [X.us(s_, 4)], [TMP.ua, STAT.ua], accum=STAT.ap[:, s_:s_ + 1])
            rstd_from(0, 4)
            for s_ in range(4):
                ACT_(HN.ap[:, s_, :], X.ap[:, s_, :], AF.Copy, [X.us(s_, 4), STAT.ua], [HN.us(s_, 4)], scale=STAT.ap[:, 8 + s_:9 + s_])
            for kt in range(8):
                h_ = pt_next()
                for s_ in range(4):
                    TR(PTB[h_][:, s_ * 128:(s_ + 1) * 128], HN.ap[:, s_, kt * 128:(kt + 1) * 128], IDB.ap,
                       [HN.us(s_, 4), IDB.ua], [PTR[h_]])
                TS_("dve", H.ap[:, kt, :], PTB[h_], GFM.ap[:, gidx, kt:kt + 1], None, ALU.mult, None,
                    [PTR[h_], GFM.ua], [H.us(kt, 8)])

        def dense_fm(rhs, rb, rres_w, evac):
            wv = wview(rb)
            for m in range(4):
                bk = pa_next()
                for kt in range(8):
                    MM(banks[bk][:], wv[:, kt, m * 128:(m + 1) * 128], rhs.ap[:, kt, :], kt == 0, kt == 7,
                       [rb.ua, rhs.us(kt, 8)], [bres[bk]])
                evac(m, bk)

        def ub_and_transposes(l):
            for cb in range(2):
                rb = ring_load(wb[l, 4 + cb], rw_wb[l][4 + cb])
                wv = wview(rb)
                for j in range(8):
                    bk = pa_next()
                    for kt in range(8):
                        MM(banks[bk][0:64, :], H.ap[:, kt, j:512:8], wv[:, kt, :], kt == 0, kt == 7,
                           [rb.ua, H.us(kt, 8)], [bres[bk]])
                    o_ = BIG.ap.rearrange("p a b -> p (a b)").rearrange("p (g j h) -> p g j h", g=64, j=8)[0:64, cb * 32:(cb + 1) * 32, j, :]
                    i_ = banks[bk][0:64, :].rearrange("p (g h) -> p g h", h=16)
                    if j % 2 == 0:
                        ACT_(o_, i_, AF.Copy, [bres[bk]], [BIG.u(cb * 4096, (cb + 1) * 4096)])
                    else:
                        P.op("dve", lambda e, o_=o_, i_=i_: e.tensor_copy(out=o_, in_=i_), [bres[bk]],
                             [BIG.u(cb * 4096, (cb + 1) * 4096)])
            for g8 in range(8):
                h_ = pt_next()
                for gl in range(8):
                    g = g8 * 8 + gl
                    TR(PTB[h_][:, gl * 64:(gl + 1) * 64], BIG.ap.rearrange("p a b -> p (a b)")[0:64, g * 128:(g + 1) * 128], id64,
                       [BIG.u(g * 128, (g + 1) * 128), IDB.ua], [PTR[h_]])
                P.op("dve", lambda e, h_=h_, g8=g8: e.tensor_copy(out=RA.ap[:, g8 * 8:(g8 + 1) * 8, :],
                                                                 in_=PTB[h_].rearrange("p (a b) -> p a b", a=8)),
                     [PTR[h_]], [RA.us(g8, 8)])

        def rot_small(o_re, o_im, c_, s_, i_re, i_im, rd, wr):
            t = [SML.ap[:, i, :] for i in range(4)]
            TT_("dve", t[0], c_, i_re, ALU.mult, rd, [SML.ua])
            TT_("dve", t[1], s_, i_im, ALU.mult, rd, [SML.ua])
            TT_("dve", t[2], c_, i_im, ALU.mult, rd, [SML.ua])
            TT_("dve", t[3], s_, i_re, ALU.mult, rd, [SML.ua])
            TT_("dve", o_re, t[0], t[1], ALU.subtract, [SML.ua], wr)
            TT_("dve", o_im, t[2], t[3], ALU.add, [SML.ua], wr)

        def rot32(o_re, o_im, c_, s_, i_re, i_im, rd, wr, eng):
            t = [ROTT.ap[:, i, :] for i in range(4)]
            TT_(eng, t[0], c_, i_re, ALU.mult, rd, [ROTT.ua])
            TT_(eng, t[1], s_, i_im, ALU.mult, rd, [ROTT.ua])
            TT_(eng, t[2], c_, i_im, ALU.mult, rd, [ROTT.ua])
            TT_(eng, t[3], s_, i_re, ALU.mult, rd, [ROTT.ua])
            TT_(eng, o_re, t[0], t[1], ALU.subtract, [ROTT.ua], wr)
            TT_(eng, o_im, t[2], t[3], ALU.add, [ROTT.ua], wr)

        def s5_states(l, k, dirs, pass2, fillers=()):
            Zre, Zim, Wre, Wim, T1, T2 = ZW
            T3, T4 = ZT34
            fillers = list(fillers)
            nslots = 4 * len(dirs)
            slot = [0]
            nfill = len(fillers)
            for d in dirs:
                if d == 0:
                    cin_re, cin_im, cin_u = CARF.ap[:, 0, :], CARF.ap[:, 1, :], CARF.ua
                else:
                    cin_re, cin_im, cin_u = SBIN.ap[:, k, 0, :], SBIN.ap[:, k, 1, :], SBIN.ua
                if pass2:
                    sb_ = SBF if d == 0 else SBB
                    col = 0 if d == 0 else 64
                    P.op("pool", lambda e, sb_=sb_, col=col, cin_re=cin_re: e.tensor_copy(out=sb_.ap[:, :, 0, col], in_=cin_re),
                         [cin_u], [sb_.ua])
                    P.op("pool", lambda e, sb_=sb_, col=col, cin_im=cin_im: e.tensor_copy(out=sb_.ap[:, :, 1, col], in_=cin_im),
                         [cin_u], [sb_.ua])
                rot32(WINIT.ap[:, d, 0, :], WINIT.ap[:, d, 1, :], ROT.ap[:, l, d, 0, :], ROT.ap[:, l, d, 1, :], cin_re, cin_im,
                      [cin_u, ROT.ua], [WINIT.ua], "dve")
                for part in range(2):
                    TT_("dve", RW.ap[:, d, part, :], WINIT.ap[:, d, part, :], RTAB.ap[:, l, d, :], ALU.mult, [WINIT.ua, RTAB.ua], [RW.ua])
            for c in range(4):
                rbm = ring_load(s5b[l, 2 + c], rw_s5b[l][2 + c])
                rbt = ring_load(s5t[l, c], rw_s5t[l][c], as_f32=True)
                m2v = rbm.ap.bitcast(BF16).rearrange("p (g d t s) -> p g d t s", g=16, d=2, t=2)
                tv = rbt.ap.rearrange("p (a b q) -> p a b q", a=4, b=8)
                prs = slice(c * 8, (c + 1) * 8)
                for d in dirs:
                    for part in range(2):
                        for pl in range(8):
                            for par in range(2):
                                gl = pl * 2 + par
                                g = c * 16 + gl
                                MM(banks[4 + part][par * 64:(par + 1) * 64, pl * 64:(pl + 1) * 64], m2v[:, gl, d, part, :],
                                   RA.ap[:, g, :], True, True, [rbm.ua, RA.us(g // 8, 8)], [bres[4 + part]])
                    xre = banks[4][:].rearrange("p (a q) -> p a q", a=8)
                    xim = banks[5][:].rearrange("p (a q) -> p a q", a=8)
                    rz = RZ[0]
                    TT_("pool", rz.ap, MASK0.ap, RTAB.ap[:, l, d, prs].rearrange("p (a o) -> p a o", o=1).to_broadcast([128, 8, 64]),
                        ALU.mult, [MASK0.ua, RTAB.ua], [rz.ua])
                    if d == 0:
                        ct = tv[:, 0]
                        st = tv[:, 1]
                        dct, dst_ = ct, st
                        xre_d, xim_d = xre, xim
                    else:
                        ct = tv[:, 2, :, ::-1]
                        st = tv[:, 3, :, ::-1]
                        dct, dst_ = tv[:, 2], tv[:, 3]
                        xre_d, xim_d = xre[:, :, ::-1], xim[:, :, ::-1]
                    TT_("dve", T1.ap, xre_d, dct, ALU.mult, [bres[4], rbt.ua], [T1.ua])
                    TT_("dve", T2.ap, xim_d, dst_, ALU.mult, [bres[5], rbt.ua], [T2.ua])
                    TT_("dve", T3.ap, xim_d, dct, ALU.mult, [bres[5], rbt.ua], [T3.ua])
                    TT_("dve", T4.ap, xre_d, dst_, ALU.mult, [bres[4], rbt.ua], [T4.ua])
                    TT_("dve", Zre.ap, T1.ap, T2.ap, ALU.add, [T1.ua, T2.ua], [Zre.ua])
                    TT_("pool", Zim.ap, T3.ap, T4.ap, ALU.subtract, [T3.ua, T4.ua], [Zim.ua])
                    for part, (Zp, Wp) in enumerate(((Zre, Wre), (Zim, Wim))):
                        TT_("dve", Zp.ap[:, :, 0], Zp.ap[:, :, 0], RW.ap[:, d, part, prs], ALU.add, [Zp.ua, RW.ua], [Zp.ua])
                        P.op("dve", lambda e, Wp=Wp, Zp=Zp, rz=rz: e.tensor_tensor_scan(
                            out=Wp.ap.rearrange("p a q -> p (a q)"), data0=rz.ap.rearrange("p a q -> p (a q)"),
                            data1=Zp.ap.rearrange("p a q -> p (a q)"), initial=0.0, op0=ALU.mult, op1=ALU.add),
                            [Zp.ua, rz.ua], [Wp.ua])
                        P.op("pool", lambda e, Wp=Wp, part=part, d=d, prs=prs: e.tensor_copy(out=WLAST.ap[:, d, part, prs],
                                                                                          in_=Wp.ap[:, :, 63]),
                             [Wp.ua], [WLAST.ua])
                    if pass2:
                        if d == 0:
                            wr_, wi_ = Wre.ap, Wim.ap
                            o_re = SBF.ap[:, prs, 0, 1:65]
                            o_im = SBF.ap[:, prs, 1, 1:65]
                            sbu = SBF.ua
                        else:
                            wr_, wi_ = Wre.ap[:, :, ::-1], Wim.ap[:, :, ::-1]
                            o_re = SBB.ap[:, prs, 0, 0:64]
                            o_im = SBB.ap[:, prs, 1, 0:64]
                            sbu = SBB.ua
                        TT_("dve", T1.ap, wr_, ct, ALU.mult, [Wre.ua, rbt.ua], [T1.ua])
                        TT_("pool", T2.ap, wi_, st, ALU.mult, [Wim.ua, rbt.ua], [T2.ua])
                        TT_("dve", T3.ap, wi_, ct, ALU.mult, [Wim.ua, rbt.ua], [T3.ua])
                        TT_("pool", T4.ap, wr_, st, ALU.mult, [Wre.ua, rbt.ua], [T4.ua])
                        TT_("dve", o_re, T1.ap, T2.ap, ALU.subtract, [T1.ua, T2.ua], [sbu])
                        TT_("dve", o_im, T3.ap, T4.ap, ALU.add, [T3.ua, T4.ua], [sbu])
                    slot[0] += 1
                    tgt = (slot[0] * nfill) // nslots
                    while nfill - len(fillers) < tgt:
                        fillers.pop(0)()
            while fillers:
                fillers.pop(0)()
            for d in dirs:
                if d == 0:
                    rot32(CARF.ap[:, 0, :], CARF.ap[:, 1, :], ROT.ap[:, l, d, 2, :], ROT.ap[:, l, d, 3, :],
                          WLAST.ap[:, d, 0, :], WLAST.ap[:, d, 1, :], [WLAST.ua, ROT.ua], [CARF.ua], "pool")
                elif not pass2 and k >= 1:
                    rot32(SBIN.ap[:, k - 1, 0, :], SBIN.ap[:, k - 1, 1, :], ROT.ap[:, l, d, 2, :], ROT.ap[:, l, d, 3, :],
                          WLAST.ap[:, d, 0, :], WLAST.ap[:, d, 1, :], [WLAST.ua, ROT.ua], [SBIN.ua], "pool")

        def s5_out(l):
            m1rb = None
            m3rb = None

            def back_transposes(kt):
                h_ = pt_next()
                for i in range(8):
                    TR(PTB[h_][:, i * 64:(i + 1) * 64], BIG.ap[0:64, i, kt * 128:(kt + 1) * 128], id64, [BIG.ua, IDB.ua], [PTR[h_]])
                P.op("dve", lambda e, h_=h_, kt=kt: e.tensor_copy(out=RB.ap[:, kt, :].rearrange("p (b i) -> p i b", i=8),
                                                                 in_=PTB[h_].rearrange("p (i b) -> p i b", i=8)),
                     [PTR[h_]], [RB.us(kt, 8)])
            pend = None
            for kt in range(8):
                if kt % 4 == 0:
                    m1rb = ring_load(s5b[l, kt // 4], rw_s5b[l][kt // 4])
                if kt % 2 == 0:
                    m3rb = ring_load(s5b[l, 6 + kt // 2], rw_s5b[l][6 + kt // 2])
                m1v = m1rb.ap.bitcast(BF16).rearrange("p (g n) -> p g n", g=32)
                m3v = m3rb.ap.bitcast(BF16).rearrange("p (a d t n) -> p a d t n", a=8, d=2, t=2)
                ybk = (4, 5) if kt % 2 == 0 else (2, 3)
                for gl in range(8):
                    g = kt * 8 + gl
                    pair, par = g // 2, g % 2
                    rows = slice(par * 64, par * 64 + 64)
                    bk = ybk[par]
                    m_ = gl // 2
                    outp = banks[bk][0:64, m_ * 128:(m_ + 1) * 128]
                    pin = pair % 8
                    MM(outp, RA.ap[:, g, :], m1v[:, g % 32, :], True, False, [RA.us(g // 8, 8), m1rb.ua], [bres[bk]])
                    MM(outp, SBF.ap[rows, pair, 0, 0:64], m3v[rows, pin, 0, 0, :], False, False, [SBF.ua, m3rb.ua], [bres[bk]])
                    MM(outp, SBF.ap[rows, pair, 1, 0:64], m3v[rows, pin, 0, 1, :], False, False, [SBF.ua, m3rb.ua], [bres[bk]])
                    MM(outp, SBB.ap[rows, pair, 0, 1:65], m3v[rows, pin, 1, 0, :], False, False, [SBB.ua, m3rb.ua], [bres[bk]])
                    MM(outp, SBB.ap[rows, pair, 1, 1:65], m3v[rows, pin, 1, 1, :], False, True, [SBB.ua, m3rb.ua], [bres[bk]])
                for par in range(2):
                    bk = ybk[par]
                    in_ = banks[bk][0:64, :].rearrange("p (m i h) -> p m i h", m=4, i=8)
                    o_ = BIG.ap[0:64, :, kt * 128:(kt + 1) * 128].rearrange("p i (m r h) -> p m r i h", m=4, r=2)[:, :, par, :, :]
                    ACT_(o_, in_, AF.Gelu_apprx_tanh, [bres[bk]], [BIG.u(kt * 128, 7 * 1024 + (kt + 1) * 128)])
                if pend is not None:
                    back_transposes(pend)
                pend = kt
            back_transposes(pend)

        def tm_project(l, blocks, lhs, ktn):
            for cb in range(2):
                bks = [pa_list[i] for i in range(4)]
                nkc = len(blocks[cb])
                for kc, bid in enumerate(blocks[cb]):
                    rb = ring_load(wb[l, bid], rw_wb[l][bid])
                    wv = wview(rb)
                    for s_ in range(4):
                        for kt in range(8):
                            MM(banks[bks[s_]][:], lhs.ap[:, kc * 8 + kt, s_ * 128:(s_ + 1) * 128], wv[:, kt, :],
                               kc == 0 and kt == 0, kc == nkc - 1 and kt == 7, [rb.ua, lhs.us(kc * 8 + kt, ktn)], [bres[bks[s_]]])
                for s_ in range(4):
                    ACT_(MIXF.ap[:, s_, cb * 512:(cb + 1) * 512], banks[bks[s_]][:], AF.Copy, [bres[bks[s_]]],
                         [MIXF.u(s_ * 1024 + cb * 512, s_ * 1024 + cb * 512 + 512)])

        def postnorm_add(gidx, X):
            for s_ in range(4):
                ACT_(TMPB.ap, MIXF.ap[:, s_, :], AF.Square, [MIXF.us(s_, 4)], [TMP.ua, STAT.ua], accum=STAT.ap[:, 16 + s_:17 + s_])
            rstd_from(16, 4)
            for s_ in range(4):
                STT_(MIXF.ap[:, s_, :], MIXF.ap[:, s_, :], STAT.ap[:, 24 + s_:25 + s_], GBC.ap[:, gidx, :], ALU.mult, ALU.mult,
                     [MIXF.us(s_, 4), STAT.ua, GBC.ua], [MIXF.us(s_, 4)])
                TT_("dve", X.ap[:, s_, :], X.ap[:, s_, :], MIXF.ap[:, s_, :], ALU.add, [X.us(s_, 4), MIXF.us(s_, 4)], [X.us(s_, 4)])

        def load_layer_consts(l):
            DMA(GFM.ap, gfm[l], [], [GFM.ua], q="sp")
            DMA(GBC.ap, gbc[l].rearrange("i p d -> p i d"), [], [GBC.ua], q="sp")
            DMA(BSB.ap.rearrange("p a b -> p (a b)"), bsb[l], [], [BSB.ua], q="sp")
            DMA(RING[0].ap[:, 0:1024], wst[l], [], [RING[0].ua], q="sp")
            P.op("dve", lambda e: e.tensor_copy(out=WST.ap.rearrange("p a b -> p (a b)"), in_=RING[0].ap[:, 0:1024]), [RING[0].ua], [WST.ua])
            P.op("pool", lambda e: e.memset(CARF.ap, 0.0), [], [CARF.ua])
            P.op("pool", lambda e: e.memset(SBIN.ap[:, NT - 1], 0.0), [], [SBIN.ua])

        import os as _os
        TAPS = _os.environ.get("KTAPS", "") == "1"

        def tap(name, b_, l, k):
            if not (TAPS and l == 0 and k == 0):
                return
            dt_ = F32 if b_.es == 4 else BF16
            o_ = nc.dram_tensor("tap_" + name, [128, b_.n], dt_, kind="ExternalOutput").ap()
            flat = b_.ap
            if len(b_.shape) == 2:
                flat = flat.rearrange("p a b -> p (a b)")
            elif len(b_.shape) == 3:
                flat = flat.rearrange("p a b c -> p (a b c)")
            P.dma("sp", o_, flat, [b_.ua], [Res()])

        xstate = {"i": 0, "pending": None}
        r_uc = [Res() for _ in range(NT)]

        def x_fetch(src_ap, rsrc, k, key):
            if xstate["pending"] == key:
                xstate["i"] += 1
                xstate["pending"] = None
                return XB[xstate["i"] % 2]
            Xb = XB[xstate["i"] % 2]
            P.dma("sp", Xb.ap, src_ap[k * TT:(k + 1) * TT, :].rearrange("(s p) d -> p s d", p=128), [rsrc[k]], [Xb.ua])
            return Xb

        def x_prefetch(src_ap, rsrc, k, key):
            Xn = XB[(xstate["i"] + 1) % 2]
            P.dma("sp", Xn.ap, src_ap[k * TT:(k + 1) * TT, :].rearrange("(s p) d -> p s d", p=128), [rsrc[k]], [Xn.ua])
            xstate["pending"] = key

        def branch_a_items(l, k):
            items = []
            for b in range(4):
                def it(b=b):
                    rb = ring_load(wb[l, 6 + b], rw_wb[l][6 + b])

                    def ev(m, bk):
                        mm = (b % 2) * 4 + m
                        ACT_(SG.ap[:, b // 2, mm, :], banks[bk][:], AF.Sigmoid, [bres[bk]], [SG.us((b // 2) * 8 + mm, 16)])
                    dense_fm(H, rb, None, ev)
                items.append(it)
            for b in (0, 1):
                def it(b=b):
                    rb = ring_load(wb[l, b], rw_wb[l][b])

                    def ev(m, bk):
                        mm = b * 4 + m
                        ACT_(RC.ap[:, mm, :], banks[bk][:], AF.Gelu_apprx_tanh, [bres[bk]], [RC.us(mm, 8)])
                    dense_fm(H, rb, None, ev)
                items.append(it)
            for cb in range(2):
                def it(cb=cb):
                    rb = ring_load(wb[l, 2 + cb], rw_wb[l][2 + cb])
                    wv = wview(rb)
                    for s_ in range(4):
                        bk = pa_next()
                        for kt in range(8):
                            MM(banks[bk][:], H.ap[:, kt, s_ * 128:(s_ + 1) * 128], wv[:, kt, :], kt == 0, kt == 7,
                               [rb.ua, H.us(kt, 8)], [bres[bk]])
                        ACT_(VN.ap[:, s_, cb * 512:(cb + 1) * 512], banks[bk][:], AF.Gelu_apprx_tanh, [bres[bk]],
                             [VN.u(s_ * 1024 + cb * 512, s_ * 1024 + cb * 512 + 512)])
                items.append(it)

            def vnorm():
                for s_ in range(4):
                    ACT_(TMPB.ap, VN.ap[:, s_, :], AF.Square, [VN.us(s_, 4)], [TMP.ua, STAT.ua], accum=STAT.ap[:, 32 + s_:33 + s_])
                rstd_from(32, 4)
                for s_ in range(4):
                    P.op("act", lambda e, s_=s_: e.activation(out=VN.ap[:, s_, :], in_=VN.ap[:, s_, :], func=AF.Copy,
                                                              scale=STAT.ap[:, 40 + s_:41 + s_]),
                         [VN.us(s_, 4), STAT.ua], [VN.us(s_, 4)])
            items.append(vnorm)
            for half in range(2):
                def it(half=half):
                    for g in range(half * 4, half * 4 + 4):
                        bk = pa_next()
                        for s_ in range(4):
                            MM(banks[bk][:, s_ * 128:(s_ + 1) * 128], VN.ap[:, s_, g * 128:(g + 1) * 128], WST.ap[:, g, :], True, True,
                               [VN.us(s_, 4), WST.ua], [bres[bk]])
                        STT_(RD.ap[:, g, :].rearrange("p (s q) -> p s q", s=4), banks[bk][:].rearrange("p (s q) -> p s q", s=4),
                             GFM.ap[:, 2, g:g + 1], BSB.ap[:, g, :].rearrange("p (o q) -> p o q", o=1).to_broadcast([128, 4, 128]),
                             ALU.mult, ALU.add, [bres[bk], GFM.ua, BSB.ua], [RD.us(g, 8)])
                        TT_("pool", RC.ap[:, g, :], RC.ap[:, g, :], RD.ap[:, g, :], ALU.mult, [RC.us(g, 8), RD.us(g, 8)], [RC.us(g, 8)])
                items.append(it)
            for b in (0, 1):
                def it(b=b):
                    rb = ring_load(wb[l, 10 + b], rw_wb[l][10 + b])

                    def ev(m, bk):
                        mm = b * 4 + m
                        TT_("dve", SG.ap[:, 0, mm, :], banks[bk][:], SG.ap[:, 0, mm, :], ALU.mult, [bres[bk], SG.us(mm, 16)],
                            [SG.us(mm, 16)])
                    dense_fm(RC, rb, None, ev)
                items.append(it)
            return items

        def layer(l, src_ap, rsrc, dst_ap, rdst):
            load_layer_consts(l)
            for k in range(NT - 1, 0, -1):
                P.cur_tag = f"L{l}P1T{k}:prenorm"
                X = x_fetch(src_ap, rsrc, k, (l, k))
                x_prefetch(src_ap, rsrc, k - 1, (l, k - 1))
                prenorm(0, X)
                P.cur_tag = f"L{l}P1T{k}:ub"
                ub_and_transposes(l)
                P.dma("sp", ucache[k], RA.ap.rearrange("p a b -> p (a b)"), [RA.ua], [r_uc[k]])
                P.cur_tag = f"L{l}P1T{k}:s5st"
                s5_states(l, k, [1], False)
            for k in range(NT):
                P.cur_tag = f"L{l}P2T{k}:prenorm"
                X = x_fetch(src_ap, rsrc, k, (l, k))
                if k + 1 < NT:
                    x_prefetch(src_ap, rsrc, k + 1, (l, k + 1))
                prenorm(0, X)
                P.cur_tag = f"L{l}P2T{k}:ub"
                if k == 0:
                    ub_and_transposes(l)
                else:
                    P.dma("sp", RA.ap.rearrange("p a b -> p (a b)"), ucache[k], [r_uc[k]], [RA.ua])
                tap("H", H, l, k)
                tap("RA", RA, l, k)
                P.cur_tag = f"L{l}P2T{k}:s5st"
                s5_states(l, k, [0, 1], True, fillers=branch_a_items(l, k))
                tap("SBF", SBF, l, k)
                tap("SBB", SBB, l, k)
                tap("AIN", RC, l, k)
                P.cur_tag = f"L{l}P2T{k}:s5out"
                s5_out(l)
                tap("RB", RB, l, k)
                P.cur_tag = f"L{l}P2T{k}:glu"
                for b in (2, 3):
                    rb = ring_load(wb[l, 12 + b], rw_wb[l][12 + b])

                    def ev(m, bk, b=b):
                        mm = (b - 2) * 4 + m
                        ACT_(RD.ap[:, mm, :], banks[bk][:], AF.Sigmoid, [bres[bk]], [RD.us(mm, 8)])
                        TT_("pool", RD.ap[:, mm, :], RD.ap[:, mm, :], SG.ap[:, 1, mm, :], ALU.mult, [RD.us(mm, 8), SG.us(8 + mm, 16)],
                            [RD.us(mm, 8)])
                    dense_fm(RB, rb, None, ev)
                for b in (0, 1):
                    rb = ring_load(wb[l, 12 + b], rw_wb[l][12 + b])

                    def ev(m, bk, b=b):
                        mm = b * 4 + m
                        TT_("dve", SG.ap[:, 1, mm, :], banks[bk][:], RD.ap[:, mm, :], ALU.mult, [bres[bk], RD.us(mm, 8)],
                            [SG.us(8 + mm, 16)])
                        TT_("pool", SG.ap[:, 0, mm, :], SG.ap[:, 0, mm, :], SG.ap[:, 1, mm, :], ALU.add,
                            [SG.us(mm, 16), SG.us(8 + mm, 16)], [SG.us(mm, 16)])
                    dense_fm(RB, rb, None, ev)
                P.cur_tag = f"L{l}P2T{k}:wo"
                SG0 = Buf(AR, SG.off, BF16, (8, 512))
                tap("MIXIN", SG0, l, k)
                tm_project(l, [[16], [17]], SG0, 8)
                tap("MIXF", MIXF, l, k)
                postnorm_add(0, X)
                tap("X1", X, l, k)
                P.cur_tag = f"L{l}P2T{k}:prenorm2"
                prenorm(1, X)
                P.cur_tag = f"L{l}P2T{k}:ff1"
                for b in range(8):
                    rb = ring_load(wb[l, 18 + b], rw_wb[l][18 + b])

                    def ev(m, bk, b=b):
                        mm = b * 4 + m
                        ACT_(RD.ap[:, mm % 8, :], banks[bk][:], AF.Square, [bres[bk]], [RD.us(mm % 8, 8)])
                        STT_(HID.ap[:, mm, :], banks[bk][:], 0.0, RD.ap[:, mm % 8, :], ALU.is_gt, ALU.mult,
                             [bres[bk], RD.us(mm % 8, 8)], [HID.us(mm, 32)])
                    dense_fm(H, rb, None, ev)
                P.cur_tag = f"L{l}P2T{k}:ff2"
                tap("HID", HID, l, k)
                tm_project(l, [[26, 27, 28, 29], [30, 31, 32, 33]], HID, 32)
                tap("FF", MIXF, l, k)
                postnorm_add(1, X)
                tap("X2", X, l, k)
                P.dma("sp", dst_ap[k * TT:(k + 1) * TT, :].rearrange("(s p) d -> p s d", p=128), X.ap, [X.ua], [rdst[k]])

        r_xin = [Res() for _ in range(NT)]
        layer(0, x_in, r_xin, x_mid, r_xmid)
        layer(1, x_mid, r_xmid, y_out, r_yout)
        print("arena top", AR.top, "of", AR.nbytes)
        print("ops:", P.emit())
    return nc


def host_layouts(p):
    f = np.float32
    out = {}
    wf = np.zeros((L, NWB, 128, 4096), f)

    def blk(w, kc, c0):
        return w[kc * 1024:(kc + 1) * 1024, c0:c0 + 512].reshape(8, 128, 512).transpose(1, 0, 2).reshape(128, 4096)
    for l in range(L):
        for b in range(10):
            wf[l, b] = blk(p["w_in"][l], 0, b * 512)
        for b in range(2):
            wf[l, 10 + b] = blk(p["w_out_a"][l], 0, b * 512)
        for b in range(4):
            wf[l, 12 + b] = blk(p["w_glu"][l], 0, b * 512)
        for b in range(2):
            wf[l, 16 + b] = blk(p["w_o"][l], 0, b * 512)
        for b in range(8):
            wf[l, 18 + b] = blk(p["w_ff1"][l], 0, b * 512)
        for cb in range(2):
            for kc in range(4):
                wf[l, 26 + cb * 4 + kc] = blk(p["w_ff2"][l], kc, cb * 512)
    out["wf"] = wf
    hh = np.arange(128) % 16
    jj = np.arange(128) // 16
    a_lam = np.zeros((L, 2, 128, 2, 4096), f)
    a_b = np.zeros((L, 2, 128, 2, 4096), f)
    a_dt = np.zeros((L, 2, 128, 64), f)
    for l in range(L):
        for d in range(2):
            a_lam[l, d, :, 0] = p["lam_re"][l, d].reshape(1, 4096)
            a_lam[l, d, :, 1] = p["lam_im"][l, d].reshape(1, 4096)
            a_b[l, d, :, 0] = p["b_re"][l, d][:, :, hh].transpose(2, 0, 1).reshape(128, 4096)
            a_b[l, d, :, 1] = p["b_im"][l, d][:, :, hh].transpose(2, 0, 1).reshape(128, 4096)
            a_dt[l, d] = p["log_dt"][l, d][None, :]
    out["a_lam"], out["a_b"], out["a_dt"] = a_lam, a_b, a_dt
    a_pw = np.zeros((128, 2), f)
    a_pw[:, 0] = 7 - jj
    a_pw[:, 1] = jj
    out["a_pw"] = a_pw
    a_dsk = np.zeros((L, 128, 64), f)
    for l in range(L):
        a_dsk[l] = p["d_skip"][l].reshape(64, 16)[:, hh].T
    out["a_dsk"] = a_dsk
    par = np.arange(128) // 64
    ss = np.arange(128) % 64
    b_lam = np.zeros((L, 2, 128, 2, 32), f)
    b_dt = np.zeros((L, 2, 128, 32), f)
    b_c = np.zeros((L, 2, 128, 2, 512), f)
    b_b = np.zeros((L, 2, 128, 2, 512), f)
    for l in range(L):
        for d in range(2):
            for pp in range(2):
                rows = slice(pp * 64, pp * 64 + 64)
                b_lam[l, d, rows, 0] = p["lam_re"][l, d][pp::2].T
                b_lam[l, d, rows, 1] = p["lam_im"][l, d][pp::2].T
                b_dt[l, d, rows] = p["log_dt"][l, d][pp::2][None, :]
                b_c[l, d, rows, 0] = p["c_re"][l, d][pp::2].transpose(2, 0, 1).reshape(64, 512)
                b_c[l, d, rows, 1] = p["c_im"][l, d][pp::2].transpose(2, 0, 1).reshape(64, 512)
                b_b[l, d, rows, 0] = p["b_re"][l, d][pp::2].transpose(1, 0, 2).reshape(64, 512)
                b_b[l, d, rows, 1] = p["b_im"][l, d][pp::2].transpose(1, 0, 2).reshape(64, 512)
    out["b_lam"], out["b_dt"], out["b_c"], out["b_b"] = b_lam, b_dt, b_c, b_b
    cst = np.zeros((128, 480), f)
    i8 = np.arange(8)
    cst[:, 0:8] = i8 + 1
    cst[:, 8:16] = 8 - i8
    cst[:, 16:24] = -(i8 + 1)
    cst[:, 24:32] = i8 - 8
    cst[:, 32:96] = np.arange(64)
    ji = np.arange(128) // 16
    cst[:, 96:224] = (ji[None, :] >= ji[:, None])
    cst[:, 224:352] = (ji[None, :] <= ji[:, None])
    cst[:, 352:480] = np.eye(128)
    out["cst"] = cst
    gfm = np.zeros((L, 128, 3, 8), f)
    gbc = np.zeros((L, 2, 128, 1024), f)
    bsb = np.zeros((L, 128, 1024), f)
    wst = np.zeros((L, 128, 1024), f)
    for l in range(L):
        gfm[l, :, 0] = p["norm_pre_mix"][l].reshape(8, 128).T
        gfm[l, :, 1] = p["norm_pre_ff"][l].reshape(8, 128).T
        gfm[l, :, 2] = p["norm_v"][l].reshape(8, 128).T
        gbc[l, 0] = p["norm_post_mix"][l][None, :]
        gbc[l, 1] = p["norm_post_ff"][l][None, :]
        bsb[l] = p["b_s"][l].reshape(1, 1024)
        wst[l] = p["w_s"][l].transpose(2, 0, 1).reshape(128, 1024)
    out["gfm"], out["gbc"], out["bsb"], out["wst"] = gfm, gbc, bsb, wst
    return out


_NC_CACHE = {}


def kernel(**inputs):
    p = {k: np.asarray(v, dtype=np.float32) for k, v in inputs.items()}
    xs = [p["x_prompt"][i] for i in range(2)] + [p["x_sample"][i] for i in range(4)]
    lay = host_layouts(p)
    if "nc" not in _NC_CACHE:
        _NC_CACHE["nc"] = build()
    nc = _NC_CACHE["nc"]
    seq_core = [0, 1, 2, 4, 5, 6]
    zero_x = np.zeros_like(xs[0])
    in_maps = []
    for c in range(NCORES):
        m = dict(lay)
        m["x"] = np.ascontiguousarray(xs[seq_core.index(c)]) if c in seq_core else zero_x
        in_maps.append(m)
    res = run_bass_kernel_spmd(nc, in_maps, core_ids=list(range(NCORES)))
    ys = [res.results[c]["y"] for c in seq_core]
    y_prompt = np.stack(ys[0:2]).astype(np.float32)
    y_sample = np.stack(ys[2:6]).astype(np.float32)
    return (y_prompt, y_sample)
```

```python
import contextlib
import math
import os
import numpy as np
import concourse.bass as bass
import concourse.mybir as mybir
from concourse.bass_utils import run_bass_kernel_spmd

F32 = mybir.dt.float32
BF16 = mybir.dt.bfloat16
AF = mybir.ActivationFunctionType
ALU = mybir.AluOpType

D = 1024
S = 8192
TT = 512
NT = S // TT
NB = 64
L = 2
NWB = 34
EPS = 1e-6
NCORES = 8
MAGIC = 12582912.0
C1 = 6.28125
C2 = 2.0 * math.pi - 6.28125
SINSCALE = 0.999996


class Res:
    __slots__ = ("last_w", "readers")

    def __init__(self):
        self.last_w = None
        self.readers = []


class Op:
    __slots__ = ("eng", "fn", "deps", "needed", "sig", "dma", "dbg", "tag")

    def __init__(self, eng, fn, dma):
        self.eng = eng
        self.fn = fn
        self.deps = []
        self.needed = False
        self.sig = None
        self.dma = dma


EPOCH = 16000
NDMA = 8


def _flat(x, out):
    for r in x:
        if isinstance(r, Res):
            out.append(r)
        else:
            _flat(r, out)
    return out


class Prog:
    ENG = ("pe", "act", "dve", "pool", "sp")

    def __init__(self, nc):
        self.nc = nc
        self.ops = []

    def op(self, eng, fn, reads=(), writes=(), dma=False):
        o = Op(eng, fn, dma)
        import sys as _s
        fr = _s._getframe(1)
        lines = []
        while fr is not None and len(lines) < 4:
            lines.append(fr.f_lineno)
            fr = fr.f_back
        o.dbg = lines
        o.tag = getattr(self, "cur_tag", "")
        reads = _flat(reads, [])
        writes = _flat(writes, [])
        deps = {}
        for r in reads:
            if r.last_w is not None:
                deps[id(r.last_w)] = r.last_w
        for r in writes:
            if r.last_w is not None:
                deps[id(r.last_w)] = r.last_w
            for q in r.readers:
                deps[id(q)] = q
        for r in reads:
            if not dma:
                r.readers = [q for q in r.readers if q.dma or q.eng != eng]
            r.readers.append(o)
        for r in writes:
            r.last_w = o
            r.readers = []
        deps.pop(id(o), None)
        for d in deps.values():
            if d.eng == "pe" and eng == "pe" and not d.dma and not dma:
                continue
            o.deps.append(d)
            d.needed = True
        self.ops.append(o)
        return o

    def dma(self, q, out, in_, reads=(), writes=()):
        return self.op(q, lambda e: e.dma_start(out=out, in_=in_), reads, writes, dma=True)

    def emit(self):
        nc = self.nc
        import os
        lim = int(os.environ.get("KLIMIT", "0"))
        if lim:
            self.ops = self.ops[:lim]
        cnt = {e: 0 for e in self.ENG}
        dcnt = {e: 0 for e in self.ENG}
        for o in self.ops:
            if o.dma:
                k = dcnt[o.eng]
                dcnt[o.eng] += 1
                o.sig = ("d", o.eng, k % NDMA, (k // NDMA + 1) * 16)
            elif o.needed:
                k = cnt[o.eng]
                cnt[o.eng] += 1
                o.sig = ("c", o.eng, k // EPOCH, k % EPOCH + 1)
        sems = {}
        namemap = {} if os.environ.get("KMAP") else None
        self.namemap = namemap
        with contextlib.ExitStack() as stack:
            for e in self.ENG:
                for ep in range((cnt[e] + EPOCH - 1) // EPOCH):
                    sems[("c", e, ep)] = stack.enter_context(nc.semaphore(f"c_{e}_{ep}"))
                for j in range(min(NDMA, dcnt[e])):
                    sems[("d", e, j)] = stack.enter_context(nc.semaphore(f"d_{e}_{j}"))
            block = stack.enter_context(nc.Block())
            per_eng = {e: [o for o in self.ops if o.eng == e] for e in self.ENG}

            def body(ename):
                def f(eng):
                    waited = {}

                    def wait(key, val):
                        if waited.get(key, 0) >= val:
                            return
                        eng.wait_ge(sems[key], val)
                        waited[key] = val
                    for o in per_eng[ename]:
                        for d in o.deps:
                            s = d.sig
                            wait((s[0], s[1], s[2]), s[3])
                        if o.dma and o.sig[3] > 16:
                            s = o.sig
                            wait((s[0], s[1], s[2]), s[3] - 16)
                        ins = o.fn(eng)
                        if namemap is not None:
                            namemap[ins.ins.name] = o.tag
                        if o.sig is not None:
                            s = o.sig
                            ins.then_inc(sems[(s[0], s[1], s[2])], 16 if o.dma else 1)
                    k = dcnt[ename]
                    for j in range(min(NDMA, k)):
                        n = (k - j + NDMA - 1) // NDMA
                        wait(("d", ename, j), n * 16)
                return f
            for e, meth in (("sp", block.sync), ("act", block.scalar), ("dve", block.vector),
                            ("pool", block.gpsimd), ("pe", block.tensor)):
                if per_eng[e]:
                    meth(body(e))
        if namemap is not None:
            import json as _json
            _json.dump(namemap, open(os.environ["KMAP"], "w"))
        return {e: len(v) for e, v in per_eng.items()}


UNIT = 512


class Arena:
    def __init__(self, nc, stack, nbytes):
        self.t = stack.enter_context(nc.sbuf_tensor("arena", [128, nbytes // 4], F32))
        self.units = [Res() for _ in range((nbytes + UNIT - 1) // UNIT)]
        self.top = 0
        self.nbytes = nbytes

    def alloc(self, nbytes):
        off = self.top
        self.top += (nbytes + UNIT - 1) // UNIT * UNIT
        assert self.top <= self.nbytes, (self.top, self.nbytes)
        return off


class Buf:
    def __init__(self, arena, off, dtype, shape):
        self.arena = arena
        self.off = off
        self.es = 4 if dtype == F32 else 2
        n = 1
        for s in shape:
            n *= s
        self.n = n
        ap = arena.t[:, off // 4:(off + n * self.es + 3) // 4]
        if dtype != F32:
            ap = ap.bitcast(dtype)
        if len(shape) == 2:
            ap = ap.rearrange("p (a b) -> p a b", a=shape[0])
        elif len(shape) == 3:
            ap = ap.rearrange("p (a b c) -> p a b c", a=shape[0], b=shape[1])
        elif len(shape) == 4:
            ap = ap.rearrange("p (a b c d) -> p a b c d", a=shape[0], b=shape[1], c=shape[2])
        self.ap = ap
        self.shape = shape
        self.ua = self.u(0, n)

    def u(self, lo, hi):
        b0 = (self.off + lo * self.es) // UNIT
        b1 = (self.off + hi * self.es + UNIT - 1) // UNIT
        return self.arena.units[b0:b1]

    def us(self, i, n_i):
        w = self.n // n_i
        return self.u(i * w, (i + 1) * w)


def build(dbg=False):
    nc = bass.Bass("TRN2", target_bir_lowering=False)

    def din(name, shape, dt=F32):
        return nc.dram_tensor(name, list(shape), dt, kind="ExternalInput").ap()

    def dint(name, shape, dt):
        return nc.dram_tensor(name, list(shape), dt, kind="Internal").ap()

    x_in = din("x", [S, D])
    wf = din("wf", [L, NWB, 128, 4096])
    a_lam = din("a_lam", [L, 2, 128, 2, 4096])
    a_b = din("a_b", [L, 2, 128, 2, 4096])
    a_dt = din("a_dt", [L, 2, 128, 64])
    a_pw = din("a_pw", [128, 2])
    a_dsk = din("a_dsk", [L, 128, 64])
    b_lam = din("b_lam", [L, 2, 128, 2, 32])
    b_dt = din("b_dt", [L, 2, 128, 32])
    b_c = din("b_c", [L, 2, 128, 2, 512])
    b_b = din("b_b", [L, 2, 128, 2, 512])
    cst = din("cst", [128, 8 + 8 + 8 + 8 + 64 + 128 + 128 + 128])
    gfm = din("gfm", [L, 128, 3, 8])
    gbc = din("gbc", [L, 2, 128, 1024])
    bsb = din("bsb", [L, 128, 1024])
    wst = din("wst", [L, 128, 1024])
    y_out = nc.dram_tensor("y", [S, D], F32, kind="ExternalOutput").ap()
    wb = dint("wb", [L, NWB, 128, 4096], BF16)
    s5b = dint("s5b", [L, 10, 128, 4096], BF16)
    s5t = dint("s5t", [L, 4, 128, 2048], F32)
    x_mid = dint("x_mid", [S, D], F32)
    ucache = dint("ucache", [NT, 128, 4096], BF16)
    dbgo = {}
    if dbg:
        dbgo["s5b"] = nc.dram_tensor("dbg_s5b", [L, 10, 128, 4096], F32, kind="ExternalOutput").ap()
        dbgo["s5t"] = nc.dram_tensor("dbg_s5t", [L, 4, 128, 2048], F32, kind="ExternalOutput").ap()

    P = Prog(nc)
    with contextlib.ExitStack() as es:
        AR = Arena(nc, es, 206 * 1024)
        banks = [es.enter_context(nc.psum_tensor(f"ps{i}", [128, 512], F32)) for i in range(8)]
        bres = [Res() for _ in range(8)]
        bres_h = [[Res(), Res()] for _ in range(8)]

        def buf(dtype, shape):
            n = 1
            for s_ in shape:
                n *= s_
            off = AR.alloc(n * (4 if dtype == F32 else 2))
            return Buf(AR, off, dtype, shape)

        def alias(b, byte_off, dtype, shape):
            return Buf(AR, b.off + byte_off, dtype, shape)

        TMP = buf(BF16, (1024,))
        TMPB = TMP
        IDB = buf(BF16, (128,))
        CST = buf(F32, (480,))
        GFM = buf(F32, (3, 8))
        GBC = buf(F32, (2, 1024))
        BSB = buf(F32, (8, 128))
        WST = buf(BF16, (8, 128))
        STAT = buf(F32, (64,))
        CARF = buf(F32, (2, 32))
        SBIN = buf(F32, (NT, 2, 32))
        RTAB = buf(F32, (L, 2, 32))
        WINIT = buf(F32, (2, 2, 32))
        SML = buf(F32, (8, 8))
        EPSC = buf(F32, (4,))
        PWA = buf(F32, (2,))
        ROT = buf(F32, (L, 2, 4, 32))
        WLAST = buf(F32, (2, 2, 32))
        ZT34 = [buf(F32, (8, 64)) for _ in range(2)]
        ROTT = buf(F32, (4, 32))
        MASK0 = buf(F32, (8, 64))
        RZ = [buf(F32, (8, 64))]
        RW = buf(F32, (2, 2, 32))
        main_base = AR.top
        XB = [buf(F32, (4, 1024)) for _ in range(2)]
        H = buf(BF16, (8, 512))
        SG = buf(BF16, (2, 8, 512))
        BIG = buf(BF16, (8, 1024))
        MIXF = alias(BIG, 0, F32, (4, 1024))
        RA = buf(BF16, (64, 64))
        VN = None
        RB = buf(BF16, (8, 512))
        VN = alias(RB, 0, BF16, (4, 1024))
        RC = buf(BF16, (8, 512))
        RD = buf(BF16, (8, 512))
        HN = alias(RD, 0, BF16, (4, 1024))
        HID = buf(BF16, (32, 512))
        hid_off = HID.off
        ZW = [Buf(AR, hid_off + i * 2048, F32, (8, 64)) for i in range(6)]
        SBF = Buf(AR, hid_off + 12288, BF16, (32, 2, 65))
        SBB = Buf(AR, hid_off + 12288 + 8320, BF16, (32, 2, 65))
        assert 12288 + 2 * 8320 <= 32768
        NRING = 4
        RING = [buf(F32, (2048,)) for _ in range(NRING)]
        assert AR.top - main_base >= 141 * 1024

        def TT_(eng, out, i0, i1, op, r, w):
            P.op(eng, lambda e: e.tensor_tensor(out=out, in0=i0, in1=i1, op=op), r, w)

        def TS_(eng, out, i0, s1, s2, op0, op1, r, w):
            if op1 is None:
                P.op(eng, lambda e: e.tensor_scalar(out=out, in0=i0, scalar1=s1, scalar2=None, op0=op0), r, w)
            else:
                P.op(eng, lambda e: e.tensor_scalar(out=out, in0=i0, scalar1=s1, scalar2=s2, op0=op0, op1=op1), r, w)

        def STT_(out, i0, sc, i1, op0, op1, r, w, eng="dve"):
            P.op(eng, lambda e: e.scalar_tensor_tensor(out=out, in0=i0, scalar=sc, in1=i1, op0=op0, op1=op1), r, w)

        def ACT_(out, in_, func, r, w, scale=1.0, bias=None, accum=None):
            def f(e):
                kw = {}
                if bias is not None:
                    kw["bias"] = bias
                if accum is not None:
                    kw["accum_out"] = accum
                return e.activation(out=out, in_=in_, func=func, scale=scale, **kw)
            P.op("act", f, r, w)

        def MM(out, lhsT, rhs, start, stop, r, w):
            P.op("pe", lambda e: e.matmul(out, lhsT=lhsT, rhs=rhs, start=start, stop=stop), r, w)

        def TR(out, in_, ident, r, w):
            P.op("pe", lambda e: e.transpose(out, in_, ident), r, w)

        dq = [0]

        def DMA(out, in_, r, w, q=None):
            if q is None:
                q = ("sp", "act")[dq[0] % 2]
                dq[0] += 1
            P.dma(q, out, in_, r, w)

        rw_wb = [[Res() for _ in range(NWB)] for _ in range(L)]
        rw_s5b = [[Res() for _ in range(10)] for _ in range(L)]
        rw_s5t = [[Res() for _ in range(4)] for _ in range(L)]
        r_xmid = [Res() for _ in range(NT)]
        r_yout = [Res() for _ in range(NT)]
        DMA(CST.ap, cst, [], [CST.ua])
        DMA(PWA.ap, a_pw, [], [PWA.ua])
        pw3 = [CST.ap[:, 0:8], CST.ap[:, 8:16]]
        pwL = [CST.ap[:, 16:24], CST.ap[:, 24:32]]
        qvec = CST.ap[:, 32:96]
        maskF = CST.ap[:, 96:224]
        maskB = CST.ap[:, 224:352]
        identF = CST.ap[:, 352:480]
        P.op("dve", lambda e: e.tensor_copy(out=IDB.ap, in_=identF), [CST.ua], [IDB.ua])
        P.op("pool", lambda e: e.memset(EPSC.ap[:, 0:1], EPS), [], [EPSC.ua])
        P.op("pool", lambda e: e.memset(EPSC.ap[:, 1:2], SINSCALE * math.pi / 2), [], [EPSC.ua])
        P.op("pool", lambda e: e.memset(EPSC.ap[:, 2:3], 0.0), [], [EPSC.ua])
        P.op("pool", lambda e: e.memset(MASK0.ap, 1.0), [], [MASK0.ua])
        P.op("pool", lambda e: e.memset(MASK0.ap[:, :, 0:1], 0.0), [], [MASK0.ua])
        eps_ap = EPSC.ap[:, 0:1]
        hpi_ap = EPSC.ap[:, 1:2]
        zero_ap = EPSC.ap[:, 2:3]

        for l in range(L):
            for b in list(range(4, 10)) + list(range(0, 4)) + list(range(10, NWB)):
                P.dma("pool", wb[l, b], wf[l, b], [], [rw_wb[l][b]])

        def setup_layer(l):
            base = main_base
            KB = 1024
            nA = 18
            A_ = [Buf(AR, base + i * 4096, F32, (16, 64)) for i in range(nA)]
            stg = Buf(AR, base + 72 * KB, BF16, (16, 2, 2, 64))
            dts = Buf(AR, base + 80 * KB, F32, (2, 64))

            def mul(o, a, b, eng="dve"):
                TT_(eng, o.ap, a.ap, b.ap, ALU.mult, [a.ua, b.ua], [o.ua])

            def sincos(x, c_out, s_out, t1, t2, shape_ap=lambda b: b.ap, reps=1):
                for (dst, off) in ((s_out, 0.0), (c_out, 0.25)):
                    cur = x
                    for rep in range(reps):
                        TS_("dve", shape_ap(t1), shape_ap(cur), 1.0 / (2 * math.pi), off, ALU.mult, ALU.add, [cur.ua], [t1.ua])
                        TS_("dve", shape_ap(t1), shape_ap(t1), MAGIC, -MAGIC, ALU.add, ALU.add, [t1.ua], [t1.ua])
                        STT_(shape_ap(t2), shape_ap(t1), -C1, shape_ap(cur), ALU.mult, ALU.add, [t1.ua, cur.ua], [t2.ua])
                        STT_(shape_ap(t2), shape_ap(t1), -C2, shape_ap(t2), ALU.mult, ALU.add, [t1.ua, t2.ua], [t2.ua])
                        cur = t2
                    ACT_(shape_ap(dst), shape_ap(t2), AF.Sin, [t2.ua, EPSC.ua], [dst.ua], scale=SINSCALE,
                         bias=(zero_ap if off == 0.0 else hpi_ap))

            def cmul(o_r, o_i, a_r, a_i, b_r, b_i, t1, t2, neg_im=False, ap=lambda b: b.ap):
                TT_("dve", ap(t1), ap(a_r), ap(b_r), ALU.mult, [a_r.ua, b_r.ua], [t1.ua])
                TT_("dve", ap(t2), ap(a_i), ap(b_i), ALU.mult, [a_i.ua, b_i.ua], [t2.ua])
                TT_("dve", ap(o_r), ap(t1), ap(t2), ALU.subtract, [t1.ua, t2.ua], [o_r.ua])
                TT_("dve", ap(t1), ap(a_r), ap(b_i), ALU.mult, [a_r.ua, b_i.ua], [t1.ua])
                TT_("dve", ap(t2), ap(a_i), ap(b_r), ALU.mult, [a_i.ua, b_r.ua], [t2.ua])
                TT_("dve", ap(o_i), ap(t1), ap(t2), ALU.add, [t1.ua, t2.ua], [o_i.ua])

            for d in range(2):
                DMA(dts.ap[:, d], a_dt[l, d], [], [dts.ua])
            ACT_(dts.ap, dts.ap, AF.Exp, [dts.ua], [dts.ua])
            for c in range(4):
                for d in range(2):
                    LR, LI, BR, BI, AR_, AI_, EA, CA, SA, T1, T2, KR, KI, QR, QI, PR, PI, T3 = A_
                    g0 = c * 16
                    DMA(LR.ap, a_lam[l, d, :, 0, g0 * 64:(g0 + 16) * 64].rearrange("p (g s) -> p g s", g=16), [], [LR.ua])
                    DMA(LI.ap, a_lam[l, d, :, 1, g0 * 64:(g0 + 16) * 64].rearrange("p (g s) -> p g s", g=16), [], [LI.ua])
                    DMA(BR.ap, a_b[l, d, :, 0, g0 * 64:(g0 + 16) * 64].rearrange("p (g s) -> p g s", g=16), [], [BR.ua])
                    DMA(BI.ap, a_b[l, d, :, 1, g0 * 64:(g0 + 16) * 64].rearrange("p (g s) -> p g s", g=16), [], [BI.ua])
                    dtb = dts.ap[:, d, g0:g0 + 16].rearrange("p (g o) -> p g o", o=1).to_broadcast([128, 16, 64])
                    TT_("dve", AR_.ap, LR.ap, dtb, ALU.mult, [LR.ua, dts.ua], [AR_.ua])
                    TT_("dve", AI_.ap, LI.ap, dtb, ALU.mult, [LI.ua, dts.ua], [AI_.ua])
                    ACT_(EA.ap, AR_.ap, AF.Exp, [AR_.ua], [EA.ua])
                    sincos(AI_, CA, SA, T1, T2)
                    mul(CA, CA, EA)
                    mul(SA, SA, EA)
                    mul(T1, LR, LR)
                    mul(T2, LI, LI, "dve")
                    TT_("dve", T1.ap, T1.ap, T2.ap, ALU.add, [T1.ua, T2.ua], [T1.ua])
                    P.op("dve", lambda e, T1=T1: e.reciprocal(out=T1.ap, in_=T1.ap), [T1.ua], [T1.ua])
                    TS_("dve", EA.ap, CA.ap, -1.0, None, ALU.add, None, [CA.ua], [EA.ua])
                    mul(T2, EA, LR)
                    mul(T3, SA, LI, "dve")
                    TT_("dve", KR.ap, T2.ap, T3.ap, ALU.add, [T2.ua, T3.ua], [KR.ua])
                    mul(T2, SA, LR)
                    mul(T3, EA, LI, "dve")
                    TT_("dve", KI.ap, T2.ap, T3.ap, ALU.subtract, [T2.ua, T3.ua], [KI.ua])
                    mul(KR, KR, T1)
                    mul(KI, KI, T1)
                    cmul(QR, QI, KR, KI, BR, BI, T1, T2)
                    pcol = PWA.ap[:, d:d + 1]
                    TS_("dve", T1.ap, AR_.ap, pcol, None, ALU.mult, None, [AR_.ua, PWA.ua], [T1.ua])
                    ACT_(EA.ap, T1.ap, AF.Exp, [T1.ua], [EA.ua])
                    TS_("dve", T3.ap, AI_.ap, pcol, None, ALU.mult, None, [AI_.ua, PWA.ua], [T3.ua])
                    sincos(T3, PR, PI, T1, T2)
                    mul(PR, PR, EA)
                    mul(PI, PI, EA)
                    TT_("dve", T1.ap, PR.ap, QR.ap, ALU.mult, [PR.ua, QR.ua], [T1.ua])
                    TT_("dve", T2.ap, PI.ap, QI.ap, ALU.mult, [PI.ua, QI.ua], [T2.ua])
                    TT_("dve", stg.ap[:, :, d, 0, :], T1.ap, T2.ap, ALU.subtract, [T1.ua, T2.ua], [stg.ua])
                    TT_("dve", T1.ap, PR.ap, QI.ap, ALU.mult, [PR.ua, QI.ua], [T1.ua])
                    TT_("dve", T2.ap, PI.ap, QR.ap, ALU.mult, [PI.ua, QR.ua], [T2.ua])
                    TT_("dve", stg.ap[:, :, d, 1, :], T1.ap, T2.ap, ALU.add, [T1.ua, T2.ua], [stg.ua])
                DMA(s5b[l, 2 + c], stg.ap.rearrange("p a b c d -> p (a b c d)"), [stg.ua], [rw_s5b[l][2 + c]])

            small = [Buf(AR, base + i * 128, F32, (32,)) for i in range(24)]
            (bLR, bLI, bDT, bAR, bAI, bEA, bCA, bSA, bT1, bT2, bT3, bKR, bKI, bR8, bPH) = small[:15]
            bC = [Buf(AR, base + 4 * KB + i * 2048, F32, (32, 16)) for i in range(2)]
            bB = [Buf(AR, base + 8 * KB + i * 2048, F32, (32, 16)) for i in range(2)]
            bQ = [Buf(AR, base + 12 * KB + i * 2048, F32, (32, 16)) for i in range(2)]
            bTq = [Buf(AR, base + 16 * KB + i * 2048, F32, (32, 16)) for i in range(2)]
            P38 = [Buf(AR, base + 20 * KB + i * 1024, F32, (32, 8)) for i in range(4)]
            P38b = [Buf(AR, base + 24 * KB + i * 1024, F32, (32, 8)) for i in range(4)]
            PL8 = [Buf(AR, base + 28 * KB + i * 1024, F32, (32, 8)) for i in range(4)]
            PL8b = [Buf(AR, base + 32 * KB + i * 1024, F32, (32, 8)) for i in range(4)]
            ANG = [Buf(AR, base + 36 * KB + i * 8192, F32, (32, 64)) for i in range(2)]
            TAB = [Buf(AR, base + 52 * KB + i * 8192, F32, (32, 64)) for i in range(2)]
            M3F = [Buf(AR, base + 68 * KB + d * 8192, F32, (8, 2, 128)) for d in range(2)]
            QLF = [Buf(AR, base + 84 * KB + d * 8192, BF16, (8, 2, 128)) for d in range(2)]
            E4 = [Buf(AR, base + 100 * KB + i * 4096, F32, (8, 8, 16)) for i in range(4)]
            stg3 = Buf(AR, base + 116 * KB, BF16, (8, 2, 2, 128))
            stg1 = Buf(AR, base + 124 * KB, BF16, (32, 128))
            M1T = [Buf(AR, base + 132 * KB + i * 2048, F32, (4, 128)) for i in range(2)]
            M1G = [Buf(AR, base + 136 * KB + i * 2048, F32, (4, 128)) for i in range(2)]
            dsk = Buf(AR, base + 140 * KB, F32, (64,))
            DMA(dsk.ap, a_dsk[l], [], [dsk.ua])

            for d in range(2):
                DMA(bLR.ap, b_lam[l, d, :, 0], [], [bLR.ua])
                DMA(bLI.ap, b_lam[l, d, :, 1], [], [bLI.ua])
                DMA(bDT.ap, b_dt[l, d], [], [bDT.ua])
                for i in range(2):
                    DMA(bC[i].ap, b_c[l, d, :, i].rearrange("p (a b) -> p a b", a=32), [], [bC[i].ua])
                    DMA(bB[i].ap, b_b[l, d, :, i].rearrange("p (a b) -> p a b", a=32), [], [bB[i].ua])
                ACT_(bDT.ap, bDT.ap, AF.Exp, [bDT.ua], [bDT.ua])
                mul(bAR, bLR, bDT)
                mul(bAI, bLI, bDT)
                ACT_(bEA.ap, bAR.ap, AF.Exp, [bAR.ua], [bEA.ua])
                sincos(bAI, bCA, bSA, bT1, bT2)
                mul(bCA, bCA, bEA)
                mul(bSA, bSA, bEA)
                mul(bT1, bLR, bLR)
                mul(bT2, bLI, bLI)
                TT_("dve", bT1.ap, bT1.ap, bT2.ap, ALU.add, [bT1.ua, bT2.ua], [bT1.ua])
                P.op("dve", lambda e: e.reciprocal(out=bT1.ap, in_=bT1.ap), [bT1.ua], [bT1.ua])
                TS_("dve", bEA.ap, bCA.ap, -1.0, None, ALU.add, None, [bCA.ua], [bEA.ua])
                mul(bT2, bEA, bLR)
                mul(bT3, bSA, bLI)
                TT_("dve", bKR.ap, bT2.ap, bT3.ap, ALU.add, [bT2.ua, bT3.ua], [bKR.ua])
                mul(bT2, bSA, bLR)
                mul(bT3, bEA, bLI)
                TT_("dve", bKI.ap, bT2.ap, bT3.ap, ALU.subtract, [bT2.ua, bT3.ua], [bKI.ua])
                mul(bKR, bKR, bT1)
                mul(bKI, bKI, bT1)
                b16 = lambda b_: b_.ap.rearrange("p (a o) -> p a o", o=1).to_broadcast([128, 32, 16])
                TT_("dve", bTq[0].ap, bB[0].ap, b16(bKR), ALU.mult, [bB[0].ua, bKR.ua], [bTq[0].ua])
                TT_("dve", bTq[1].ap, bB[1].ap, b16(bKI), ALU.mult, [bB[1].ua, bKI.ua], [bTq[1].ua])
                TT_("dve", bQ[0].ap, bTq[0].ap, bTq[1].ap, ALU.subtract, [bTq[0].ua, bTq[1].ua], [bQ[0].ua])
                TT_("dve", bTq[0].ap, bB[1].ap, b16(bKR), ALU.mult, [bB[1].ua, bKR.ua], [bTq[0].ua])
                TT_("dve", bTq[1].ap, bB[0].ap, b16(bKI), ALU.mult, [bB[0].ua, bKI.ua], [bTq[1].ua])
                TT_("dve", bQ[1].ap, bTq[0].ap, bTq[1].ap, ALU.add, [bTq[0].ua, bTq[1].ua], [bQ[1].ua])
                ACT_(RTAB.ap[:, l, d, :], bAR.ap, AF.Exp, [bAR.ua], [RTAB.ua], scale=8.0)
                TS_("dve", bPH.ap, bAI.ap, 8.0, None, ALU.mult, None, [bAI.ua], [bPH.ua])
                TT_("dve", ANG[0].ap, bPH.ap.rearrange("p (a o) -> p a o", o=1).to_broadcast([128, 32, 64]),
                    qvec.rearrange("p (o q) -> p o q", o=1).to_broadcast([128, 32, 64]), ALU.mult,
                    [bPH.ua, CST.ua], [ANG[0].ua])
                sincos(ANG[0], TAB[0], TAB[1], ANG[1], Buf(AR, E4[0].off, F32, (32, 64)), reps=2)
                for cs in range(2):
                    for (ki, col) in ((0, 1), (2, 63)):
                        P.op("dve", lambda e, cs=cs, ki=ki, col=col, d=d: e.tensor_copy(out=ROT.ap[:, l, d, ki + cs, :],
                                                                                       in_=TAB[cs].ap[:, :, col]),
                             [TAB[cs].ua], [ROT.ua])
                for c in range(4):
                    for cs in range(2):
                        DMA(s5t[l, c, :, (d * 2 + cs) * 512:(d * 2 + cs + 1) * 512].rearrange("p (a q) -> p a q", a=8),
                            TAB[cs].ap[:, c * 8:(c + 1) * 8, :], [TAB[cs].ua], [rw_s5t[l][c]])
                for (PP, pv) in ((P38 if d == 0 else P38b, pw3[d]), (PL8 if d == 0 else PL8b, pwL[d])):
                    b8 = lambda b_: b_.ap.rearrange("p (a o) -> p a o", o=1).to_broadcast([128, 32, 8])
                    pvb = pv.rearrange("p (o q) -> p o q", o=1).to_broadcast([128, 32, 8])
                    TT_("dve", PP[3].ap, b8(bAR), pvb, ALU.mult, [bAR.ua, CST.ua], [PP[3].ua])
                    ACT_(PP[0].ap, PP[3].ap, AF.Exp, [PP[3].ua], [PP[0].ua])
                    TT_("dve", PP[3].ap, b8(bAI), pvb, ALU.mult, [bAI.ua, CST.ua], [PP[3].ua])
                    t_a = Buf(AR, E4[1].off, F32, (32, 8))
                    t_b = Buf(AR, E4[1].off + 1024, F32, (32, 8))
                    sincos(PP[3], PP[1], PP[2], t_a, t_b)
                    mul(PP[1], PP[1], PP[0])
                    mul(PP[2], PP[2], PP[0])
                PP3 = P38 if d == 0 else P38b
                PPL = PL8 if d == 0 else PL8b
                for c in range(4):
                    ps_ = slice(c * 8, (c + 1) * 8)

                    def bc_i(b_):
                        return b_.ap[:, ps_, :].rearrange("p a (i o) -> p a i o", o=1).to_broadcast([128, 8, 8, 16])

                    def bc_h(b_):
                        return b_.ap[:, ps_, :].rearrange("p a (o h) -> p a o h", o=1).to_broadcast([128, 8, 8, 16])

                    def o4(b_, part):
                        return b_.ap[:, :, part, :].rearrange("p a (i h) -> p a i h", i=8)
                    TT_("dve", E4[0].ap, bc_h(bC[0]), bc_i(PP3[1]), ALU.mult, [bC[0].ua, PP3[1].ua], [E4[0].ua])
                    TT_("dve", E4[1].ap, bc_h(bC[1]), bc_i(PP3[2]), ALU.mult, [bC[1].ua, PP3[2].ua], [E4[1].ua])
                    TT_("dve", o4(M3F[d], 0), E4[0].ap, E4[1].ap, ALU.subtract, [E4[0].ua, E4[1].ua], [M3F[d].ua])
                    TT_("dve", E4[0].ap, bc_h(bC[0]), bc_i(PP3[2]), ALU.mult, [bC[0].ua, PP3[2].ua], [E4[0].ua])
                    TT_("dve", E4[1].ap, bc_h(bC[1]), bc_i(PP3[1]), ALU.mult, [bC[1].ua, PP3[1].ua], [E4[1].ua])
                    STT_(o4(M3F[d], 1), E4[0].ap, -1.0, E4[1].ap, ALU.mult, ALU.subtract, [E4[0].ua, E4[1].ua], [M3F[d].ua])
                    TT_("dve", E4[2].ap, bc_h(bQ[0]), bc_i(PPL[1]), ALU.mult, [bQ[0].ua, PPL[1].ua], [E4[2].ua])
                    TT_("dve", E4[3].ap, bc_h(bQ[1]), bc_i(PPL[2]), ALU.mult, [bQ[1].ua, PPL[2].ua], [E4[3].ua])
                    TT_("dve", o4(QLF[d], 0), E4[2].ap, E4[3].ap, ALU.subtract, [E4[2].ua, E4[3].ua], [QLF[d].ua])
                    TT_("dve", E4[2].ap, bc_h(bQ[0]), bc_i(PPL[2]), ALU.mult, [bQ[0].ua, PPL[2].ua], [E4[2].ua])
                    TT_("dve", E4[3].ap, bc_h(bQ[1]), bc_i(PPL[1]), ALU.mult, [bQ[1].ua, PPL[1].ua], [E4[3].ua])
                    TT_("dve", o4(QLF[d], 1), E4[2].ap, E4[3].ap, ALU.add, [E4[2].ua, E4[3].ua], [QLF[d].ua])
                    P.op("act", lambda e, d=d: e.activation(out=stg3.ap[:, :, d, :, :], in_=M3F[d].ap, func=AF.Copy),
                         [M3F[d].ua], [stg3.ua])
                    DMA(s5b[l, 6 + c].rearrange("p (a b c e) -> p a b c e", a=8, b=2, c=2)[:, :, d, :, :],
                        stg3.ap[:, :, d, :, :], [stg3.ua], [rw_s5b[l][6 + c]])
                    for half in range(2):
                        for pl4 in range(4):
                            pl = half * 4 + pl4
                            for par in range(2):
                                rows = slice(par * 64, par * 64 + 64)
                                bk = 4 + par
                                outp = banks[bk][:, pl4 * 128:(pl4 + 1) * 128]
                                MM(outp, QLF[d].ap[rows, pl, 0, :], stg3.ap[rows, pl, d, 0, :], True, False,
                                   [QLF[d].ua, stg3.ua], [bres[bk]])
                                MM(outp, QLF[d].ap[rows, pl, 1, :], stg3.ap[rows, pl, d, 1, :], False, True,
                                   [QLF[d].ua, stg3.ua], [bres[bk]])
                        for par in range(2):
                            bk = 4 + par
                            idx = (c * 2 + half) * 2 + par
                            msk = (maskF if d == 0 else maskB).rearrange("p (o n) -> p o n", o=1).to_broadcast([128, 4, 128])
                            TT_("dve", M1T[par].ap, banks[bk][:].rearrange("p (a n) -> p a n", a=4), msk, ALU.mult,
                                [bres[bk], CST.ua], [M1T[par].ua])
                            if d == 0:
                                DMA(m1f[l, idx], M1T[par].ap.rearrange("p a n -> p (a n)"), [M1T[par].ua], [r_m1f[l][idx]])
                            else:
                                DMA(M1G[par].ap.rearrange("p a n -> p (a n)"), m1f[l, idx], [r_m1f[l][idx]], [M1G[par].ua])
                                TT_("dve", M1T[par].ap, M1T[par].ap, M1G[par].ap, ALU.add, [M1T[par].ua, M1G[par].ua], [M1T[par].ua])
                                for m4 in range(4):
                                    gg = 2 * (c * 8 + half * 4 + m4) + par
                                    STT_(stg1.ap[:, gg % 32, :], identF, dsk.ap[:, gg:gg + 1], M1T[par].ap[:, m4, :],
                                         ALU.mult, ALU.add, [CST.ua, dsk.ua, M1T[par].ua], [stg1.ua])
                    if d == 1 and c % 2 == 1:
                        DMA(s5b[l, c // 2], stg1.ap.rearrange("p a b -> p (a b)"), [stg1.ua], [rw_s5b[l][c // 2]])

        m1f = dint("m1f", [L, 16, 128, 512], F32)
        r_m1f = [[Res() for _ in range(16)] for _ in range(L)]

        for l in range(L):
            setup_layer(l)
        if dbg:
            for l in range(L):
                for b in range(10):
                    P.dma("pool", dbgo["s5b"][l, b], s5b[l, b], [rw_s5b[l][b]], [Res()])
                for b in range(4):
                    P.dma("sp", dbgo["s5t"][l, b], s5t[l, b], [rw_s5t[l][b]], [Res()])
            X = XB[0]
            P.op("pool", lambda e: e.memset(X.ap, 0.0), [], [X.ua])
            for k in range(NT):
                DMA(y_out[k * TT:(k + 1) * TT, :].rearrange("(s p) d -> p s d", p=128), X.ap, [X.ua], [r_yout[k]])
            print("ops:", P.emit())
            return nc

        PTB = [banks[6 + h][:].bitcast(BF16)[:, 0:512] for h in range(2)]
        PTR = [bres[6], bres[7]]
        pa_list = [0, 1, 2, 3]
        pa_i = [0]

        def pa_next():
            b_ = pa_list[pa_i[0] % len(pa_list)]
            pa_i[0] += 1
            return b_
        ring_i = [0]

        def ring_load(dram_ap, rres, as_f32=False):
            s_ = ring_i[0] % NRING
            ring_i[0] += 1
            rb = RING[s_]
            if as_f32:
                P.dma("sp", rb.ap, dram_ap, [rres], [rb.ua])
            else:
                P.dma("sp", rb.ap.bitcast(BF16), dram_ap, [rres], [rb.ua])
            return rb

        def wview(rb):
            return rb.ap.bitcast(BF16).rearrange("p (k c) -> p k c", k=8)

        id64 = IDB.ap[0:64, 0:64]
        pt_i = [0]

        def pt_next():
            h_ = pt_i[0] % 2
            pt_i[0] += 1
            return h_

        def rstd_from(ss_cols, n):
            ACT_(STAT.ap[:, ss_cols + 4:ss_cols + 4 + n], STAT.ap[:, ss_cols:ss_cols + n], AF.Sqrt, [STAT.ua, EPSC.ua], [STAT.ua],
                 scale=1.0 / 1024.0, bias=eps_ap)
            P.op("dve", lambda e: e.reciprocal(out=STAT.ap[:, ss_cols + 8:ss_cols + 8 + n], in_=STAT.ap[:, ss_cols + 4:ss_cols + 4 + n]),
                 [STAT.ua], [STAT.ua])

        def prenorm(gidx, X):
            for s_ in range(4):
                ACT_(TMPB.ap, X.ap[:, s_, :], AF.Square, [X.us(s_, 4)], [TMP.ua, STAT.ua], accum=STAT.ap[:, s_:s_ + 1])
            rstd_from(0, 4)
            for s_ in range(4):
                ACT_(HN.ap[:, s_, :], X.ap[:, s_, :], AF.Copy, [X.us(s_, 4), STAT.ua], [HN.us(s_, 4)], scale=STAT.ap[:, 8 + s_:9 + s_])
            for kt in range(8):
                h_ = pt_next()
                for s_ in range(4):
                    TR(PTB[h_][:, s_ * 128:(s_ + 1) * 128], HN.ap[:, s_, kt * 128:(kt + 1) * 128], IDB.ap,
                       [HN.us(s_, 4), IDB.ua], [PTR[h_]])
                TS_("dve", H.ap[:, kt, :], PTB[h_], GFM.ap[:, gidx, kt:kt + 1], None, ALU.mult, None,
                    [PTR[h_], GFM.ua], [H.us(kt, 8)])

        def dense_fm(rhs, rb, rres_w, evac):
            wv = wview(rb)
            for m in range(4):
                bk = pa_next()
                for kt in range(8):
                    MM(banks[bk][:], wv[:, kt, m * 128:(m + 1) * 128], rhs.ap[:, kt, :], kt == 0, kt == 7,
                       [rb.ua, rhs.us(kt, 8)], [bres[bk]])
                evac(m, bk)

        def ub_and_transposes(l):
            for cb in range(2):
                rb = ring_load(wb[l, 4 + cb], rw_wb[l][4 + cb])
                wv = wview(rb)
                for j in range(8):
                    bk = pa_next()
                    for kt in range(8):
                        MM(banks[bk][0:64, :], H.ap[:, kt, j:512:8], wv[:, kt, :], kt == 0, kt == 7,
                           [rb.ua, H.us(kt, 8)], [bres[bk]])
                    o_ = BIG.ap.rearrange("p a b -> p (a b)").rearrange("p (g j h) -> p g j h", g=64, j=8)[0:64, cb * 32:(cb + 1) * 32, j, :]
                    i_ = banks[bk][0:64, :].rearrange("p (g h) -> p g h", h=16)
                    if j % 2 == 0:
                        ACT_(o_, i_, AF.Copy, [bres[bk]], [BIG.u(cb * 4096, (cb + 1) * 4096)])
                    else:
                        P.op("dve", lambda e, o_=o_, i_=i_: e.tensor_copy(out=o_, in_=i_), [bres[bk]],
                             [BIG.u(cb * 4096, (cb + 1) * 4096)])
            for g8 in range(8):
                h_ = pt_next()
                for gl in range(8):
                    g = g8 * 8 + gl
                    TR(PTB[h_][:, gl * 64:(gl + 1) * 64], BIG.ap.rearrange("p a b -> p (a b)")[0:64, g * 128:(g + 1) * 128], id64,
                       [BIG.u(g * 128, (g + 1) * 128), IDB.ua], [PTR[h_]])
                P.op("dve", lambda e, h_=h_, g8=g8: e.tensor_copy(out=RA.ap[:, g8 * 8:(g8 + 1) * 8, :],
                                                                 in_=PTB[h_].rearrange("p (a b) -> p a b", a=8)),
                     [PTR[h_]], [RA.us(g8, 8)])

        def rot_small(o_re, o_im, c_, s_, i_re, i_im, rd, wr):
            t = [SML.ap[:, i, :] for i in range(4)]
            TT_("dve", t[0], c_, i_re, ALU.mult, rd, [SML.ua])
            TT_("dve", t[1], s_, i_im, ALU.mult, rd, [SML.ua])
            TT_("dve", t[2], c_, i_im, ALU.mult, rd, [SML.ua])
            TT_("dve", t[3], s_, i_re, ALU.mult, rd, [SML.ua])
            TT_("dve", o_re, t[0], t[1], ALU.subtract, [SML.ua], wr)
            TT_("dve", o_im, t[2], t[3], ALU.add, [SML.ua], wr)

        def rot32(o_re, o_im, c_, s_, i_re, i_im, rd, wr, eng):
            t = [ROTT.ap[:, i, :] for i in range(4)]
            TT_(eng, t[0], c_, i_re, ALU.mult, rd, [ROTT.ua])
            TT_(eng, t[1], s_, i_im, ALU.mult, rd, [ROTT.ua])
            TT_(eng, t[2], c_, i_im, ALU.mult, rd, [ROTT.ua])
            TT_(eng, t[3], s_, i_re, ALU.mult, rd, [ROTT.ua])
            TT_(eng, o_re, t[0], t[1], ALU.subtract, [ROTT.ua], wr)
            TT_(eng, o_im, t[2], t[3], ALU.add, [ROTT.ua], wr)

        def s5_states(l, k, dirs, pass2, fillers=()):
            Zre, Zim, Wre, Wim, T1, T2 = ZW
            T3, T4 = ZT34
            fillers = list(fillers)
            nslots = 4 * len(dirs)
            slot = [0]
            nfill = len(fillers)
            for d in dirs:
                if d == 0:
                    cin_re, cin_im, cin_u = CARF.ap[:, 0, :], CARF.ap[:, 1, :], CARF.ua
                else:
                    cin_re, cin_im, cin_u = SBIN.ap[:, k, 0, :], SBIN.ap[:, k, 1, :], SBIN.ua
                if pass2:
                    sb_ = SBF if d == 0 else SBB
                    col = 0 if d == 0 else 64
                    P.op("pool", lambda e, sb_=sb_, col=col, cin_re=cin_re: e.tensor_copy(out=sb_.ap[:, :, 0, col], in_=cin_re),
                         [cin_u], [sb_.ua])
                    P.op("pool", lambda e, sb_=sb_, col=col, cin_im=cin_im: e.tensor_copy(out=sb_.ap[:, :, 1, col], in_=cin_im),
                         [cin_u], [sb_.ua])
                rot32(WINIT.ap[:, d, 0, :], WINIT.ap[:, d, 1, :], ROT.ap[:, l, d, 0, :], ROT.ap[:, l, d, 1, :], cin_re, cin_im,
                      [cin_u, ROT.ua], [WINIT.ua], "dve")
                for part in range(2):
                    TT_("dve", RW.ap[:, d, part, :], WINIT.ap[:, d, part, :], RTAB.ap[:, l, d, :], ALU.mult, [WINIT.ua, RTAB.ua], [RW.ua])
            for c in range(4):
                rbm = ring_load(s5b[l, 2 + c], rw_s5b[l][2 + c])
                rbt = ring_load(s5t[l, c], rw_s5t[l][c], as_f32=True)
                m2v = rbm.ap.bitcast(BF16).rearrange("p (g d t s) -> p g d t s", g=16, d=2, t=2)
                tv = rbt.ap.rearrange("p (a b q) -> p a b q", a=4, b=8)
                prs = slice(c * 8, (c + 1) * 8)
                for d in dirs:
                    for part in range(2):
                        for pl in range(8):
                            for par in range(2):
                                gl = pl * 2 + par
                                g = c * 16 + gl
                                MM(banks[4 + part][par * 64:(par + 1) * 64, pl * 64:(pl + 1) * 64], m2v[:, gl, d, part, :],
                                   RA.ap[:, g, :], True, True, [rbm.ua, RA.us(g // 8, 8)], [bres[4 + part]])
                    xre = banks[4][:].rearrange("p (a q) -> p a q", a=8)
                    xim = banks[5][:].rearrange("p (a q) -> p a q", a=8)
                    rz = RZ[0]
                    TT_("pool", rz.ap, MASK0.ap, RTAB.ap[:, l, d, prs].rearrange("p (a o) -> p a o", o=1).to_broadcast([128, 8, 64]),
                        ALU.mult, [MASK0.ua, RTAB.ua], [rz.ua])
                    if d == 0:
                        ct = tv[:, 0]
                        st = tv[:, 1]
                        dct, dst_ = ct, st
                        xre_d, xim_d = xre, xim
                    else:
                        ct = tv[:, 2, :, ::-1]
                        st = tv[:, 3, :, ::-1]
                        dct, dst_ = tv[:, 2], tv[:, 3]
                        xre_d, xim_d = xre[:, :, ::-1], xim[:, :, ::-1]
                    TT_("dve", T1.ap, xre_d, dct, ALU.mult, [bres[4], rbt.ua], [T1.ua])
                    TT_("dve", T2.ap, xim_d, dst_, ALU.mult, [bres[5], rbt.ua], [T2.ua])
                    TT_("dve", T3.ap, xim_d, dct, ALU.mult, [bres[5], rbt.ua], [T3.ua])
                    TT_("dve", T4.ap, xre_d, dst_, ALU.mult, [bres[4], rbt.ua], [T4.ua])
                    TT_("dve", Zre.ap, T1.ap, T2.ap, ALU.add, [T1.ua, T2.ua], [Zre.ua])
                    TT_("pool", Zim.ap, T3.ap, T4.ap, ALU.subtract, [T3.ua, T4.ua], [Zim.ua])
                    for part, (Zp, Wp) in enumerate(((Zre, Wre), (Zim, Wim))):
                        TT_("dve", Zp.ap[:, :, 0], Zp.ap[:, :, 0], RW.ap[:, d, part, prs], ALU.add, [Zp.ua, RW.ua], [Zp.ua])
                        P.op("dve", lambda e, Wp=Wp, Zp=Zp, rz=rz: e.tensor_tensor_scan(
                            out=Wp.ap.rearrange("p a q -> p (a q)"), data0=rz.ap.rearrange("p a q -> p (a q)"),
                            data1=Zp.ap.rearrange("p a q -> p (a q)"), initial=0.0, op0=ALU.mult, op1=ALU.add),
                            [Zp.ua, rz.ua], [Wp.ua])
                        P.op("pool", lambda e, Wp=Wp, part=part, d=d, prs=prs: e.tensor_copy(out=WLAST.ap[:, d, part, prs],
                                                                                          in_=Wp.ap[:, :, 63]),
                             [Wp.ua], [WLAST.ua])
                    if pass2:
                        if d == 0:
                            wr_, wi_ = Wre.ap, Wim.ap
                            o_re = SBF.ap[:, prs, 0, 1:65]
                            o_im = SBF.ap[:, prs, 1, 1:65]
                            sbu = SBF.ua
                        else:
                            wr_, wi_ = Wre.ap[:, :, ::-1], Wim.ap[:, :, ::-1]
                            o_re = SBB.ap[:, prs, 0, 0:64]
                            o_im = SBB.ap[:, prs, 1, 0:64]
                            sbu = SBB.ua
                        TT_("dve", T1.ap, wr_, ct, ALU.mult, [Wre.ua, rbt.ua], [T1.ua])
                        TT_("pool", T2.ap, wi_, st, ALU.mult, [Wim.ua, rbt.ua], [T2.ua])
                        TT_("dve", T3.ap, wi_, ct, ALU.mult, [Wim.ua, rbt.ua], [T3.ua])
                        TT_("pool", T4.ap, wr_, st, ALU.mult, [Wre.ua, rbt.ua], [T4.ua])
                        TT_("dve", o_re, T1.ap, T2.ap, ALU.subtract, [T1.ua, T2.ua], [sbu])
                        TT_("dve", o_im, T3.ap, T4.ap, ALU.add, [T3.ua, T4.ua], [sbu])
                    slot[0] += 1
                    tgt = (slot[0] * nfill) // nslots
                    while nfill - len(fillers) < tgt:
                        fillers.pop(0)()
            while fillers:
                fillers.pop(0)()
            for d in dirs:
                if d == 0:
                    rot32(CARF.ap[:, 0, :], CARF.ap[:, 1, :], ROT.ap[:, l, d, 2, :], ROT.ap[:, l, d, 3, :],
                          WLAST.ap[:, d, 0, :], WLAST.ap[:, d, 1, :], [WLAST.ua, ROT.ua], [CARF.ua], "pool")
                elif not pass2 and k >= 1:
                    rot32(SBIN.ap[:, k - 1, 0, :], SBIN.ap[:, k - 1, 1, :], ROT.ap[:, l, d, 2, :], ROT.ap[:, l, d, 3, :],
                          WLAST.ap[:, d, 0, :], WLAST.ap[:, d, 1, :], [WLAST.ua, ROT.ua], [SBIN.ua], "pool")

        def s5_out(l):
            m1rb = None
            m3rb = None

            def back_transposes(kt):
                h_ = pt_next()
                for i in range(8):
                    TR(PTB[h_][:, i * 64:(i + 1) * 64], BIG.ap[0:64, i, kt * 128:(kt + 1) * 128], id64, [BIG.ua, IDB.ua], [PTR[h_]])
                P.op("dve", lambda e, h_=h_, kt=kt: e.tensor_copy(out=RB.ap[:, kt, :].rearrange("p (b i) -> p i b", i=8),
                                                                 in_=PTB[h_].rearrange("p (i b) -> p i b", i=8)),
                     [PTR[h_]], [RB.us(kt, 8)])
            pend = None
            for kt in range(8):
                if kt % 4 == 0:
                    m1rb = ring_load(s5b[l, kt // 4], rw_s5b[l][kt // 4])
                if kt % 2 == 0:
                    m3rb = ring_load(s5b[l, 6 + kt // 2], rw_s5b[l][6 + kt // 2])
                m1v = m1rb.ap.bitcast(BF16).rearrange("p (g n) -> p g n", g=32)
                m3v = m3rb.ap.bitcast(BF16).rearrange("p (a d t n) -> p a d t n", a=8, d=2, t=2)
                ybk = (4, 5) if kt % 2 == 0 else (2, 3)
                for gl in range(8):
                    g = kt * 8 + gl
                    pair, par = g // 2, g % 2
                    rows = slice(par * 64, par * 64 + 64)
                    bk = ybk[par]
                    m_ = gl // 2
                    outp = banks[bk][0:64, m_ * 128:(m_ + 1) * 128]
                    pin = pair % 8
                    MM(outp, RA.ap[:, g, :], m1v[:, g % 32, :], True, False, [RA.us(g // 8, 8), m1rb.ua], [bres[bk]])
                    MM(outp, SBF.ap[rows, pair, 0, 0:64], m3v[rows, pin, 0, 0, :], False, False, [SBF.ua, m3rb.ua], [bres[bk]])
                    MM(outp, SBF.ap[rows, pair, 1, 0:64], m3v[rows, pin, 0, 1, :], False, False, [SBF.ua, m3rb.ua], [bres[bk]])
                    MM(outp, SBB.ap[rows, pair, 0, 1:65], m3v[rows, pin, 1, 0, :], False, False, [SBB.ua, m3rb.ua], [bres[bk]])
                    MM(outp, SBB.ap[rows, pair, 1, 1:65], m3v[rows, pin, 1, 1, :], False, True, [SBB.ua, m3rb.ua], [bres[bk]])
                for par in range(2):
                    bk = ybk[par]
                    in_ = banks[bk][0:64, :].rearrange("p (m i h) -> p m i h", m=4, i=8)
                    o_ = BIG.ap[0:64, :, kt * 128:(kt + 1) * 128].rearrange("p i (m r h) -> p m r i h", m=4, r=2)[:, :, par, :, :]
                    ACT_(o_, in_, AF.Gelu_apprx_tanh, [bres[bk]], [BIG.u(kt * 128, 7 * 1024 + (kt + 1) * 128)])
                if pend is not None:
                    back_transposes(pend)
                pend = kt
            back_transposes(pend)

        def tm_project(l, blocks, lhs, ktn):
            for cb in range(2):
                bks = [pa_list[i] for i in range(4)]
                nkc = len(blocks[cb])
                for kc, bid in enumerate(blocks[cb]):
                    rb = ring_load(wb[l, bid], rw_wb[l][bid])
                    wv = wview(rb)
                    for s_ in range(4):
                        for kt in range(8):
                            MM(banks[bks[s_]][:], lhs.ap[:, kc * 8 + kt, s_ * 128:(s_ + 1) * 128], wv[:, kt, :],
                               kc == 0 and kt == 0, kc == nkc - 1 and kt == 7, [rb.ua, lhs.us(kc * 8 + kt, ktn)], [bres[bks[s_]]])
                for s_ in range(4):
                    ACT_(MIXF.ap[:, s_, cb * 512:(cb + 1) * 512], banks[bks[s_]][:], AF.Copy, [bres[bks[s_]]],
                         [MIXF.u(s_ * 1024 + cb * 512, s_ * 1024 + cb * 512 + 512)])

        def postnorm_add(gidx, X):
            for s_ in range(4):
                ACT_(TMPB.ap, MIXF.ap[:, s_, :], AF.Square, [MIXF.us(s_, 4)], [TMP.ua, STAT.ua], accum=STAT.ap[:, 16 + s_:17 + s_])
            rstd_from(16, 4)
            for s_ in range(4):
                STT_(MIXF.ap[:, s_, :], MIXF.ap[:, s_, :], STAT.ap[:, 24 + s_:25 + s_], GBC.ap[:, gidx, :], ALU.mult, ALU.mult,
                     [MIXF.us(s_, 4), STAT.ua, GBC.ua], [MIXF.us(s_, 4)])
                TT_("dve", X.ap[:, s_, :], X.ap[:, s_, :], MIXF.ap[:, s_, :], ALU.add, [X.us(s_, 4), MIXF.us(s_, 4)], [X.us(s_, 4)])

        def load_layer_consts(l):
            DMA(GFM.ap, gfm[l], [], [GFM.ua], q="sp")
            DMA(GBC.ap, gbc[l].rearrange("i p d -> p i d"), [], [GBC.ua], q="sp")
            DMA(BSB.ap.rearrange("p a b -> p (a b)"), bsb[l], [], [BSB.ua], q="sp")
            DMA(RING[0].ap[:, 0:1024], wst[l], [], [RING[0].ua], q="sp")
            P.op("dve", lambda e: e.tensor_copy(out=WST.ap.rearrange("p a b -> p (a b)"), in_=RING[0].ap[:, 0:1024]), [RING[0].ua], [WST.ua])
            P.op("pool", lambda e: e.memset(CARF.ap, 0.0), [], [CARF.ua])
            P.op("pool", lambda e: e.memset(SBIN.ap[:, NT - 1], 0.0), [], [SBIN.ua])

        import os as _os
        TAPS = _os.environ.get("KTAPS", "") == "1"

        def tap(name, b_, l, k):
            if not (TAPS and l == 0 and k == 0):
                return
            dt_ = F32 if b_.es == 4 else BF16
            o_ = nc.dram_tensor("tap_" + name, [128, b_.n], dt_, kind="ExternalOutput").ap()
            flat = b_.ap
            if len(b_.shape) == 2:
                flat = flat.rearrange("p a b -> p (a b)")
            elif len(b_.shape) == 3:
                flat = flat.rearrange("p a b c -> p (a b c)")
            P.dma("sp", o_, flat, [b_.ua], [Res()])

        xstate = {"i": 0, "pending": None}
        r_uc = [Res() for _ in range(NT)]

        def x_fetch(src_ap, rsrc, k, key):
            if xstate["pending"] == key:
                xstate["i"] += 1
                xstate["pending"] = None
                return XB[xstate["i"] % 2]
            Xb = XB[xstate["i"] % 2]
            P.dma("sp", Xb.ap, src_ap[k * TT:(k + 1) * TT, :].rearrange("(s p) d -> p s d", p=128), [rsrc[k]], [Xb.ua])
            return Xb

        def x_prefetch(src_ap, rsrc, k, key):
            Xn = XB[(xstate["i"] + 1) % 2]
            P.dma("sp", Xn.ap, src_ap[k * TT:(k + 1) * TT, :].rearrange("(s p) d -> p s d", p=128), [rsrc[k]], [Xn.ua])
            xstate["pending"] = key

        def branch_a_items(l, k):
            items = []
            for b in range(4):
                def it(b=b):
                    rb = ring_load(wb[l, 6 + b], rw_wb[l][6 + b])

                    def ev(m, bk):
                        mm = (b % 2) * 4 + m
                        ACT_(SG.ap[:, b // 2, mm, :], banks[bk][:], AF.Sigmoid, [bres[bk]], [SG.us((b // 2) * 8 + mm, 16)])
                    dense_fm(H, rb, None, ev)
                items.append(it)
            for b in (0, 1):
                def it(b=b):
                    rb = ring_load(wb[l, b], rw_wb[l][b])

                    def ev(m, bk):
                        mm = b * 4 + m
                        ACT_(RC.ap[:, mm, :], banks[bk][:], AF.Gelu_apprx_tanh, [bres[bk]], [RC.us(mm, 8)])
                    dense_fm(H, rb, None, ev)
                items.append(it)
            for cb in range(2):
                def it(cb=cb):
                    rb = ring_load(wb[l, 2 + cb], rw_wb[l][2 + cb])
                    wv = wview(rb)
                    for s_ in range(4):
                        bk = pa_next()
                        for kt in range(8):
                            MM(banks[bk][:], H.ap[:, kt, s_ * 128:(s_ + 1) * 128], wv[:, kt, :], kt == 0, kt == 7,
                               [rb.ua, H.us(kt, 8)], [bres[bk]])
                        ACT_(VN.ap[:, s_, cb * 512:(cb + 1) * 512], banks[bk][:], AF.Gelu_apprx_tanh, [bres[bk]],
                             [VN.u(s_ * 1024 + cb * 512, s_ * 1024 + cb * 512 + 512)])
                items.append(it)

            def vnorm():
                for s_ in range(4):
                    ACT_(TMPB.ap, VN.ap[:, s_, :], AF.Square, [VN.us(s_, 4)], [TMP.ua, STAT.ua], accum=STAT.ap[:, 32 + s_:33 + s_])
                rstd_from(32, 4)
                for s_ in range(4):
                    P.op("act", lambda e, s_=s_: e.activation(out=VN.ap[:, s_, :], in_=VN.ap[:, s_, :], func=AF.Copy,
                                                              scale=STAT.ap[:, 40 + s_:41 + s_]),
                         [VN.us(s_, 4), STAT.ua], [VN.us(s_, 4)])
            items.append(vnorm)
            for half in range(2):
                def it(half=half):
                    for g in range(half * 4, half * 4 + 4):
                        bk = pa_next()
                        for s_ in range(4):
                            MM(banks[bk][:, s_ * 128:(s_ + 1) * 128], VN.ap[:, s_, g * 128:(g + 1) * 128], WST.ap[:, g, :], True, True,
                               [VN.us(s_, 4), WST.ua], [bres[bk]])
                        STT_(RD.ap[:, g, :].rearrange("p (s q) -> p s q", s=4), banks[bk][:].rearrange("p (s q) -> p s q", s=4),
                             GFM.ap[:, 2, g:g + 1], BSB.ap[:, g, :].rearrange("p (o q) -> p o q", o=1).to_broadcast([128, 4, 128]),
                             ALU.mult, ALU.add, [bres[bk], GFM.ua, BSB.ua], [RD.us(g, 8)])
                        TT_("pool", RC.ap[:, g, :], RC.ap[:, g, :], RD.ap[:, g, :], ALU.mult, [RC.us(g, 8), RD.us(g, 8)], [RC.us(g, 8)])
                items.append(it)
            for b in (0, 1):
                def it(b=b):
                    rb = ring_load(wb[l, 10 + b], rw_wb[l][10 + b])

                    def ev(m, bk):
                        mm = b * 4 + m
                        TT_("dve", SG.ap[:, 0, mm, :], banks[bk][:], SG.ap[:, 0, mm, :], ALU.mult, [bres[bk], SG.us(mm, 16)],
                            [SG.us(mm, 16)])
                    dense_fm(RC, rb, None, ev)
                items.append(it)
            return items

        def layer(l, src_ap, rsrc, dst_ap, rdst):
            load_layer_consts(l)
            for k in range(NT - 1, 0, -1):
                P.cur_tag = f"L{l}P1T{k}:prenorm"
                X = x_fetch(src_ap, rsrc, k, (l, k))
                x_prefetch(src_ap, rsrc, k - 1, (l, k - 1))
                prenorm(0, X)
                P.cur_tag = f"L{l}P1T{k}:ub"
                ub_and_transposes(l)
                P.dma("sp", ucache[k], RA.ap.rearrange("p a b -> p (a b)"), [RA.ua], [r_uc[k]])
                P.cur_tag = f"L{l}P1T{k}:s5st"
                s5_states(l, k, [1], False)
            for k in range(NT):
                P.cur_tag = f"L{l}P2T{k}:prenorm"
                X = x_fetch(src_ap, rsrc, k, (l, k))
                if k + 1 < NT:
                    x_prefetch(src_ap, rsrc, k + 1, (l, k + 1))
                prenorm(0, X)
                P.cur_tag = f"L{l}P2T{k}:ub"
                if k == 0:
                    ub_and_transposes(l)
                else:
                    P.dma("sp", RA.ap.rearrange("p a b -> p (a b)"), ucache[k], [r_uc[k]], [RA.ua])
                tap("H", H, l, k)
                tap("RA", RA, l, k)
                P.cur_tag = f"L{l}P2T{k}:s5st"
                s5_states(l, k, [0, 1], True, fillers=branch_a_items(l, k))
                tap("SBF", SBF, l, k)
                tap("SBB", SBB, l, k)
                tap("AIN", RC, l, k)
                P.cur_tag = f"L{l}P2T{k}:s5out"
                s5_out(l)
                tap("RB", RB, l, k)
                P.cur_tag = f"L{l}P2T{k}:glu"
                for b in (2, 3):
                    rb = ring_load(wb[l, 12 + b], rw_wb[l][12 + b])

                    def ev(m, bk, b=b):
                        mm = (b - 2) * 4 + m
                        ACT_(RD.ap[:, mm, :], banks[bk][:], AF.Sigmoid, [bres[bk]], [RD.us(mm, 8)])
                        TT_("pool", RD.ap[:, mm, :], RD.ap[:, mm, :], SG.ap[:, 1, mm, :], ALU.mult, [RD.us(mm, 8), SG.us(8 + mm, 16)],
                            [RD.us(mm, 8)])
                    dense_fm(RB, rb, None, ev)
                for b in (0, 1):
                    rb = ring_load(wb[l, 12 + b], rw_wb[l][12 + b])

                    def ev(m, bk, b=b):
                        mm = b * 4 + m
                        TT_("dve", SG.ap[:, 1, mm, :], banks[bk][:], RD.ap[:, mm, :], ALU.mult, [bres[bk], RD.us(mm, 8)],
                            [SG.us(8 + mm, 16)])
                        TT_("pool", SG.ap[:, 0, mm, :], SG.ap[:, 0, mm, :], SG.ap[:, 1, mm, :], ALU.add,
                            [SG.us(mm, 16), SG.us(8 + mm, 16)], [SG.us(mm, 16)])
                    dense_fm(RB, rb, None, ev)
                P.cur_tag = f"L{l}P2T{k}:wo"
                SG0 = Buf(AR, SG.off, BF16, (8, 512))
                tap("MIXIN", SG0, l, k)
                tm_project(l, [[16], [17]], SG0, 8)
                tap("MIXF", MIXF, l, k)
                postnorm_add(0, X)
                tap("X1", X, l, k)
                P.cur_tag = f"L{l}P2T{k}:prenorm2"
                prenorm(1, X)
                P.cur_tag = f"L{l}P2T{k}:ff1"
                for b in range(8):
                    rb = ring_load(wb[l, 18 + b], rw_wb[l][18 + b])

                    def ev(m, bk, b=b):
                        mm = b * 4 + m
                        ACT_(RD.ap[:, mm % 8, :], banks[bk][:], AF.Square, [bres[bk]], [RD.us(mm % 8, 8)])
                        STT_(HID.ap[:, mm, :], banks[bk][:], 0.0, RD.ap[:, mm % 8, :], ALU.is_gt, ALU.mult,
                             [bres[bk], RD.us(mm % 8, 8)], [HID.us(mm, 32)])
                    dense_fm(H, rb, None, ev)
                P.cur_tag = f"L{l}P2T{k}:ff2"
                tap("HID", HID, l, k)
                tm_project(l, [[26, 27, 28, 29], [30, 31, 32, 33]], HID, 32)
                tap("FF", MIXF, l, k)
                postnorm_add(1, X)
                tap("X2", X, l, k)
                P.dma("sp", dst_ap[k * TT:(k + 1) * TT, :].rearrange("(s p) d -> p s d", p=128), X.ap, [X.ua], [rdst[k]])

        r_xin = [Res() for _ in range(NT)]
        layer(0, x_in, r_xin, x_mid, r_xmid)
        layer(1, x_mid, r_xmid, y_out, r_yout)
        print("arena top", AR.top, "of", AR.nbytes)
        print("ops:", P.emit())
    return nc


def host_layouts(p):
    f = np.float32
    out = {}
    wf = np.zeros((L, NWB, 128, 4096), f)

    def blk(w, kc, c0):
        return w[kc * 1024:(kc + 1) * 1024, c0:c0 + 512].reshape(8, 128, 512).transpose(1, 0, 2).reshape(128, 4096)
    for l in range(L):
        for b in range(10):
            wf[l, b] = blk(p["w_in"][l], 0, b * 512)
        for b in range(2):
            wf[l, 10 + b] = blk(p["w_out_a"][l], 0, b * 512)
        for b in range(4):
            wf[l, 12 + b] = blk(p["w_glu"][l], 0, b * 512)
        for b in range(2):
            wf[l, 16 + b] = blk(p["w_o"][l], 0, b * 512)
        for b in range(8):
            wf[l, 18 + b] = blk(p["w_ff1"][l], 0, b * 512)
        for cb in range(2):
            for kc in range(4):
                wf[l, 26 + cb * 4 + kc] = blk(p["w_ff2"][l], kc, cb * 512)
    out["wf"] = wf
    hh = np.arange(128) % 16
    jj = np.arange(128) // 16
    a_lam = np.zeros((L, 2, 128, 2, 4096), f)
    a_b = np.zeros((L, 2, 128, 2, 4096), f)
    a_dt = np.zeros((L, 2, 128, 64), f)
    for l in range(L):
        for d in range(2):
            a_lam[l, d, :, 0] = p["lam_re"][l, d].reshape(1, 4096)
            a_lam[l, d, :, 1] = p["lam_im"][l, d].reshape(1, 4096)
            a_b[l, d, :, 0] = p["b_re"][l, d][:, :, hh].transpose(2, 0, 1).reshape(128, 4096)
            a_b[l, d, :, 1] = p["b_im"][l, d][:, :, hh].transpose(2, 0, 1).reshape(128, 4096)
            a_dt[l, d] = p["log_dt"][l, d][None, :]
    out["a_lam"], out["a_b"], out["a_dt"] = a_lam, a_b, a_dt
    a_pw = np.zeros((128, 2), f)
    a_pw[:, 0] = 7 - jj
    a_pw[:, 1] = jj
    out["a_pw"] = a_pw
    a_dsk = np.zeros((L, 128, 64), f)
    for l in range(L):
        a_dsk[l] = p["d_skip"][l].reshape(64, 16)[:, hh].T
    out["a_dsk"] = a_dsk
    par = np.arange(128) // 64
    ss = np.arange(128) % 64
    b_lam = np.zeros((L, 2, 128, 2, 32), f)
    b_dt = np.zeros((L, 2, 128, 32), f)
    b_c = np.zeros((L, 2, 128, 2, 512), f)
    b_b = np.zeros((L, 2, 128, 2, 512), f)
    for l in range(L):
        for d in range(2):
            for pp in range(2):
                rows = slice(pp * 64, pp * 64 + 64)
                b_lam[l, d, rows, 0] = p["lam_re"][l, d][pp::2].T
                b_lam[l, d, rows, 1] = p["lam_im"][l, d][pp::2].T
                b_dt[l, d, rows] = p["log_dt"][l, d][pp::2][None, :]
                b_c[l, d, rows, 0] = p["c_re"][l, d][pp::2].transpose(2, 0, 1).reshape(64, 512)
                b_c[l, d, rows, 1] = p["c_im"][l, d][pp::2].transpose(2, 0, 1).reshape(64, 512)
                b_b[l, d, rows, 0] = p["b_re"][l, d][pp::2].transpose(1, 0, 2).reshape(64, 512)
                b_b[l, d, rows, 1] = p["b_im"][l, d][pp::2].transpose(1, 0, 2).reshape(64, 512)
    out["b_lam"], out["b_dt"], out["b_c"], out["b_b"] = b_lam, b_dt, b_c, b_b
    cst = np.zeros((128, 480), f)
    i8 = np.arange(8)
    cst[:, 0:8] = i8 + 1
    cst[:, 8:16] = 8 - i8
    cst[:, 16:24] = -(i8 + 1)
    cst[:, 24:32] = i8 - 8
    cst[:, 32:96] = np.arange(64)
    ji = np.arange(128) // 16
    cst[:, 96:224] = (ji[None, :] >= ji[:, None])
    cst[:, 224:352] = (ji[None, :] <= ji[:, None])
    cst[:, 352:480] = np.eye(128)
    out["cst"] = cst
    gfm = np.zeros((L, 128, 3, 8), f)
    gbc = np.zeros((L, 2, 128, 1024), f)
    bsb = np.zeros((L, 128, 1024), f)
    wst = np.zeros((L, 128, 1024), f)
    for l in range(L):
        gfm[l, :, 0] = p["norm_pre_mix"][l].reshape(8, 128).T
        gfm[l, :, 1] = p["norm_pre_ff"][l].reshape(8, 128).T
        gfm[l, :, 2] = p["norm_v"][l].reshape(8, 128).T
        gbc[l, 0] = p["norm_post_mix"][l][None, :]
        gbc[l, 1] = p["norm_post_ff"][l][None, :]
        bsb[l] = p["b_s"][l].reshape(1, 1024)
        wst[l] = p["w_s"][l].transpose(2, 0, 1).reshape(128, 1024)
    out["gfm"], out["gbc"], out["bsb"], out["wst"] = gfm, gbc, bsb, wst
    return out


_NC_CACHE = {}


def kernel(**inputs):
    p = {k: np.asarray(v, dtype=np.float32) for k, v in inputs.items()}
    xs = [p["x_prompt"][i] for i in range(2)] + [p["x_sample"][i] for i in range(4)]
    lay = host_layouts(p)
    if "nc" not in _NC_CACHE:
        _NC_CACHE["nc"] = build()
    nc = _NC_CACHE["nc"]
    in_maps = []
    for c in range(NCORES):
        m = dict(lay)
        m["x"] = np.ascontiguousarray(xs[c % 6])
        in_maps.append(m)
    res = run_bass_kernel_spmd(nc, in_maps, core_ids=list(range(NCORES)))
    ys = [res.results[c]["y"] for c in range(6)]
    y_prompt = np.stack(ys[0:2]).astype(np.float32)
    y_sample = np.stack(ys[2:6]).astype(np.float32)
    return (y_prompt, y_sample)
```
